# Optimizing a Trainium2 kernel written in Bass

```python
import jax
import jax.numpy as jnp
from jax import lax
import numpy as np


D_MODEL = 1024
BATCH = 8
SEQ = 4096
DEPTH = 1

GRID_W = 64
CTX_LEN = 256
RET_HEADS = 4
RET_QK_DIM = 128
RET_V_DIM = 256
RET_CHUNK = 128
RET_ROPE_THETA = 10000.0
ATT_HEADS = 8
ATT_KV_HEADS = 2
ATT_GROUP = ATT_HEADS // ATT_KV_HEADS
HEAD_DIM = 128
Q_BLOCK = 128
ROPE_THETA = 10000.0
D_FF = 4 * D_MODEL
EPS = 1e-6
D_RET_QK = RET_HEADS * RET_QK_DIM
D_RET_V = RET_HEADS * RET_V_DIM
D_ATT_Q = ATT_HEADS * HEAD_DIM
D_ATT_KV = ATT_KV_HEADS * HEAD_DIM
IN_SPLITS = (D_RET_QK, D_RET_QK, D_RET_V, D_RET_V, D_ATT_Q, D_ATT_KV, D_ATT_KV, D_MODEL, D_MODEL)
D_IN = sum(IN_SPLITS)

kernel_name = 'hybrid_retention_gqa_dit_layer'


def _rms(x, g):
    xf = x.astype(jnp.float32)
    y = xf * lax.rsqrt(jnp.mean(xf * xf, axis=-1, keepdims=True) + EPS)
    return (y * g.astype(jnp.float32)).astype(x.dtype)


def _rope_angles(pos, dim, theta):
    half = dim // 2
    inv = theta ** (-jnp.arange(half, dtype=jnp.float32) / half)
    ang = pos.astype(jnp.float32)[:, None] * inv[None, :]
    return jnp.cos(ang), jnp.sin(ang)


def _rotate(x, cos, sin):
    xf = x.astype(jnp.float32)
    half = x.shape[-1] // 2
    x1, x2 = xf[..., :half], xf[..., half:]
    return jnp.concatenate([x1 * cos - x2 * sin, x1 * sin + x2 * cos], axis=-1).astype(x.dtype)


def _axial_rotate(x, tabs):
    cr, sr, cc, sc = tabs
    half = x.shape[-1] // 2
    return jnp.concatenate([_rotate(x[..., :half], cr, sr), _rotate(x[..., half:], cc, sc)], axis=-1)


def _split_proj(p):
    offs = np.cumsum(IN_SPLITS)[:-1].tolist()
    return jnp.split(p, offs, axis=-1)


def _retention_dir(q, k, v, log_gamma, state0, inclusive):
    b, h, n, dk = q.shape
    dv = v.shape[-1]
    nc = n // RET_CHUNK
    cs = RET_CHUNK
    qc = q.astype(jnp.float32).reshape(b, h, nc, cs, dk).transpose(2, 0, 1, 3, 4)
    kc = k.astype(jnp.float32).reshape(b, h, nc, cs, dk).transpose(2, 0, 1, 3, 4)
    vc = v.astype(jnp.float32).reshape(b, h, nc, cs, dv).transpose(2, 0, 1, 3, 4)
    idx = jnp.arange(cs, dtype=jnp.float32)
    lg = log_gamma[:, None, None]
    diff = idx[:, None] - idx[None, :]
    mask = (diff >= 0) if inclusive else (diff > 0)
    decay_intra = jnp.where(mask, jnp.exp(lg * jnp.maximum(diff, 0.0)), 0.0)
    q_decay = jnp.exp(lg * (idx + 1.0)[None, :, None])
    k_decay = jnp.exp(lg * (cs - 1.0 - idx)[None, :, None])
    chunk_decay = jnp.exp(lg * cs)

    def step(s, inp):
        qi, ki, vi = inp
        scores = jnp.einsum('bhid,bhjd->bhij', qi, ki) * decay_intra
        o = jnp.einsum('bhij,bhjv->bhiv', scores, vi) + jnp.einsum('bhid,bhdv->bhiv', qi * q_decay, s)
        s = s * chunk_decay + jnp.einsum('bhjd,bhjv->bhdv', ki * k_decay, vi)
        return s, o

    s_final, o = lax.scan(step, state0, (qc, kc, vc))
    o = o.transpose(1, 2, 0, 3, 4).reshape(b, h, n, dv)
    return o, s_final


def _ret_heads(q, k, v, cos, sin):
    b, n = q.shape[:2]
    q = _rotate(q.reshape(b, n, RET_HEADS, RET_QK_DIM).transpose(0, 2, 1, 3), cos, sin)
    k = _rotate(k.reshape(b, n, RET_HEADS, RET_QK_DIM).transpose(0, 2, 1, 3), cos, sin) * (RET_QK_DIM ** -0.5)
    v = v.reshape(b, n, RET_HEADS, RET_V_DIM).transpose(0, 2, 1, 3)
    return q, k, v


def _ret_out(o, gate, gn_w, gn_b):
    b, h, n, dv = o.shape
    mu = jnp.mean(o, axis=-1, keepdims=True)
    var = jnp.mean(jnp.square(o - mu), axis=-1, keepdims=True)
    o = ((o - mu) * lax.rsqrt(var + EPS)).transpose(0, 2, 1, 3).reshape(b, n, h * dv)
    o = o * gn_w.astype(jnp.float32) + gn_b.astype(jnp.float32)
    return (jax.nn.silu(gate.astype(jnp.float32)) * o).astype(gate.dtype)


def _att_heads(q, k, v, qg, kg):
    b, n = q.shape[:2]
    q = _rms(q.reshape(b, n, ATT_KV_HEADS, ATT_GROUP, HEAD_DIM), qg).transpose(0, 2, 3, 1, 4)
    k = _rms(k.reshape(b, n, ATT_KV_HEADS, HEAD_DIM), kg).transpose(0, 2, 1, 3)
    v = v.reshape(b, n, ATT_KV_HEADS, HEAD_DIM).transpose(0, 2, 1, 3)
    return q, k, v


def _attend_blocks(q, k, v):
    b, kvh, g, nq, d = q.shape
    nb = nq // Q_BLOCK
    qb = q.reshape(b, kvh, g, nb, Q_BLOCK, d).transpose(3, 0, 1, 2, 4, 5)
    scale = HEAD_DIM ** -0.5

    def one(qi):
        s = jnp.einsum('bkgqd,bksd->bkgqs', qi, k).astype(jnp.float32) * scale
        p = jax.nn.softmax(s, axis=-1).astype(v.dtype)
        return jnp.einsum('bkgqs,bksd->bkgqd', p, v)

    o = lax.map(one, qb)
    return o.transpose(1, 3, 4, 2, 0, 5).reshape(b, nq, kvh * g * d) if False else o.transpose(1, 0, 4, 2, 3, 5).reshape(b, nq, kvh * g * d)


def _merge(o_ret, o_att, g_ret, g_att, w_br_ret, w_br_att, w_out):
    y = jax.nn.sigmoid(g_ret) * (o_ret @ w_br_ret) + jax.nn.sigmoid(g_att) * (o_att @ w_br_att)
    return y @ w_out


def _sq_relu_mlp(h, w_up, w_down):
    return jnp.square(jax.nn.relu(h @ w_up)) @ w_down


def setup_inputs(seed: int = 0) -> dict:
    key = jax.random.key(seed)
    ks = jax.random.split(key, 24)
    f32 = jnp.float32

    def nrm(k, shape, scale):
        return jax.random.normal(k, shape, f32) * scale

    base_lam = jnp.log(-jnp.log(1.0 - 2.0 ** (-5.0 - jnp.arange(RET_HEADS, dtype=f32))))
    return {
        'x': nrm(ks[0], (BATCH, SEQ, D_MODEL), 1.0),
        'c': nrm(ks[1], (BATCH, D_MODEL), 1.0),
        'ctx': nrm(ks[2], (BATCH, CTX_LEN, D_MODEL), 1.0),
        'c_ctx': nrm(ks[3], (D_MODEL,), 1.0),
        'mod_w': nrm(ks[4], (DEPTH, D_MODEL, 6 * D_MODEL), 0.5 * D_MODEL ** -0.5),
        'mod_b': nrm(ks[5], (DEPTH, 6 * D_MODEL), 0.02),
        'norm_mix_g': 1.0 + nrm(ks[6], (DEPTH, D_MODEL), 0.02),
        'norm_mlp_g': 1.0 + nrm(ks[7], (DEPTH, D_MODEL), 0.02),
        'w_in': nrm(ks[8], (DEPTH, D_MODEL, D_IN), D_MODEL ** -0.5),
        'ret_log_lam_fwd': base_lam[None, :] + nrm(ks[9], (DEPTH, RET_HEADS), 0.02),
        'ret_log_lam_bwd': base_lam[None, :] + nrm(ks[10], (DEPTH, RET_HEADS), 0.02),
        'ret_gn_w': 1.0 + nrm(ks[11], (DEPTH, D_RET_V), 0.02),
        'ret_gn_b': nrm(ks[12], (DEPTH, D_RET_V), 0.02),
        'att_q_norm_g': 1.0 + nrm(ks[13], (DEPTH, HEAD_DIM), 0.02),
        'att_k_norm_g': 1.0 + nrm(ks[14], (DEPTH, HEAD_DIM), 0.02),
        'w_br_ret': nrm(ks[15], (DEPTH, D_RET_V, D_MODEL), D_RET_V ** -0.5),
        'w_br_att': nrm(ks[16], (DEPTH, D_ATT_Q, D_MODEL), D_ATT_Q ** -0.5),
        'w_out': nrm(ks[17], (DEPTH, D_MODEL, D_MODEL), D_MODEL ** -0.5),
        'w_mlp_up': nrm(ks[18], (DEPTH, D_MODEL, D_FF), D_MODEL ** -0.5),
        'w_mlp_down': nrm(ks[19], (DEPTH, D_FF, D_MODEL), D_FF ** -0.5),
        'final_norm_g': 1.0 + nrm(ks[20], (D_MODEL,), 0.02),
    }


def reference(x, c, ctx, c_ctx, mod_w, mod_b, norm_mix_g, norm_mlp_g, w_in,
              ret_log_lam_fwd, ret_log_lam_bwd, ret_gn_w, ret_gn_b,
              att_q_norm_g, att_k_norm_g, w_br_ret, w_br_att, w_out,
              w_mlp_up, w_mlp_down, final_norm_g):
    n = x.shape[1]
    n_ctx = ctx.shape[1]
    rows = n // GRID_W
    t_row = jnp.repeat(jnp.arange(rows), GRID_W)
    t_col = jnp.tile(jnp.arange(GRID_W), rows)
    cr, sr = _rope_angles(t_row, HEAD_DIM // 2, ROPE_THETA)
    cc, sc = _rope_angles(t_col, HEAD_DIM // 2, ROPE_THETA)
    ax_tabs = (cr, sr, cc, sc)
    ret_cos_c, ret_sin_c = _rope_angles(jnp.arange(n_ctx), RET_QK_DIM, RET_ROPE_THETA)
    ret_cos_l, ret_sin_l = _rope_angles(n_ctx + jnp.arange(n), RET_QK_DIM, RET_ROPE_THETA)

    silu_c = jax.nn.silu(c)
    silu_cc = jax.nn.silu(c_ctx)
    x_l, x_c = x, ctx
    flip = lambda a: jnp.flip(a, axis=2)

    for layer in range(DEPTH):
        upd_ctx = layer < DEPTH - 1
        mod_lat = (silu_c @ mod_w[layer] + mod_b[layer])[:, None, :]
        mod_ctx = (silu_cc @ mod_w[layer] + mod_b[layer])[None, None, :]
        sh_m_l, sc_m_l, g_m_l, sh_f_l, sc_f_l, g_f_l = jnp.split(mod_lat, 6, axis=-1)
        sh_m_c, sc_m_c, g_m_c, sh_f_c, sc_f_c, g_f_c = jnp.split(mod_ctx, 6, axis=-1)

        h_l = _rms(x_l, norm_mix_g[layer]) * (1.0 + sc_m_l) + sh_m_l
        h_c = _rms(x_c, norm_mix_g[layer]) * (1.0 + sc_m_c) + sh_m_c
        rq_l, rk_l, rv_l, rg_l, aq_l, ak_l, av_l, gr_l, ga_l = _split_proj(h_l @ w_in[layer])
        rq_c, rk_c, rv_c, rg_c, aq_c, ak_c, av_c, gr_c, ga_c = _split_proj(h_c @ w_in[layer])

        lg_f = -jnp.exp(ret_log_lam_fwd[layer].astype(jnp.float32))
        lg_b = -jnp.exp(ret_log_lam_bwd[layer].astype(jnp.float32))
        q_c, k_c, v_c = _ret_heads(rq_c, rk_c, rv_c, ret_cos_c, ret_sin_c)
        q_l, k_l, v_l = _ret_heads(rq_l, rk_l, rv_l, ret_cos_l, ret_sin_l)
        b = x_l.shape[0]
        zero = jnp.zeros((b, RET_HEADS, RET_QK_DIM, RET_V_DIM), jnp.float32)
        oc_f, s_f = _retention_dir(q_c, k_c, v_c, lg_f, zero, True)
        oc_b, s_b = _retention_dir(flip(q_c), flip(k_c), flip(v_c), lg_b, zero, False)
        ol_f, _ = _retention_dir(q_l, k_l, v_l, lg_f, s_f, True)
        ol_b, _ = _retention_dir(flip(q_l), flip(k_l), flip(v_l), lg_b, s_b, False)
        ret_l = _ret_out(ol_f + flip(ol_b), rg_l, ret_gn_w[layer], ret_gn_b[layer])

        aq_l, ak_l, av_l = _att_heads(aq_l, ak_l, av_l, att_q_norm_g[layer], att_k_norm_g[layer])
        aq_c, ak_c, av_c = _att_heads(aq_c, ak_c, av_c, att_q_norm_g[layer], att_k_norm_g[layer])
        aq_l = _axial_rotate(aq_l, ax_tabs)
        ak_l = _axial_rotate(ak_l, ax_tabs)
        k_all = jnp.concatenate([ak_l, ak_c], axis=2)
        v_all = jnp.concatenate([av_l, av_c], axis=2)
        att_l = _attend_blocks(aq_l, k_all, v_all)

        y_l = _merge(ret_l, att_l, gr_l, ga_l, w_br_ret[layer], w_br_att[layer], w_out[layer])
        x_l = x_l + g_m_l * y_l
        if upd_ctx:
            ret_c = _ret_out(oc_f + flip(oc_b), rg_c, ret_gn_w[layer], ret_gn_b[layer])
            att_c = _attend_blocks(aq_c, ak_c, av_c)
            y_c = _merge(ret_c, att_c, gr_c, ga_c, w_br_ret[layer], w_br_att[layer], w_out[layer])
            x_c = x_c + g_m_c * y_c

        f_l = _rms(x_l, norm_mlp_g[layer]) * (1.0 + sc_f_l) + sh_f_l
        x_l = x_l + g_f_l * _sq_relu_mlp(f_l, w_mlp_up[layer], w_mlp_down[layer])
        if upd_ctx:
            f_c = _rms(x_c, norm_mlp_g[layer]) * (1.0 + sc_f_c) + sh_f_c
            x_c = x_c + g_f_c * _sq_relu_mlp(f_c, w_mlp_up[layer], w_mlp_down[layer])

    return _rms(x_l, final_norm_g)
```

```python
import numpy as np
from contextlib import ExitStack
import concourse.bass as bass
import concourse.mybir as mybir
from concourse.bass_utils import run_bass_kernel_spmd

F32 = mybir.dt.float32
BF16 = mybir.dt.bfloat16
AF = mybir.ActivationFunctionType
ALU = mybir.AluOpType
AX = mybir.AxisListType

D = 1024
NTOK = 4096
NCTX = 256
NCH = NTOK // 128
NSC = NCH + NCTX // 128
DIN = 6656
DFF = 4096
EPS = 1e-6
NWT = 35
WT_IN = 0
WT_BR_RET = 13
WT_BR_ATT = 15
WT_OUT = 17
WT_UP = 19
WT_DOWN = 27


class Buf:
    def __init__(self, name, t):
        self.name = name
        self.t = t
        self.w = None
        self.r = {}

    def __getitem__(self, idx):
        return self.t[idx]


class KB:
    ENG = ("pe", "act", "dve", "pool", "sp")

    def __init__(self, nc, stack):
        self.nc = nc
        self.stack = stack
        self.q = {e: [] for e in self.ENG}
        self.cnt = {e: 0 for e in self.ENG}
        self.waited = {e: {} for e in self.ENG}
        self.sems = {}
        self.dcnt = {}
        for e in self.ENG:
            self.sems[e] = stack.enter_context(nc.semaphore("s_" + e))
        self.final_waits = {}

    def sem(self, key):
        if key not in self.sems:
            self.sems[key] = self.stack.enter_context(self.nc.semaphore("s_" + key))
            self.dcnt[key] = 0
        return self.sems[key]

    def sb(self, name, shape, dt):
        return Buf(name, self.stack.enter_context(self.nc.sbuf_tensor("sb_" + name, list(shape), dt)))

    def _deps(self, eng, reads, writes):
        deps = {}

        def add(k, v):
            if deps.get(k, 0) < v:
                deps[k] = v

        for b in reads:
            if b.w is not None:
                add(*b.w)
        for b in writes:
            if b.w is not None:
                add(*b.w)
            for k, v in b.r.items():
                add(k, v)
        waits = []
        for k, v in deps.items():
            if k == eng:
                if eng in ("pe", "sp"):
                    continue
                if v > self.cnt[eng]:
                    continue
            if self.waited[eng].get(k, 0) >= v:
                continue
            self.waited[eng][k] = v
            waits.append((k, v))
        return waits

    def _mark(self, tok, reads, writes):
        for b in writes:
            b.w = tok
            b.r = {}
        for b in reads:
            if b in writes:
                continue
            k, v = tok
            if b.r.get(k, 0) < v:
                b.r[k] = v

    dry = False

    def op(self, eng, fn, reads=(), writes=(), inc=True):
        if self.dry:
            return None
        pr = [b for b in reads if getattr(b, "psum", False)]
        if pr:
            reads = [b for b in reads if not getattr(b, "psum", False)]
            writes = list(writes) + [b for b in pr if b not in writes]
        waits = self._deps(eng, reads, writes)
        if inc:
            self.cnt[eng] += 1
            tok = (eng, self.cnt[eng])
        else:
            tok = (eng, self.cnt[eng] + 1)
        sems = self.sems

        def thunk(e, waits=waits, fn=fn, inc=inc, eng=eng):
            for k, v in waits:
                e.wait_ge(sems[k], v)
            ins = fn(e)
            if inc:
                ins.then_inc(sems[eng], 1)

        self.q[eng].append(thunk)
        self._mark(tok, reads, writes)
        return tok

    def dma(self, eng, out, in_, reads, writes, key, group_total=None, **kw):
        if self.dry:
            return None
        s = self.sem(key)
        waits = self._deps(eng, reads, writes)
        self.dcnt[key] += 1
        val = 16 * (group_total if group_total is not None else self.dcnt[key])
        tok = (key, val)
        sems = self.sems

        def thunk(e, waits=waits):
            for k, v in waits:
                e.wait_ge(sems[k], v)
            e.dma_start(out=out, in_=in_, **kw).then_inc(s, 16)

        self.q[eng].append(thunk)
        self._mark(tok, reads, writes)
        return tok

    def wait_all(self, eng, toks):
        sems = self.sems

        def thunk(e):
            for k, v in toks:
                e.wait_ge(sems[k], v)

        self.q[eng].append(thunk)

    def run(self):
        nc = self.nc
        with nc.Block() as block:
            @block.tensor
            def _(e):
                for f in self.q["pe"]:
                    f(e)

            @block.scalar
            def _(e):
                for f in self.q["act"]:
                    f(e)

            @block.vector
            def _(e):
                for f in self.q["dve"]:
                    f(e)

            @block.gpsimd
            def _(e):
                for f in self.q["pool"]:
                    f(e)

            @block.sync
            def _(e):
                for f in self.q["sp"]:
                    f(e)


def host_consts():
    i = np.arange(128, dtype=np.float32)
    ident = np.eye(128, dtype=np.float32)
    iota1 = np.tile((i + 1)[None, :], (128, 1))
    iota2 = np.tile((128 - i)[None, :], (128, 1))
    jj = i[:, None]
    ii = i[None, :]
    A = np.maximum(ii - jj, 0)
    B = np.maximum(jj - ii, 0)
    M1 = (ii >= jj).astype(np.float32)
    M2 = (jj > ii).astype(np.float32)
    p127 = (127 - i)[:, None]
    pj = i[:, None]
    return np.ascontiguousarray(
        np.concatenate([ident, iota1, iota2, A, B, M1, M2, p127, pj], axis=1).astype(np.float32))


def host_tabs():
    f32 = np.float32
    half = 64
    inv = (f32(10000.0) ** (-(np.arange(half, dtype=f32) / f32(half)))).astype(f32)
    pos = np.arange(NCTX + NTOK, dtype=f32)
    ang = (pos[:, None] * inv[None, :]).astype(f32)
    c, s = np.cos(ang).astype(f32), np.sin(ang).astype(f32)
    rt = np.concatenate([c, c, -s, s], axis=1).astype(f32)
    h2 = 32
    inv2 = (f32(10000.0) ** (-(np.arange(h2, dtype=f32) / f32(h2)))).astype(f32)
    t = np.arange(NTOK)
    row = (t // 64).astype(f32)
    col = (t % 64).astype(f32)
    ar = (row[:, None] * inv2[None, :]).astype(f32)
    ac = (col[:, None] * inv2[None, :]).astype(f32)
    cr, sr, cc, sc = np.cos(ar), np.sin(ar), np.cos(ac), np.sin(ac)
    at = np.concatenate([cr, cr, cc, cc, -sr, sr, -sc, sc], axis=1).astype(f32)
    return np.ascontiguousarray(rt), np.ascontiguousarray(at)


C_ID, C_I1, C_I2, C_A, C_B, C_M1, C_M2 = [k * 128 for k in range(7)]
C_P127 = 7 * 128
C_PJ = 7 * 128 + 1
NCONST = 7 * 128 + 2
V_C, V_CC, V_MODB, V_GMIX, V_GMLP = 0, 8, 16, 64, 72
NVEC = 80
BC_GNW, BC_GNB, BC_FIN, BC_QG, BC_KG = 0, 1024, 2048, 3072, 3200
NBC = 3328


def build(debug=None, stop_after=None):
    debug = debug or []
    nc = bass.Bass("TRN2", target_bir_lowering=False)

    def din(name, shape, dt=F32):
        return nc.dram_tensor(name, list(shape), dt, kind="ExternalInput").ap()

    x_d = din("x", [NTOK, D])
    ctx_d = din("ctx", [NCTX, D])
    vecs_d = din("vecs", [128, NVEC])
    lam_d = din("lam", [128, 8])
    bc_d = din("bc", [128, NBC])
    rt_d = din("rt", [NCTX + NTOK, 256])
    at_d = din("at", [NTOK, 256])
    consts_d = din("consts", [128, NCONST])
    modw_d = din("mod_w", [D, 6 * D])
    win_d = din("w_in", [D, DIN])
    wbr_ret_d = din("w_br_ret", [D, D])
    wbr_att_d = din("w_br_att", [D, D])
    wout_d = din("w_out", [D, D])
    wup_d = din("w_mlp_up", [D, DFF])
    wdown_d = din("w_mlp_down", [DFF, D])
    y_d = nc.dram_tensor("y", [NTOK, D], F32, kind="ExternalOutput").ap()
    wsc_d = nc.dram_tensor("wsc", [NWT, 128, 8, 512], BF16, kind="Internal").ap()
    sbs_d = nc.dram_tensor("sbs_scr", [NCH, 128, 4, 256], BF16, kind="Internal").ap()
    dbg_out = {}

    stack = ExitStack()
    with stack:
        kb = KB(nc, stack)
        sb = kb.sb
        psF_t = stack.enter_context(nc.psum_tensor("psF", [128, 7, 512], F32))
        psT_t = stack.enter_context(nc.psum_tensor("psT", [128, 8, 128], BF16))
        PB = [Buf("psb%d" % i, None) for i in range(7)]
        PT = Buf("psT", psT_t)
        for _b in PB + [PT]:
            _b.psum = True

        def bank(i):
            return psF_t[:, i, :]

        def bank2(i):
            return psF_t[:, i:i + 2, :]

        consts = sb("consts", [128, NCONST], F32)
        vecs = sb("vecs", [128, NVEC], F32)
        lam = sb("lam", [128, 8], F32)
        bcv = sb("bcv", [128, NBC], F32)
        identb = sb("identb", [128, 128], BF16)
        onesf = sb("onesf", [128, 128], F32)
        negh = sb("negh", [128, 16], F32)
        modv = sb("modv", [128, 48, 2], F32)
        silu_c = sb("silu_c", [128, 8, 2], F32)
        gvec = sb("gvec", [128, 8, 8], F32)
        gm_bc = sb("gm_bc", [128, 1024], F32)
        gf_bc = sb("gf_bc", [128, 1024], F32)
        LG = sb("LG", [128, 8], F32)
        DT = sb("DT", [128, 4, 128], F32)
        QF = sb("QF", [128, 4, 128], F32)
        QB = sb("QB", [128, 4, 128], F32)
        KD = sb("KD", [128, 8], F32)
        CD = sb("CD", [128, 8], F32)
        KT = sb("KT", [128, 2, NSC * 128], BF16)
        VA = sb("VA", [128, NSC, 2, 129], BF16)
        SBSD = [Buf("sbsd%d" % i, None) for i in range(NCH)]
        sbs_ring = [sb("sbsr%d" % i, [128, 4, 256], BF16) for i in range(2)]
        Sf = sb("Sf", [128, 4, 256], F32)
        Sb_ = sb("Sb", [128, 4, 256], F32)
        Sf_bf = sb("Sf_bf", [128, 4, 256], BF16)
        wringF = [sb("wF%d" % i, [128, 8, 512], BF16) for i in range(2)]
        wringM = [sb("wM%d" % i, [128, 8, 512], BF16) for i in range(2)]
        xring = [sb("x%d" % i, [128, 1024], F32) for i in range(2)]
        rtab = [sb("rtab%d" % i, [128, 256], F32) for i in range(2)]
        atab = [sb("atab%d" % i, [128, 256], F32) for i in range(2)]
        st4 = sb("st4", [128, 16], F32)
        st4b = sb("st4b", [128, 16], F32)
        xs = sb("xs", [128, 1024], BF16)
        hT = sb("hT", [128, 8, 128], BF16)
        f1 = sb("f1", [128, 1024], F32)
        f2 = sb("f2", [128, 1024], F32)
        f3 = sb("f3", [128, 1024], F32)
        qk_tok = sb("qk_tok", [128, 8, 128], BF16)
        qT = sb("qT", [128, 4, 128], BF16)
        qfT = sb("qfT", [128, 4, 128], BF16)
        qbT = sb("qbT", [128, 4, 128], BF16)
        kT = sb("kT", [128, 4, 128], BF16)
        kdf = sb("kdf", [128, 4, 128], BF16)
        v_tok = sb("v_tok", [128, 4, 256], BF16)
        sg = sb("sg", [128, 1024], BF16)
        aq_tok = sb("aq_tok", [128, 8, 128], BF16)
        ak_tok = sb("ak_tok", [128, 2, 128], BF16)
        PTm = sb("PTm", [128, 4, 128], BF16)
        kdb = PTm
        oret = sb("oret", [128, 1024], BF16)
        bnst = sb("bnst", [128, 4, 6], F32)
        bnag = sb("bnag", [128, 4, 2], F32)
        QT = [sb("QT%d" % i, [128, 8, 128], BF16) for i in range(2)]
        oretT = [sb("oretT%d" % i, [128, 8, 128], BF16) for i in range(2)]
        sgr = [sb("sgr%d" % i, [128, 1024], BF16) for i in range(2)]
        sga = [sb("sga%d" % i, [128, 1024], BF16) for i in range(2)]
        ET = [sb("ET%d" % i, [128, 512], BF16) for i in range(2)]
        for _i in range(2):
            _b = Buf("ETc%d" % _i, None)
            _b.t = consts[:, _i * 256:(_i + 1) * 256].bitcast(BF16)
            ET.append(_b)
        oatt_tok = sb("oatt_tok", [128, 8, 128], BF16)
        oattT = sb("oattT", [128, 8, 128], BF16)
        rs8 = sb("rs8", [128, 8], F32)
        m1 = sb("m1", [128, 1024], F32)
        m1h = [Buf("m1a", None), Buf("m1b", None)]
        m2 = sb("m2", [128, 1024], F32)
        xl = sb("xl", [128, 1024], F32)
        mrg = sb("mrg", [128, 1024], BF16)
        mT = sb("mT", [128, 8, 128], BF16)
        xs2 = sb("xs2", [128, 1024], BF16)
        hT2 = sb("hT2", [128, 8, 128], BF16)
        hid = sb("hid", [128, 32, 128], BF16)
        st4m = sb("st4m", [128, 16], F32)

        def cslice(off, n=128):
            return consts[:, off:off + n]

        WSC = [Buf("wsc%d" % i, None) for i in range(NWT)]

        def precast(tile, src, group, total):
            kb.dma("pool", wsc_d[tile], src.rearrange("(k p) n -> p k n", p=128), [], [WSC[tile]],
                   key="pc_" + group, group_total=total)

        for b in (1, 2, 3, 8):
            precast(WT_IN + b, win_d[:, b * 512:(b + 1) * 512], "a", 4)
        for b in (6, 7, 0, 4, 5, 9, 10, 11, 12):
            precast(WT_IN + b, win_d[:, b * 512:(b + 1) * 512], "b", 9)
        for cb in range(2):
            precast(WT_BR_RET + cb, wbr_ret_d[:, cb * 512:(cb + 1) * 512], "c", 6)
        for cb in range(2):
            precast(WT_BR_ATT + cb, wbr_att_d[:, cb * 512:(cb + 1) * 512], "c", 6)
        for cb in range(2):
            precast(WT_OUT + cb, wout_d[:, cb * 512:(cb + 1) * 512], "c", 6)
        for cb in range(8):
            precast(WT_UP + cb, wup_d[:, cb * 512:(cb + 1) * 512], "d", 8)
        for cb in range(2):
            for kg in range(4):
                precast(WT_DOWN + cb * 4 + kg,
                        wdown_d[kg * 1024:(kg + 1) * 1024, cb * 512:(cb + 1) * 512], "e", 8)

        kb.dma("sp", consts[:], consts_d[:, :], [], [consts], key="c0")
        kb.dma("sp", vecs[:], vecs_d[:, :], [], [vecs], key="c1")
        kb.dma("sp", lam[:], lam_d[:, :], [], [lam], key="c2")
        kb.dma("sp", bcv[:], bc_d[:, :], [], [bcv], key="c3")
        kb.op("dve", lambda e: e.tensor_copy(out=identb[:], in_=cslice(C_ID)), [consts], [identb])
        kb.op("pool", lambda e: e.memset(onesf[:], 1.0), [], [onesf])
        kb.op("pool", lambda e: e.memset(negh[:], -0.5), [], [negh])
        kb.op("pool", lambda e: e.memset(Sf[:], 0.0), [], [Sf])
        kb.op("pool", lambda e: e.memset(Sb_[:], 0.0), [], [Sb_])
        kb.op("pool", lambda e: e.memset(VA[:, :, :, 128:129], 1.0), [], [VA])

        kb.op("act", lambda e: e.activation(out=LG[:], in_=lam[:], func=AF.Exp), [lam], [LG])
        kb.op("dve", lambda e: e.tensor_scalar(out=LG[:], in0=LG[:], scalar1=-1.0, scalar2=None, op0=ALU.mult),
              [LG], [LG])
        sc_k = 128.0 ** -0.5
        for h in range(4):
            kb.op("act", lambda e, h=h: e.activation(out=QF[:, h, :], in_=cslice(C_I1), func=AF.Exp,
                                                     scale=LG[:, h:h + 1]), [LG, consts], [QF])
            kb.op("act", lambda e, h=h: e.activation(out=QB[:, h, :], in_=cslice(C_I2), func=AF.Exp,
                                                     scale=LG[:, 4 + h:5 + h]), [LG, consts], [QB])
            kb.op("act", lambda e, h=h: e.activation(out=KD[:, h:h + 1], in_=consts[:, C_P127:C_P127 + 1],
                                                     func=AF.Exp, scale=LG[:, h:h + 1]), [LG, consts], [KD])
            kb.op("act", lambda e, h=h: e.activation(out=KD[:, 4 + h:5 + h], in_=consts[:, C_PJ:C_PJ + 1],
                                                     func=AF.Exp, scale=LG[:, 4 + h:5 + h]), [LG, consts], [KD])
            kb.op("act", lambda e, h=h: e.activation(out=f1[:, 0:128], in_=cslice(C_A), func=AF.Exp,
                                                     scale=LG[:, h:h + 1]), [LG, consts], [f1])
            kb.op("act", lambda e, h=h: e.activation(out=f2[:, 0:128], in_=cslice(C_B), func=AF.Exp,
                                                     scale=LG[:, 4 + h:5 + h]), [LG, consts], [f2])
            kb.op("dve", lambda e: e.tensor_tensor(out=f1[:, 0:128], in0=f1[:, 0:128], in1=cslice(C_M1),
                                                   op=ALU.mult), [f1, consts], [f1])
            kb.op("dve", lambda e: e.tensor_tensor(out=f2[:, 0:128], in0=f2[:, 0:128], in1=cslice(C_M2),
                                                   op=ALU.mult), [f2, consts], [f2])
            kb.op("dve", lambda e: e.tensor_tensor(out=f1[:, 0:128], in0=f1[:, 0:128], in1=f2[:, 0:128],
                                                   op=ALU.add), [f1, f2], [f1])
            kb.op("dve", lambda e, h=h: e.tensor_scalar(out=DT[:, h, :], in0=f1[:, 0:128], scalar1=sc_k,
                                                        scalar2=None, op0=ALU.mult), [f1], [DT])
        kb.op("dve", lambda e: e.tensor_scalar(out=KD[:], in0=KD[:], scalar1=sc_k, scalar2=None, op0=ALU.mult),
              [KD], [KD])
        kb.op("act", lambda e: e.activation(out=CD[:], in_=LG[:], func=AF.Exp, scale=128.0), [LG], [CD])

        for col in range(2):
            kb.op("act", lambda e, col=col: e.activation(out=silu_c[:, :, col], in_=vecs[:, V_C + 8 * col:V_C + 8 * col + 8],
                                                         func=AF.Silu), [vecs], [silu_c])
        MODPS = PB[0]
        ring4 = wringF + wringM
        for slab in range(24):
            mwb = ring4[slab % 4]
            mw = mwb[:].bitcast(F32)
            kb.dma("sp" if slab % 2 == 0 else "act", mw,
                   modw_d[:, slab * 256:(slab + 1) * 256].rearrange("(k p) n -> p k n", p=128),
                   [], [mwb], key="modw%d" % (slab % 4))
            for jj in range(2):
                j = slab * 2 + jj
                for k in range(8):
                    kb.op("pe", lambda e, j=j, jj=jj, k=k, mw=mw: e.matmul(
                        psF_t[:, 0, 2 * j:2 * j + 2], lhsT=mw[:, k, jj * 128:(jj + 1) * 128], rhs=silu_c[:, k, :],
                        start=(k == 0), stop=(k == 7)), [mwb, silu_c], [MODPS], inc=(k == 7))
        kb.op("dve", lambda e: e.tensor_tensor(
            out=modv[:], in0=psF_t[:, 0, 0:96].rearrange("p (j c) -> p j c", c=2),
            in1=vecs[:, V_MODB:V_MODB + 48].unsqueeze(2).broadcast_to([128, 48, 2]), op=ALU.add),
            [MODPS, vecs], [modv])
        for (dst, gcol, sc0, col) in ((0, V_GMIX, 8, 0), (2, V_GMIX, 8, 1), (4, V_GMLP, 32, 0)):
            kb.op("dve", lambda e, dst=dst, gcol=gcol, sc0=sc0, col=col: e.scalar_tensor_tensor(
                out=gvec[:, :, dst], in0=modv[:, sc0:sc0 + 8, col], scalar=1.0, in1=vecs[:, gcol:gcol + 8],
                op0=ALU.add, op1=ALU.mult), [modv, vecs], [gvec])
        for (dst, j0, col) in ((1, 0, 0), (3, 0, 1), (5, 24, 0), (6, 16, 0), (7, 40, 0)):
            kb.op("dve", lambda e, dst=dst, j0=j0, col=col: e.tensor_copy(out=gvec[:, :, dst], in_=modv[:, j0:j0 + 8, col]),
                  [modv], [gvec])
        for (dstb, gi, pb) in ((gm_bc, 6, 1), (gf_bc, 7, 3)):
            for k in range(8):
                kb.op("dve", lambda e, k=k, gi=gi: e.tensor_scalar(
                    out=f3[:, k * 128:(k + 1) * 128], in0=cslice(C_ID), scalar1=gvec[:, k, gi:gi + 1], scalar2=None,
                    op0=ALU.mult), [consts, gvec], [f3])
            for k in range(8):
                kb.op("pe", lambda e, k=k, pb=pb: e.matmul(
                    psF_t[:, pb + k // 4, (k % 4) * 128:(k % 4 + 1) * 128], lhsT=onesf[:], rhs=f3[:, k * 128:(k + 1) * 128],
                    start=True, stop=True), [onesf, f3], [PB[pb + k // 4]], inc=True)
            kb.op("act", lambda e, dstb=dstb, pb=pb: e.activation(out=dstb[:].rearrange("p (a n) -> p a n", a=2),
                                                                  in_=bank2(pb), func=AF.Identity),
                  [PB[pb], PB[pb + 1]], [dstb])

        def rstd_pool(ss_ap, n, scale, dst_ap, ssbuf, dstbuf):
            kb.op("pool", lambda e: e.tensor_scalar(out=dst_ap, in0=ss_ap, scalar1=scale, scalar2=EPS,
                                                    op0=ALU.mult, op1=ALU.add), [ssbuf], [dstbuf])
            kb.op("pool", lambda e: e.tensor_tensor(out=dst_ap, in0=dst_ap, in1=negh[:, 0:n], op=ALU.pow),
                  [dstbuf, negh], [dstbuf])

        def norm_part1(xt, xs_b, st_b):
            kb.op("act", lambda e: e.activation(out=xs_b[:], in_=xt[:], func=AF.Square, accum_out=st_b[:, 0:1]),
                  [xt], [xs_b, st_b])
            rstd_pool(st_b[:, 0:1], 1, 1.0 / D, st_b[:, 1:2], st_b, st_b)
            kb.op("dve", lambda e: e.tensor_scalar(out=xs_b[:], in0=xt[:], scalar1=st_b[:, 1:2], scalar2=None, op0=ALU.mult),
                  [xt, st_b], [xs_b])

        def norm_part2(gi, dst_hT, xs_b, tmp_b):
            for k in range(8):
                kb.op("pe", lambda e, k=k: e.transpose(out=psT_t[:, k, :], in_=xs_b[:, k * 128:(k + 1) * 128], identity=identb[:]),
                      [xs_b, identb], [PT], inc=(k == 7))
            kb.op("dve", lambda e: e.tensor_tensor(
                out=tmp_b[:].rearrange("p (k t) -> p k t", k=8), in0=psT_t[:],
                in1=gvec[:, :, gi:gi + 1].broadcast_to([128, 8, 128]), op=ALU.mult), [PT, gvec], [tmp_b])
            kb.op("pool", lambda e: e.tensor_tensor(
                out=dst_hT[:], in0=tmp_b[:].rearrange("p (k t) -> p k t", k=8),
                in1=gvec[:, :, gi + 1:gi + 2].broadcast_to([128, 8, 128]), op=ALU.add), [tmp_b, gvec], [dst_hT])

        def mm8(hTb, wb, pb, out_ap=None):
            for k in range(8):
                kb.op("pe", lambda e, k=k: e.matmul(bank(pb) if out_ap is None else out_ap, lhsT=hTb[:, k, :], rhs=wb[:, k, :],
                                                    start=(k == 0), stop=(k == 7)),
                      [hTb, wb], [PB[pb]], inc=(k == 7))

        def rotate(dst, dst_bufs, src_ap, src_bufs, tab, H, a):
            h = 128 // (2 * a)
            t1 = f1[:, 0:H * 128].rearrange("p (H d) -> p H d", H=H)
            t2 = f2[:, 0:H * 128].rearrange("p (H d) -> p H d", H=H)
            kb.op("dve", lambda e: e.tensor_tensor(out=t1, in0=src_ap, in1=tab[:, 0:128].unsqueeze(1).broadcast_to([128, H, 128]),
                                                   op=ALU.mult), src_bufs + [tab], [f1])
            for ai in range(a):
                for two in range(2):
                    o0 = ai * 2 * h + two * h
                    s0 = ai * 2 * h + (1 - two) * h
                    kb.op("dve", lambda e, o0=o0, s0=s0: e.tensor_tensor(
                        out=t2[:, :, o0:o0 + h], in0=src_ap[:, :, s0:s0 + h],
                        in1=tab[:, 128 + o0:128 + o0 + h].unsqueeze(1).broadcast_to([128, H, h]), op=ALU.mult),
                        src_bufs + [tab], [f2])
            kb.op("pool", lambda e: e.tensor_tensor(out=dst, in0=t1, in1=t2, op=ALU.add), [f1, f2], dst_bufs)

        def head_rms(src_ap, src_bufs, H, gcol, dstf):
            sq = f1[:, 0:H * 128]
            kb.op("act", lambda e: e.activation(out=sq.rearrange("p (H d) -> p H d", H=H), in_=src_ap, func=AF.Square),
                  src_bufs, [f1])
            kb.op("dve", lambda e: e.tensor_reduce(out=st4b[:, 0:H], in_=sq.rearrange("p (H d) -> p H d", H=H),
                                                   axis=AX.X, op=ALU.add), [f1], [st4b])
            rstd_pool(st4b[:, 0:H], H, 1.0 / 128, st4b[:, 0:H], st4b, st4b)
            kb.op("dve", lambda e: e.tensor_tensor(out=dstf, in0=src_ap,
                                                   in1=st4b[:, 0:H].unsqueeze(2).broadcast_to([128, H, 128]), op=ALU.mult),
                  src_bufs + [st4b], [f3])
            kb.op("pool", lambda e: e.tensor_tensor(out=dstf, in0=dstf,
                                                    in1=bcv[:, gcol:gcol + 128].unsqueeze(1).broadcast_to([128, H, 128]),
                                                    op=ALU.mult), [f3, bcv], [f3])

        def transposes(src_ap_fn, src_bufs, n, dst_ap, dst_bufs, eng="dve"):
            for k in range(n):
                kb.op("pe", lambda e, k=k: e.transpose(out=psT_t[:, k, :], in_=src_ap_fn(k), identity=identb[:]),
                      src_bufs + [identb], [PT], inc=(k == n - 1))
            if eng == "act":
                kb.op("act", lambda e: e.activation(out=dst_ap, in_=psT_t[:, 0:n, :], func=AF.Identity), [PT], dst_bufs)
            else:
                kb.op("dve", lambda e: e.tensor_copy(out=dst_ap, in_=psT_t[:, 0:n, :]), [PT], dst_bufs)

        def state_update_half(S, kd, cdoff, pb, hh):
            for h in (2 * hh, 2 * hh + 1):
                kb.op("pe", lambda e, h=h: e.matmul(psF_t[:, pb, (h % 2) * 256:(h % 2 + 1) * 256],
                                                    lhsT=kd[:, h, :], rhs=v_tok[:, h, :], start=True, stop=True),
                      [kd, v_tok], [PB[pb]], inc=True)
            for h in (2 * hh, 2 * hh + 1):
                kb.op("dve", lambda e, h=h: e.scalar_tensor_tensor(
                    out=S[:, h, :], in0=S[:, h, :], scalar=CD[:, cdoff + h:cdoff + h + 1],
                    in1=psF_t[:, pb, (h % 2) * 256:(h % 2 + 1) * 256], op0=ALU.mult, op1=ALU.add),
                    [S, CD, PB[pb]], [S])

        def dump(name, buf, shape, dt=F32):
            if name not in debug or kb.dry:
                return
            d = nc.dram_tensor("dbg_" + name, list(shape), dt, kind="ExternalOutput").ap()
            dbg_out[name] = d
            tok = kb.dma("sp", d, buf[:], [buf], [], key="dbg_" + name)
            kb.final_waits[tok[0]] = tok[1]

        WA = {}
        for i, b in enumerate((1, 2, 3, 8)):
            slot = (wringF + wringM)[i]
            kb.dma("sp", slot[:], wsc_d[WT_IN + b], [WSC[WT_IN + b]], [slot], key="wA%d" % i)
            WA[b] = slot

        tilesA = [("c", 0, "f"), ("c", 1, "fb"), ("c", 0, "b")] + [("l", c, "s") for c in range(NCH - 1, -1, -1)]
        if stop_after == "const":
            tilesA = []

        def loadA(ti):
            kind, c, mode = tilesA[ti]
            xt = xring[ti % 2]
            rtb = rtab[ti % 2]
            atb = atab[ti % 2]
            if kind == "c":
                src = ctx_d[c * 128:(c + 1) * 128, :]
                pos0 = c * 128
            else:
                src = x_d[c * 128:(c + 1) * 128, :]
                pos0 = NCTX + c * 128
            kb.dma("sp", xt[:], src, [], [xt], key="x%d" % (ti % 2))
            kb.dma("sp", rtb[:], rt_d[pos0:pos0 + 128, :], [], [rtb], key="rt%d" % (ti % 2))
            if kind == "l":
                kb.dma("sp", atb[:], at_d[c * 128:(c + 1) * 128, :], [], [atb], key="at%d" % (ti % 2))

        if tilesA:
            loadA(0)
        for ti, (kind, c, mode) in enumerate(tilesA):
            xt = xring[ti % 2]
            rtb = rtab[ti % 2]
            atb = atab[ti % 2]
            sc_idx = (NCH + c) if kind == "c" else c
            if ti + 1 < len(tilesA):
                loadA(ti + 1)
            norm_part1(xt, xs, st4)
            norm_part2(0 if kind == "l" else 2, hT, xs, f3)
            for b, pb in ((1, 0), (2, 1), (3, 2), (8, 3)):
                mm8(hT, WA[b], pb)
            rotate(qk_tok[:, 4:8, :], [qk_tok], psF_t[:, 0, :].rearrange("p (H d) -> p H d", H=4), [PB[0]], rtb, 4, 1)
            kb.op("act", lambda e: e.activation(out=v_tok[:].rearrange("p h v -> p (h v)").rearrange("p (a n) -> p a n", a=2),
                                                in_=bank2(1), func=AF.Identity), [PB[1], PB[2]], [v_tok])
            do_kv = not (kind == "c" and mode == "b")
            if do_kv:
                akf = f3[:, 0:256].rearrange("p (H d) -> p H d", H=2)
                head_rms(psF_t[:, 3, 0:256].rearrange("p (H d) -> p H d", H=2), [PB[3]], 2, BC_KG, akf)
                if kind == "l":
                    rotate(ak_tok[:], [ak_tok], akf, [f3], atb, 2, 2)
                else:
                    kb.op("pool", lambda e, akf=akf: e.tensor_copy(out=ak_tok[:], in_=akf), [f3], [ak_tok])
                transposes(lambda k: ak_tok[:, k, :], [ak_tok], 2,
                           KT[:, :, sc_idx * 128:(sc_idx + 1) * 128], [KT])
                kb.op("act", lambda e, sc_idx=sc_idx: e.activation(
                    out=VA[:, sc_idx, :, 0:128], in_=psF_t[:, 3, 256:512].rearrange("p (a d) -> p a d", a=2),
                    func=AF.Identity), [PB[3]], [VA])
            if "f" in mode:
                kb.op("dve", lambda e: e.tensor_tensor(out=kdf[:], in0=qk_tok[:, 4:8, :],
                                                       in1=KD[:, 0:4].unsqueeze(2).broadcast_to([128, 4, 128]), op=ALU.mult),
                      [qk_tok, KD], [kdf])
                state_update_half(Sf, kdf, 0, 4, 0)
                state_update_half(Sf, kdf, 0, 5, 1)
            if "b" in mode or mode == "s":
                if mode == "s":
                    sr = sbs_ring[c % 2]
                    kb.op("act", lambda e, sr=sr: e.activation(out=sr[:], in_=Sb_[:], func=AF.Identity), [Sb_], [sr])
                    kb.dma("sp", sbs_d[c], sr[:], [sr], [SBSD[c]], key="sbsw%d" % (c % 2))
                if not (mode == "s" and c == 0):
                    kb.op("dve", lambda e: e.tensor_tensor(out=kdb[:], in0=qk_tok[:, 4:8, :],
                                                           in1=KD[:, 4:8].unsqueeze(2).broadcast_to([128, 4, 128]), op=ALU.mult),
                          [qk_tok, KD], [kdb])
                    state_update_half(Sb_, kdb, 4, 4, 0)
                    state_update_half(Sb_, kdb, 4, 5, 1)

        dump("KT", KT, [128, 2, NSC * 128], BF16)
        dump("Sf", Sf, [128, 4, 256])

        chunksB = list(range(NCH))
        if stop_after in ("const", "A"):
            chunksB = []
        if isinstance(stop_after, tuple) and stop_after[0] == "B":
            chunksB = list(range(stop_after[1]))
        nB = len(chunksB)
        att_scale = 128.0 ** -0.5
        F_BLOCKS = (6, 7, 0, 1, 2, 3, 4, 5, 9, 10, 11, 12)
        planF = []
        planM = []
        for _ in chunksB:
            planF.extend([WT_IN + b for b in F_BLOCKS])
            planM.extend([WT_BR_RET, WT_BR_RET + 1, WT_BR_ATT, WT_BR_ATT + 1, WT_OUT, WT_OUT + 1] +
                         [WT_UP + i for i in range(8)] + [WT_DOWN + i for i in range(8)])
        ring4b = wringF + wringM
        wst = {"order": [], "index": {}, "issued": 0}

        def wtile(which, n):
            if kb.dry:
                wst["index"][(which, n)] = len(wst["order"])
                wst["order"].append((planF if which == "F" else planM)[n])
                return ring4b[0]
            g = wst["index"][(which, n)]
            order = wst["order"]
            nw = len(ring4b)
            while wst["issued"] < min(len(order), g + nw):
                i = wst["issued"]
                t = order[i]
                slot = ring4b[i % nw]
                kb.dma("sp", slot[:], wsc_d[t], [WSC[t]], [slot], key="wr%d" % (i % nw))
                wst["issued"] += 1
            return ring4b[g % nw]

        bstate = {"nb": 0}

        def nbank():
            i = bstate["nb"]
            bstate["nb"] += 1
            return 4 + (i % 3)

        x0 = len(tilesA)

        def load_chunk_inputs(cj):
            cc = chunksB[cj]
            s2 = (x0 + cj) % 2
            kb.dma("sp", xring[s2][:], x_d[cc * 128:(cc + 1) * 128, :], [], [xring[s2]], key="x%d" % s2)
            kb.dma("sp", rtab[s2][:], rt_d[NCTX + cc * 128:NCTX + (cc + 1) * 128, :], [], [rtab[s2]], key="rt%d" % s2)
            kb.dma("sp", atab[s2][:], at_d[cc * 128:(cc + 1) * 128, :], [], [atab[s2]], key="at%d" % s2)
            kb.dma("sp", sbs_ring[cj % 2][:], sbs_d[cc], [SBSD[cc]], [sbs_ring[cj % 2]], key="sbsr%d" % (cj % 2))

        def stage_F(cj):
            c = chunksB[cj]
            par = cj % 2
            s2 = (x0 + cj) % 2
            xt, rtb, atb, sr = xring[s2], rtab[s2], atab[s2], sbs_ring[cj % 2]
            QTd, oretTd, sgrd, sgad = QT[par], oretT[par], sgr[par], sga[par]
            wb0 = cj * 12
            norm_part1(xt, xs, st4)
            yield
            norm_part2(0, hT, xs, f3)
            yield
            for half in range(2):
                pb = nbank()
                mm8(hT, wtile("F", wb0 + half), pb)
                yield
                aqf = f3[:, half * 512:(half + 1) * 512].rearrange("p (H d) -> p H d", H=4)
                head_rms(bank(pb).rearrange("p (H d) -> p H d", H=4), [PB[pb]], 4, BC_QG, aqf)
                rotate(aq_tok[:, half * 4:(half + 1) * 4, :], [aq_tok], aqf, [f3], atb, 4, 2)
                yield
            for half in range(2):
                pb = nbank()
                mm8(hT, wtile("F", wb0 + 2 + half), pb)
                yield
                rotate(qk_tok[:, half * 4:(half + 1) * 4, :], [qk_tok],
                       bank(pb).rearrange("p (H d) -> p H d", H=4), [PB[pb]], rtb, 4, 1)
                if half == 0:
                    transposes(lambda k: aq_tok[:, k, :], [aq_tok], 8, QTd[:], [QTd])
                else:
                    kb.op("pool", lambda e: e.tensor_tensor(out=kdf[:], in0=qk_tok[:, 4:8, :],
                                                            in1=KD[:, 0:4].unsqueeze(2).broadcast_to([128, 4, 128]), op=ALU.mult),
                          [qk_tok, KD], [kdf])
                yield
            for half in range(2):
                pb = nbank()
                mm8(hT, wtile("F", wb0 + 4 + half), pb)
                yield
                kb.op("act", lambda e, half=half, pb=pb: e.activation(
                    out=v_tok[:, 2 * half:2 * half + 2, :].rearrange("p h v -> p (h v)"), in_=bank(pb), func=AF.Identity),
                    [PB[pb]], [v_tok])
                yield
            for k in range(8):
                kb.op("pe", lambda e, k=k: e.transpose(out=psT_t[:, k, :], in_=qk_tok[:, k, :], identity=identb[:]),
                      [qk_tok, identb], [PT], inc=(k == 7))
            kb.op("dve", lambda e: e.tensor_copy(out=qT[:], in_=psT_t[:, 0:4, :]), [PT], [qT])
            kb.op("dve", lambda e: e.tensor_tensor(out=qfT[:], in0=psT_t[:, 0:4, :], in1=QF[:], op=ALU.mult), [PT, QF], [qfT])
            kb.op("dve", lambda e: e.tensor_tensor(out=qbT[:], in0=psT_t[:, 0:4, :], in1=QB[:], op=ALU.mult), [PT, QB], [qbT])
            kb.op("dve", lambda e: e.tensor_copy(out=kT[:], in_=psT_t[:, 4:8, :]), [PT], [kT])
            kb.op("act", lambda e: e.activation(out=Sf_bf[:], in_=Sf[:], func=AF.Identity), [Sf], [Sf_bf])
            yield
            for half in range(2):
                pb = nbank()
                mm8(hT, wtile("F", wb0 + 6 + half), pb)
                yield
                hs = slice(half * 512, (half + 1) * 512)
                kb.op("act", lambda e, hs=hs, pb=pb: e.activation(out=f2[:, hs], in_=bank(pb), func=AF.Tanh, scale=0.5),
                      [PB[pb]], [f2])
                kb.op("dve", lambda e, hs=hs, pb=pb: e.scalar_tensor_tensor(out=sg[:, hs], in0=f2[:, hs], scalar=1.0, in1=bank(pb),
                                                                             op0=ALU.add, op1=ALU.mult), [f2, PB[pb]], [sg])
                yield
            pbs = nbank()
            for h in range(4):
                kb.op("pe", lambda e, h=h: e.matmul(psF_t[:, pbs, h * 128:(h + 1) * 128], lhsT=kT[:, h, :], rhs=qT[:, h, :],
                                                    start=True, stop=True), [kT, qT], [PB[pbs]], inc=(h == 3))
            yield
            kb.op("dve", lambda e: e.tensor_tensor(out=PTm[:], in0=psF_t[:, pbs, :].rearrange("p (h i) -> p h i", h=4),
                                                   in1=DT[:], op=ALU.mult), [PB[pbs], DT], [PTm])
            yield
            for gi_, dstg in ((0, sgrd), (1, sgad)):
                for half in range(2):
                    pb = nbank()
                    mm8(hT, wtile("F", wb0 + 8 + 2 * gi_ + half), pb)
                    yield
                    hs = slice(half * 512, (half + 1) * 512)
                    kb.op("act", lambda e, hs=hs, pb=pb: e.activation(out=f2[:, hs], in_=bank(pb), func=AF.Tanh, scale=0.5),
                          [PB[pb]], [f2])
                    kb.op("pool", lambda e, hs=hs, dstg=dstg: e.tensor_scalar(out=dstg[:, hs], in0=f2[:, hs], scalar1=0.5, scalar2=0.5,
                                                                               op0=ALU.mult, op1=ALU.add), [f2], [dstg])
                    yield
            for hh in range(2):
                pb = nbank()
                for h in (2 * hh, 2 * hh + 1):
                    oap = psF_t[:, pb, (h % 2) * 256:(h % 2 + 1) * 256]
                    kb.op("pe", lambda e, h=h, oap=oap: e.matmul(oap, lhsT=PTm[:, h, :], rhs=v_tok[:, h, :], start=True, stop=False),
                          [PTm, v_tok], [PB[pb]], inc=False)
                    kb.op("pe", lambda e, h=h, oap=oap: e.matmul(oap, lhsT=qfT[:, h, :], rhs=Sf_bf[:, h, :], start=False, stop=False),
                          [qfT, Sf_bf], [PB[pb]], inc=False)
                    kb.op("pe", lambda e, h=h, oap=oap: e.matmul(oap, lhsT=qbT[:, h, :], rhs=sr[:, h, :], start=False, stop=True),
                          [qbT, sr], [PB[pb]], inc=True)
                yield
                for h in (2 * hh, 2 * hh + 1):
                    oap = psF_t[:, pb, (h % 2) * 256:(h % 2 + 1) * 256]
                    kb.op("dve", lambda e, h=h, oap=oap: e.bn_stats(out=bnst[:, h, :], in_=oap), [PB[pb]], [bnst])
                    kb.op("dve", lambda e, h=h: e.bn_aggr(out=bnag[:, h, :], in_=bnst[:, h, :]), [bnst], [bnag])
                rstd_pool(bnag[:, 2 * hh:2 * hh + 2, 1], 2, 1.0, st4b[:, 12 + 2 * hh:14 + 2 * hh], bnag, st4b)
                for h in (2 * hh, 2 * hh + 1):
                    oap = psF_t[:, pb, (h % 2) * 256:(h % 2 + 1) * 256]
                    kb.op("dve", lambda e, h=h, oap=oap: e.tensor_scalar(
                        out=f1[:, h * 256:(h + 1) * 256], in0=oap, scalar1=bnag[:, h, 0:1], scalar2=st4b[:, 12 + h:13 + h],
                        op0=ALU.subtract, op1=ALU.mult), [PB[pb], bnag, st4b], [f1])
                yield
            for hh in range(2):
                pb = nbank()
                state_update_half(Sf, kdf, 0, pb, hh)
                yield
            kb.op("pool", lambda e: e.tensor_tensor(out=f1[:], in0=f1[:], in1=bcv[:, BC_GNW:BC_GNW + 1024], op=ALU.mult),
                  [f1, bcv], [f1])
            kb.op("pool", lambda e: e.tensor_tensor(out=f1[:], in0=f1[:], in1=bcv[:, BC_GNB:BC_GNB + 1024], op=ALU.add),
                  [f1, bcv], [f1])
            kb.op("dve", lambda e: e.scalar_tensor_tensor(out=oret[:], in0=f1[:], scalar=0.5, in1=sg[:], op0=ALU.mult, op1=ALU.mult),
                  [f1, sg], [oret])
            yield
            transposes(lambda k: oret[:, k * 128:(k + 1) * 128], [oret], 8, oretTd[:], [oretTd])
            if cj == 0:
                dump("qk_tok", qk_tok, [128, 8, 128], BF16)
                dump("oret", oret, [128, 1024], BF16)
                dump("QT", QTd, [128, 8, 128], BF16)
            yield

        def stage_A(cj):
            par = cj % 2
            QTd = QT[par]
            iters = [(kvh, sc) for kvh in range(2) for sc in range(NSC)]
            SBANK = (0, 1)
            OBK = (2, 3)

            def emit_S(i):
                kvh, sc = iters[i]
                sbk = SBANK[i % 2]
                qrhs = QTd[:, kvh * 4:(kvh + 1) * 4, :].rearrange("p h t -> p (h t)")
                kb.op("pe", lambda e: e.matmul(bank(sbk), lhsT=KT[:, kvh, sc * 128:(sc + 1) * 128], rhs=qrhs,
                                               start=True, stop=True), [KT, QTd], [PB[sbk]], inc=True)

            def emit_PV(i):
                kvh, sc = iters[i]
                et = ET[i % len(ET)]
                for g in range(4):
                    ob = OBK[g // 2]
                    oap = psF_t[:, ob, (g % 2) * 129:(g % 2) * 129 + 129]
                    kb.op("pe", lambda e, g=g, oap=oap: e.matmul(
                        oap, lhsT=et[:, g * 128:(g + 1) * 128], rhs=VA[:, sc, kvh, :],
                        start=(sc == 0 and g % 2 == 0), stop=(sc == NSC - 1), skip_group_check=True),
                        [et, VA], [PB[ob]], inc=(g == 3))
                if sc == NSC - 1:
                    for g in range(4):
                        ob = OBK[g // 2]
                        off = (g % 2) * 129
                        kb.op("dve", lambda e, g=g, ob=ob, off=off: e.reciprocal(
                            out=rs8[:, kvh * 4 + g:kvh * 4 + g + 1], in_=psF_t[:, ob, off + 128:off + 129]), [PB[ob]], [rs8])
                        kb.op("dve", lambda e, g=g, ob=ob, off=off: e.tensor_scalar(
                            out=oatt_tok[:, kvh * 4 + g, :], in0=psF_t[:, ob, off:off + 128],
                            scalar1=rs8[:, kvh * 4 + g:kvh * 4 + g + 1], scalar2=None, op0=ALU.mult),
                            [PB[ob], rs8], [oatt_tok])

            emit_S(0)
            for i, (kvh, sc) in enumerate(iters):
                if i + 1 < len(iters):
                    emit_S(i + 1)
                sbk = SBANK[i % 2]
                et = ET[i % len(ET)]
                kb.op("act", lambda e, sbk=sbk, et=et: e.activation(out=et[:], in_=bank(sbk), func=AF.Exp, scale=att_scale),
                      [PB[sbk]], [et])
                if i >= 1:
                    emit_PV(i - 1)
                yield
            emit_PV(len(iters) - 1)
            yield

        def stage_M(cj):
            c = chunksB[cj]
            par = cj % 2
            oretTd, sgrd, sgad = oretT[par], sgr[par], sga[par]
            wb0 = cj * 22
            kb.dma("sp", xl[:], x_d[c * 128:(c + 1) * 128, :], [], [xl], key="xl")
            for cb in range(2):
                pb = nbank()
                hs = slice(cb * 512, (cb + 1) * 512)
                mm8(oretTd, wtile("M", wb0 + cb), pb)
                yield
                kb.op("dve", lambda e, hs=hs, pb=pb: e.tensor_tensor(out=m1[:, hs], in0=bank(pb), in1=sgrd[:, hs], op=ALU.mult),
                      [PB[pb], sgrd], [m1h[cb]])
                yield
            transposes(lambda k: oatt_tok[:, k, :], [oatt_tok], 8, oattT[:], [oattT])
            if cj == 0:
                dump("oattT", oattT, [128, 8, 128], BF16)
            yield
            for cb in range(2):
                pb = nbank()
                hs = slice(cb * 512, (cb + 1) * 512)
                mm8(oattT, wtile("M", wb0 + 2 + cb), pb)
                yield
                kb.op("dve", lambda e, hs=hs, pb=pb: e.tensor_tensor(out=m2[:, hs], in0=bank(pb), in1=sgad[:, hs], op=ALU.mult),
                      [PB[pb], sgad], [m2])
                yield
            kb.op("pool", lambda e: e.tensor_tensor(out=mrg[:], in0=m1[:], in1=m2[:], op=ALU.add), [m1h[0], m1h[1], m2], [mrg])
            yield
            transposes(lambda k: mrg[:, k * 128:(k + 1) * 128], [mrg], 8, mT[:], [mT])
            if cj == 0:
                dump("mrg", mrg, [128, 1024], BF16)
            yield
            for cb in range(2):
                pb = nbank()
                hs = slice(cb * 512, (cb + 1) * 512)
                mm8(mT, wtile("M", wb0 + 4 + cb), pb)
                yield
                kb.op("dve", lambda e, hs=hs, pb=pb: e.tensor_tensor(out=m1[:, hs], in0=bank(pb), in1=gm_bc[:, hs], op=ALU.mult),
                      [PB[pb], gm_bc], [m1h[cb]])
                yield
            kb.op("pool", lambda e: e.tensor_tensor(out=xl[:], in0=xl[:], in1=m1[:], op=ALU.add), [xl, m1h[0], m1h[1]], [xl])
            norm_part1(xl, xs2, st4m)
            if cj == 0:
                dump("xl", xl, [128, 1024])
            yield
            norm_part2(4, hT2, xs2, m2)
            yield
            for cb in range(8):
                pb = nbank()
                wb = wtile("M", wb0 + 6 + cb)
                for jj in range(4):
                    for k in range(8):
                        kb.op("pe", lambda e, k=k, jj=jj, wb=wb, pb=pb: e.matmul(
                            psF_t[:, pb, jj * 128:(jj + 1) * 128], lhsT=wb[:, k, jj * 128:(jj + 1) * 128], rhs=hT2[:, k, :],
                            start=(k == 0), stop=(k == 7)), [wb, hT2], [PB[pb]], inc=(k == 7 and jj == 3))
                hv = cb % 2
                hs = slice(hv * 512, (hv + 1) * 512)
                yield
                kb.op("act", lambda e, pb=pb, hs=hs: e.activation(out=m1[:, hs], in_=bank(pb), func=AF.Relu), [PB[pb]], [m1h[hv]])
                kb.op("pool", lambda e, cb=cb, hs=hs: e.tensor_tensor(out=hid[:, cb * 4:(cb + 1) * 4, :].rearrange("p j t -> p (j t)"),
                                                                      in0=m1[:, hs], in1=m1[:, hs], op=ALU.mult), [m1h[hv]], [hid])
                yield
            for cb in range(2):
                hs = slice(cb * 512, (cb + 1) * 512)
                for kg in range(4):
                    pb = nbank()
                    wb = wtile("M", wb0 + 14 + cb * 4 + kg)
                    for k in range(8):
                        j = kg * 8 + k
                        kb.op("pe", lambda e, k=k, j=j, wb=wb, pb=pb: e.matmul(
                            bank(pb), lhsT=hid[:, j, :], rhs=wb[:, k, :], start=(k == 0), stop=(k == 7)),
                            [hid, wb], [PB[pb]], inc=(k == 7))
                    yield
                    if kg == 0:
                        kb.op("dve", lambda e, hs=hs, pb=pb: e.tensor_copy(out=m1[:, hs], in_=bank(pb)), [PB[pb]], [m1h[cb]])
                    else:
                        kb.op("dve", lambda e, hs=hs, pb=pb: e.tensor_tensor(out=m1[:, hs], in0=m1[:, hs], in1=bank(pb), op=ALU.add),
                              [PB[pb], m1h[cb]], [m1h[cb]])
                    yield
                kb.op("pool", lambda e, hs=hs: e.tensor_tensor(out=m1[:, hs], in0=m1[:, hs], in1=gf_bc[:, hs], op=ALU.mult),
                      [m1h[cb], gf_bc], [m1h[cb]])
            kb.op("pool", lambda e: e.tensor_tensor(out=m1[:], in0=m1[:], in1=xl[:], op=ALU.add), [m1h[0], m1h[1], xl], [m1h[0], m1h[1]])
            kb.op("act", lambda e: e.activation(out=xs2[:], in_=m1[:], func=AF.Square, accum_out=st4m[:, 4:5]),
                  [m1h[0], m1h[1]], [xs2, st4m])
            rstd_pool(st4m[:, 4:5], 1, 1.0 / D, st4m[:, 5:6], st4m, st4m)
            kb.op("dve", lambda e: e.scalar_tensor_tensor(out=m2[:], in0=m1[:], scalar=st4m[:, 5:6],
                                                          in1=bcv[:, BC_FIN:BC_FIN + 1024], op0=ALU.mult, op1=ALU.mult),
                  [m1h[0], m1h[1], st4m, bcv], [m2])
            tok = kb.dma("sp", y_d[c * 128:(c + 1) * 128, :], m2[:], [m2], [], key="st0")
            if tok is not None:
                kb.final_waits[tok[0]] = tok[1]
            yield

        def exhaust(g):
            for _ in g:
                pass

        def step(g):
            try:
                next(g)
                return True
            except StopIteration:
                return False

        def phaseB():
            bstate["nb"] = 0
            if nB > 0:
                load_chunk_inputs(0)
                if nB > 1:
                    load_chunk_inputs(1)
                exhaust(stage_F(0))
            RATIO = 1.3
            ucount = [0, 0]
            for cj in range(nB):
                others = []
                if cj >= 1:
                    others.append(stage_M(cj - 1))
                if cj + 1 < nB:
                    if cj + 2 < nB:
                        pass
                    others.append(stage_F(cj + 1))
                ga = stage_A(cj)
                acc = 0.0
                rr = 0
                while step(ga):
                    acc += RATIO
                    while acc >= 1.0 and others:
                        acc -= 1.0
                        g = others[rr % len(others)]
                        if not step(g):
                            others.remove(g)
                        else:
                            rr += 1
                            ucount[0] += 1
                for g in others:
                    for _ in g:
                        ucount[1] += 1
                dbg_out["_units"] = list(ucount)
                if cj + 2 < nB:
                    load_chunk_inputs(cj + 2)
            if nB > 0:
                exhaust(stage_M(nB - 1))


        kb.dry = True
        phaseB()
        kb.dry = False
        phaseB()

        kb.wait_all("sp", list(kb.final_waits.items()))
        kb.run()
    return nc, dbg_out


def prep_inputs(inputs):
    f = np.float32
    x = np.asarray(inputs["x"], f)
    c = np.asarray(inputs["c"], f)
    ctx = np.asarray(inputs["ctx"], f)
    c_ctx = np.asarray(inputs["c_ctx"], f)

    def pl(v):
        return np.asarray(v, f).reshape(-1, 128).T

    rt, at = host_tabs()
    consts = host_consts()
    lam = np.concatenate([np.asarray(inputs["ret_log_lam_fwd"], f)[0], np.asarray(inputs["ret_log_lam_bwd"], f)[0]])
    lam_rep = np.ascontiguousarray(np.broadcast_to(lam[None, :], (128, 8)))
    bc = np.concatenate([np.asarray(inputs["ret_gn_w"], f)[0], np.asarray(inputs["ret_gn_b"], f)[0],
                         np.asarray(inputs["final_norm_g"], f), np.asarray(inputs["att_q_norm_g"], f)[0],
                         np.asarray(inputs["att_k_norm_g"], f)[0]])
    bc_rep = np.ascontiguousarray(np.broadcast_to(bc[None, :], (128, NBC)))
    shared = {
        "lam": lam_rep, "bc": bc_rep, "rt": rt, "at": at, "consts": consts,
        "mod_w": np.ascontiguousarray(np.asarray(inputs["mod_w"], f)[0]),
        "w_in": np.ascontiguousarray(np.asarray(inputs["w_in"], f)[0]),
        "w_br_ret": np.ascontiguousarray(np.asarray(inputs["w_br_ret"], f)[0]),
        "w_br_att": np.ascontiguousarray(np.asarray(inputs["w_br_att"], f)[0]),
        "w_out": np.ascontiguousarray(np.asarray(inputs["w_out"], f)[0]),
        "w_mlp_up": np.ascontiguousarray(np.asarray(inputs["w_mlp_up"], f)[0]),
        "w_mlp_down": np.ascontiguousarray(np.asarray(inputs["w_mlp_down"], f)[0]),
    }
    in_maps = []
    for b in range(8):
        vecs = np.concatenate([pl(c[b]), pl(c_ctx), pl(np.asarray(inputs["mod_b"], f)[0]),
                               pl(np.asarray(inputs["norm_mix_g"], f)[0]), pl(np.asarray(inputs["norm_mlp_g"], f)[0])], axis=1)
        m = dict(shared)
        m["x"] = np.ascontiguousarray(x[b])
        m["ctx"] = np.ascontiguousarray(ctx[b])
        m["vecs"] = np.ascontiguousarray(vecs.astype(f))
        in_maps.append(m)
    return in_maps


_CACHE = {}


def kernel(**inputs):
    in_maps = prep_inputs(inputs)
    if "nc" not in _CACHE:
        _CACHE["nc"] = build()[0]
    nc = _CACHE["nc"]
    res = run_bass_kernel_spmd(nc, in_maps, core_ids=list(range(8)))
    out = np.stack([np.asarray(r["y"], np.float32) for r in res.results], axis=0)
    return out
```

```python
import numpy as np
from contextlib import ExitStack
import concourse.bass as bass
import concourse.mybir as mybir
from concourse.bass_utils import run_bass_kernel_spmd

F32 = mybir.dt.float32
BF16 = mybir.dt.bfloat16
AF = mybir.ActivationFunctionType
ALU = mybir.AluOpType
AX = mybir.AxisListType

D = 1024
NTOK = 4096
NCTX = 256
NCH = NTOK // 128
NSC = NCH + NCTX // 128
DIN = 6656
DFF = 4096
EPS = 1e-6
NWT = 35
WT_IN = 0
WT_BR_RET = 13
WT_BR_ATT = 15
WT_OUT = 17
WT_UP = 19
WT_DOWN = 27


class Buf:
    def __init__(self, name, t):
        self.name = name
        self.t = t
        self.w = None
        self.r = {}

    def __getitem__(self, idx):
        return self.t[idx]


class KB:
    ENG = ("pe", "act", "dve", "pool", "sp")

    def __init__(self, nc, stack):
        self.nc = nc
        self.stack = stack
        self.q = {e: [] for e in self.ENG}
        self.cnt = {e: 0 for e in self.ENG}
        self.waited = {e: {} for e in self.ENG}
        self.sems = {}
        self.dcnt = {}
        for e in self.ENG:
            self.sems[e] = stack.enter_context(nc.semaphore("s_" + e))
        self.final_waits = {}

    def sem(self, key):
        if key not in self.sems:
            self.sems[key] = self.stack.enter_context(self.nc.semaphore("s_" + key))
            self.dcnt[key] = 0
        return self.sems[key]

    def sb(self, name, shape, dt):
        return Buf(name, self.stack.enter_context(self.nc.sbuf_tensor("sb_" + name, list(shape), dt)))

    def _deps(self, eng, reads, writes):
        deps = {}

        def add(k, v):
            if deps.get(k, 0) < v:
                deps[k] = v

        for b in reads:
            if b.w is not None:
                add(*b.w)
        for b in writes:
            if b.w is not None:
                add(*b.w)
            for k, v in b.r.items():
                add(k, v)
        waits = []
        for k, v in deps.items():
            if k == eng:
                if eng in ("pe", "sp"):
                    continue
                if v > self.cnt[eng]:
                    continue
            if self.waited[eng].get(k, 0) >= v:
                continue
            self.waited[eng][k] = v
            waits.append((k, v))
        return waits

    def _mark(self, tok, reads, writes):
        for b in writes:
            b.w = tok
            b.r = {}
        for b in reads:
            if b in writes:
                continue
            k, v = tok
            if b.r.get(k, 0) < v:
                b.r[k] = v

    dry = False

    def op(self, eng, fn, reads=(), writes=(), inc=True):
        if self.dry:
            return None
        pr = [b for b in reads if getattr(b, "psum", False)]
        if pr:
            reads = [b for b in reads if not getattr(b, "psum", False)]
            writes = list(writes) + [b for b in pr if b not in writes]
        waits = self._deps(eng, reads, writes)
        if inc:
            self.cnt[eng] += 1
            tok = (eng, self.cnt[eng])
        else:
            tok = (eng, self.cnt[eng] + 1)
        sems = self.sems

        def thunk(e, waits=waits, fn=fn, inc=inc, eng=eng):
            for k, v in waits:
                e.wait_ge(sems[k], v)
            ins = fn(e)
            if inc:
                ins.then_inc(sems[eng], 1)

        self.q[eng].append(thunk)
        self._mark(tok, reads, writes)
        return tok

    def dma(self, eng, out, in_, reads, writes, key, group_total=None, **kw):
        if self.dry:
            return None
        s = self.sem(key)
        waits = self._deps(eng, reads, writes)
        self.dcnt[key] += 1
        val = 16 * (group_total if group_total is not None else self.dcnt[key])
        tok = (key, val)
        sems = self.sems

        def thunk(e, waits=waits):
            for k, v in waits:
                e.wait_ge(sems[k], v)
            e.dma_start(out=out, in_=in_, **kw).then_inc(s, 16)

        self.q[eng].append(thunk)
        self._mark(tok, reads, writes)
        return tok

    def wait_all(self, eng, toks):
        sems = self.sems

        def thunk(e):
            for k, v in toks:
                e.wait_ge(sems[k], v)

        self.q[eng].append(thunk)

    def run(self):
        nc = self.nc
        with nc.Block() as block:
            @block.tensor
            def _(e):
                for f in self.q["pe"]:
                    f(e)

            @block.scalar
            def _(e):
                for f in self.q["act"]:
                    f(e)

            @block.vector
            def _(e):
                for f in self.q["dve"]:
                    f(e)

            @block.gpsimd
            def _(e):
                for f in self.q["pool"]:
                    f(e)

            @block.sync
            def _(e):
                for f in self.q["sp"]:
                    f(e)


def host_consts():
    i = np.arange(128, dtype=np.float32)
    ident = np.eye(128, dtype=np.float32)
    iota1 = np.tile((i + 1)[None, :], (128, 1))
    iota2 = np.tile((128 - i)[None, :], (128, 1))
    jj = i[:, None]
    ii = i[None, :]
    A = np.maximum(ii - jj, 0)
    B = np.maximum(jj - ii, 0)
    M1 = (ii >= jj).astype(np.float32)
    M2 = (jj > ii).astype(np.float32)
    p127 = (127 - i)[:, None]
    pj = i[:, None]
    return np.ascontiguousarray(
        np.concatenate([ident, iota1, iota2, A, B, M1, M2, p127, pj], axis=1).astype(np.float32))


def host_tabs():
    f32 = np.float32
    half = 64
    inv = (f32(10000.0) ** (-(np.arange(half, dtype=f32) / f32(half)))).astype(f32)
    pos = np.arange(NCTX + NTOK, dtype=f32)
    ang = (pos[:, None] * inv[None, :]).astype(f32)
    c, s = np.cos(ang).astype(f32), np.sin(ang).astype(f32)
    rt = np.concatenate([c, c, -s, s], axis=1).astype(f32)
    h2 = 32
    inv2 = (f32(10000.0) ** (-(np.arange(h2, dtype=f32) / f32(h2)))).astype(f32)
    t = np.arange(NTOK)
    row = (t // 64).astype(f32)
    col = (t % 64).astype(f32)
    ar = (row[:, None] * inv2[None, :]).astype(f32)
    ac = (col[:, None] * inv2[None, :]).astype(f32)
    cr, sr, cc, sc = np.cos(ar), np.sin(ar), np.cos(ac), np.sin(ac)
    at = np.concatenate([cr, cr, cc, cc, -sr, sr, -sc, sc], axis=1).astype(f32)
    return np.ascontiguousarray(rt), np.ascontiguousarray(at)


C_ID, C_I1, C_I2, C_A, C_B, C_M1, C_M2 = [k * 128 for k in range(7)]
C_P127 = 7 * 128
C_PJ = 7 * 128 + 1
NCONST = 7 * 128 + 2
V_C, V_CC, V_MODB, V_GMIX, V_GMLP = 0, 8, 16, 64, 72
NVEC = 80
BC_GNW, BC_GNB, BC_FIN, BC_QG, BC_KG = 0, 1024, 2048, 3072, 3200
NBC = 3328


def build(debug=None, stop_after=None):
    debug = debug or []
    nc = bass.Bass("TRN2", target_bir_lowering=False)

    def din(name, shape, dt=F32):
        return nc.dram_tensor(name, list(shape), dt, kind="ExternalInput").ap()

    x_d = din("x", [NTOK, D])
    ctx_d = din("ctx", [NCTX, D])
    vecs_d = din("vecs", [128, NVEC])
    lam_d = din("lam", [128, 8])
    bc_d = din("bc", [128, NBC])
    rt_d = din("rt", [NCTX + NTOK, 256])
    at_d = din("at", [NTOK, 256])
    consts_d = din("consts", [128, NCONST])
    modw_d = din("mod_w", [D, 6 * D])
    win_d = din("w_in", [D, DIN])
    wbr_ret_d = din("w_br_ret", [D, D])
    wbr_att_d = din("w_br_att", [D, D])
    wout_d = din("w_out", [D, D])
    wup_d = din("w_mlp_up", [D, DFF])
    wdown_d = din("w_mlp_down", [DFF, D])
    y_d = nc.dram_tensor("y", [NTOK, D], F32, kind="ExternalOutput").ap()
    wsc_d = nc.dram_tensor("wsc", [NWT, 128, 8, 512], BF16, kind="Internal").ap()
    sbs_d = nc.dram_tensor("sbs_scr", [NCH, 128, 4, 256], BF16, kind="Internal").ap()
    dbg_out = {}

    stack = ExitStack()
    with stack:
        kb = KB(nc, stack)
        sb = kb.sb
        psF_t = stack.enter_context(nc.psum_tensor("psF", [128, 7, 512], F32))
        psT_t = stack.enter_context(nc.psum_tensor("psT", [128, 8, 128], BF16))
        PB = [Buf("psb%d" % i, None) for i in range(7)]
        PT = Buf("psT", psT_t)
        for _b in PB + [PT]:
            _b.psum = True

        def bank(i):
            return psF_t[:, i, :]

        def bank2(i):
            return psF_t[:, i:i + 2, :]

        consts = sb("consts", [128, NCONST], F32)
        vecs = sb("vecs", [128, NVEC], F32)
        lam = sb("lam", [128, 8], F32)
        bcv = sb("bcv", [128, NBC], F32)
        identb = sb("identb", [128, 128], BF16)
        onesf = sb("onesf", [128, 128], F32)
        negh = sb("negh", [128, 16], F32)
        modv = sb("modv", [128, 48, 2], F32)
        silu_c = sb("silu_c", [128, 8, 2], F32)
        gvec = sb("gvec", [128, 8, 8], F32)
        gm_bc = sb("gm_bc", [128, 1024], F32)
        gf_bc = sb("gf_bc", [128, 1024], F32)
        LG = sb("LG", [128, 8], F32)
        DT = sb("DT", [128, 4, 128], F32)
        QF = sb("QF", [128, 4, 128], F32)
        QB = sb("QB", [128, 4, 128], F32)
        KD = sb("KD", [128, 8], F32)
        CD = sb("CD", [128, 8], F32)
        KT = sb("KT", [128, 2, NSC * 128], BF16)
        VA = sb("VA", [128, NSC, 2, 129], BF16)
        SBSD = [Buf("sbsd%d" % i, None) for i in range(NCH)]
        sbs_ring = [sb("sbsr%d" % i, [128, 4, 256], BF16) for i in range(2)]
        Sf = sb("Sf", [128, 4, 256], F32)
        Sb_ = sb("Sb", [128, 4, 256], F32)
        Sf_bf = sb("Sf_bf", [128, 4, 256], BF16)
        wringF = [sb("wF%d" % i, [128, 8, 512], BF16) for i in range(2)]
        wringM = [sb("wM%d" % i, [128, 8, 512], BF16) for i in range(2)]
        xring = [sb("x%d" % i, [128, 1024], F32) for i in range(2)]
        rtab = [sb("rtab%d" % i, [128, 256], F32) for i in range(2)]
        atab = [sb("atab%d" % i, [128, 256], F32) for i in range(2)]
        st4 = sb("st4", [128, 16], F32)
        st4b = sb("st4b", [128, 16], F32)
        xs = sb("xs", [128, 1024], BF16)
        hT = sb("hT", [128, 8, 128], BF16)
        f1 = sb("f1", [128, 1024], F32)
        f2 = sb("f2", [128, 1024], F32)
        f3 = sb("f3", [128, 1024], F32)
        qk_tok = sb("qk_tok", [128, 8, 128], BF16)
        qT = sb("qT", [128, 4, 128], BF16)
        qfT = sb("qfT", [128, 4, 128], BF16)
        qbT = sb("qbT", [128, 4, 128], BF16)
        kT = sb("kT", [128, 4, 128], BF16)
        kdf = sb("kdf", [128, 4, 128], BF16)
        v_tok = sb("v_tok", [128, 4, 256], BF16)
        sg = sb("sg", [128, 1024], BF16)
        aq_tok = sb("aq_tok", [128, 8, 128], BF16)
        ak_tok = sb("ak_tok", [128, 2, 128], BF16)
        PTm = sb("PTm", [128, 4, 128], BF16)
        kdb = PTm
        oret = sb("oret", [128, 1024], BF16)
        bnst = sb("bnst", [128, 4, 6], F32)
        bnag = sb("bnag", [128, 4, 2], F32)
        QT = [sb("QT%d" % i, [128, 8, 128], BF16) for i in range(2)]
        oretT = [sb("oretT%d" % i, [128, 8, 128], BF16) for i in range(2)]
        sgr = [sb("sgr%d" % i, [128, 1024], BF16) for i in range(2)]
        sga = [sb("sga%d" % i, [128, 1024], BF16) for i in range(2)]
        ET = [sb("ET%d" % i, [128, 512], BF16) for i in range(2)]
        for _i in range(2):
            _b = Buf("ETc%d" % _i, None)
            _b.t = consts[:, _i * 256:(_i + 1) * 256].bitcast(BF16)
            ET.append(_b)
        oatt_tok = sb("oatt_tok", [128, 8, 128], BF16)
        oattT = sb("oattT", [128, 8, 128], BF16)
        rs8 = sb("rs8", [128, 8], F32)
        m1 = sb("m1", [128, 1024], F32)
        m1h = [Buf("m1a", None), Buf("m1b", None)]
        m2 = sb("m2", [128, 1024], F32)
        xl = sb("xl", [128, 1024], F32)
        mrg = sb("mrg", [128, 1024], BF16)
        mT = sb("mT", [128, 8, 128], BF16)
        xs2 = sb("xs2", [128, 1024], BF16)
        hT2 = sb("hT2", [128, 8, 128], BF16)
        hid = sb("hid", [128, 32, 128], BF16)
        st4m = sb("st4m", [128, 16], F32)

        def cslice(off, n=128):
            return consts[:, off:off + n]

        WSC = [Buf("wsc%d" % i, None) for i in range(NWT)]

        def precast(tile, src, group, total):
            kb.dma("pool", wsc_d[tile], src.rearrange("(k p) n -> p k n", p=128), [], [WSC[tile]],
                   key="pc_" + group, group_total=total)

        for b in (1, 2, 3, 8):
            precast(WT_IN + b, win_d[:, b * 512:(b + 1) * 512], "a", 4)
        late_precasts = []
        for b in (6, 7, 0, 4, 5, 9, 10, 11, 12):
            late_precasts.append((WT_IN + b, win_d[:, b * 512:(b + 1) * 512], "b", 9))
        for cb in range(2):
            late_precasts.append((WT_BR_RET + cb, wbr_ret_d[:, cb * 512:(cb + 1) * 512], "c", 6))
        for cb in range(2):
            late_precasts.append((WT_BR_ATT + cb, wbr_att_d[:, cb * 512:(cb + 1) * 512], "c", 6))
        for cb in range(2):
            late_precasts.append((WT_OUT + cb, wout_d[:, cb * 512:(cb + 1) * 512], "c", 6))
        for cb in range(8):
            late_precasts.append((WT_UP + cb, wup_d[:, cb * 512:(cb + 1) * 512], "d", 8))
        for cb in range(2):
            for kg in range(4):
                late_precasts.append((WT_DOWN + cb * 4 + kg,
                                      wdown_d[kg * 1024:(kg + 1) * 1024, cb * 512:(cb + 1) * 512], "e", 8))

        kb.dma("sp", consts[:], consts_d[:, :], [], [consts], key="c0")
        kb.dma("sp", vecs[:], vecs_d[:, :], [], [vecs], key="c1")
        kb.dma("sp", lam[:], lam_d[:, :], [], [lam], key="c2")
        kb.dma("sp", bcv[:], bc_d[:, :], [], [bcv], key="c3")
        kb.op("dve", lambda e: e.tensor_copy(out=identb[:], in_=cslice(C_ID)), [consts], [identb])
        kb.op("pool", lambda e: e.memset(onesf[:], 1.0), [], [onesf])
        kb.op("pool", lambda e: e.memset(negh[:], -0.5), [], [negh])
        kb.op("pool", lambda e: e.memset(Sf[:], 0.0), [], [Sf])
        kb.op("pool", lambda e: e.memset(Sb_[:], 0.0), [], [Sb_])
        kb.op("pool", lambda e: e.memset(VA[:, :, :, 128:129], 1.0), [], [VA])

        kb.op("act", lambda e: e.activation(out=LG[:], in_=lam[:], func=AF.Exp), [lam], [LG])
        kb.op("dve", lambda e: e.tensor_scalar(out=LG[:], in0=LG[:], scalar1=-1.0, scalar2=None, op0=ALU.mult),
              [LG], [LG])
        sc_k = 128.0 ** -0.5
        for h in range(4):
            kb.op("act", lambda e, h=h: e.activation(out=QF[:, h, :], in_=cslice(C_I1), func=AF.Exp,
                                                     scale=LG[:, h:h + 1]), [LG, consts], [QF])
            kb.op("act", lambda e, h=h: e.activation(out=QB[:, h, :], in_=cslice(C_I2), func=AF.Exp,
                                                     scale=LG[:, 4 + h:5 + h]), [LG, consts], [QB])
            kb.op("act", lambda e, h=h: e.activation(out=KD[:, h:h + 1], in_=consts[:, C_P127:C_P127 + 1],
                                                     func=AF.Exp, scale=LG[:, h:h + 1]), [LG, consts], [KD])
            kb.op("act", lambda e, h=h: e.activation(out=KD[:, 4 + h:5 + h], in_=consts[:, C_PJ:C_PJ + 1],
                                                     func=AF.Exp, scale=LG[:, 4 + h:5 + h]), [LG, consts], [KD])
            kb.op("act", lambda e, h=h: e.activation(out=f1[:, 0:128], in_=cslice(C_A), func=AF.Exp,
                                                     scale=LG[:, h:h + 1]), [LG, consts], [f1])
            kb.op("act", lambda e, h=h: e.activation(out=f2[:, 0:128], in_=cslice(C_B), func=AF.Exp,
                                                     scale=LG[:, 4 + h:5 + h]), [LG, consts], [f2])
            kb.op("dve", lambda e: e.tensor_tensor(out=f1[:, 0:128], in0=f1[:, 0:128], in1=cslice(C_M1),
                                                   op=ALU.mult), [f1, consts], [f1])
            kb.op("dve", lambda e: e.tensor_tensor(out=f2[:, 0:128], in0=f2[:, 0:128], in1=cslice(C_M2),
                                                   op=ALU.mult), [f2, consts], [f2])
            kb.op("dve", lambda e: e.tensor_tensor(out=f1[:, 0:128], in0=f1[:, 0:128], in1=f2[:, 0:128],
                                                   op=ALU.add), [f1, f2], [f1])
            kb.op("dve", lambda e, h=h: e.tensor_scalar(out=DT[:, h, :], in0=f1[:, 0:128], scalar1=sc_k,
                                                        scalar2=None, op0=ALU.mult), [f1], [DT])
        kb.op("dve", lambda e: e.tensor_scalar(out=KD[:], in0=KD[:], scalar1=sc_k, scalar2=None, op0=ALU.mult),
              [KD], [KD])
        kb.op("act", lambda e: e.activation(out=CD[:], in_=LG[:], func=AF.Exp, scale=128.0), [LG], [CD])

        for col in range(2):
            kb.op("act", lambda e, col=col: e.activation(out=silu_c[:, :, col], in_=vecs[:, V_C + 8 * col:V_C + 8 * col + 8],
                                                         func=AF.Silu), [vecs], [silu_c])
        MODPS = PB[0]
        ring4 = wringF + wringM
        for slab in range(24):
            mwb = ring4[slab % 4]
            mw = mwb[:].bitcast(F32)
            kb.dma("sp" if slab % 2 == 0 else "act", mw,
                   modw_d[:, slab * 256:(slab + 1) * 256].rearrange("(k p) n -> p k n", p=128),
                   [], [mwb], key="modw%d" % (slab % 4))
            for jj in range(2):
                j = slab * 2 + jj
                for k in range(8):
                    kb.op("pe", lambda e, j=j, jj=jj, k=k, mw=mw: e.matmul(
                        psF_t[:, 0, 2 * j:2 * j + 2], lhsT=mw[:, k, jj * 128:(jj + 1) * 128], rhs=silu_c[:, k, :],
                        start=(k == 0), stop=(k == 7)), [mwb, silu_c], [MODPS], inc=(k == 7))
        kb.op("dve", lambda e: e.tensor_tensor(
            out=modv[:], in0=psF_t[:, 0, 0:96].rearrange("p (j c) -> p j c", c=2),
            in1=vecs[:, V_MODB:V_MODB + 48].unsqueeze(2).broadcast_to([128, 48, 2]), op=ALU.add),
            [MODPS, vecs], [modv])
        for (dst, gcol, sc0, col) in ((0, V_GMIX, 8, 0), (2, V_GMIX, 8, 1), (4, V_GMLP, 32, 0)):
            kb.op("dve", lambda e, dst=dst, gcol=gcol, sc0=sc0, col=col: e.scalar_tensor_tensor(
                out=gvec[:, :, dst], in0=modv[:, sc0:sc0 + 8, col], scalar=1.0, in1=vecs[:, gcol:gcol + 8],
                op0=ALU.add, op1=ALU.mult), [modv, vecs], [gvec])
        for (dst, j0, col) in ((1, 0, 0), (3, 0, 1), (5, 24, 0), (6, 16, 0), (7, 40, 0)):
            kb.op("dve", lambda e, dst=dst, j0=j0, col=col: e.tensor_copy(out=gvec[:, :, dst], in_=modv[:, j0:j0 + 8, col]),
                  [modv], [gvec])
        for (dstb, gi, pb) in ((gm_bc, 6, 1), (gf_bc, 7, 3)):
            for k in range(8):
                kb.op("dve", lambda e, k=k, gi=gi: e.tensor_scalar(
                    out=f3[:, k * 128:(k + 1) * 128], in0=cslice(C_ID), scalar1=gvec[:, k, gi:gi + 1], scalar2=None,
                    op0=ALU.mult), [consts, gvec], [f3])
            for k in range(8):
                kb.op("pe", lambda e, k=k, pb=pb: e.matmul(
                    psF_t[:, pb + k // 4, (k % 4) * 128:(k % 4 + 1) * 128], lhsT=onesf[:], rhs=f3[:, k * 128:(k + 1) * 128],
                    start=True, stop=True), [onesf, f3], [PB[pb + k // 4]], inc=True)
            kb.op("act", lambda e, dstb=dstb, pb=pb: e.activation(out=dstb[:].rearrange("p (a n) -> p a n", a=2),
                                                                  in_=bank2(pb), func=AF.Identity),
                  [PB[pb], PB[pb + 1]], [dstb])

        def rstd_pool(ss_ap, n, scale, dst_ap, ssbuf, dstbuf):
            kb.op("pool", lambda e: e.tensor_scalar(out=dst_ap, in0=ss_ap, scalar1=scale, scalar2=EPS,
                                                    op0=ALU.mult, op1=ALU.add), [ssbuf], [dstbuf])
            kb.op("pool", lambda e: e.tensor_tensor(out=dst_ap, in0=dst_ap, in1=negh[:, 0:n], op=ALU.pow),
                  [dstbuf, negh], [dstbuf])

        def norm_part1(xt, xs_b, st_b):
            kb.op("act", lambda e: e.activation(out=xs_b[:], in_=xt[:], func=AF.Square, accum_out=st_b[:, 0:1]),
                  [xt], [xs_b, st_b])
            rstd_pool(st_b[:, 0:1], 1, 1.0 / D, st_b[:, 1:2], st_b, st_b)
            kb.op("dve", lambda e: e.tensor_scalar(out=xs_b[:], in0=xt[:], scalar1=st_b[:, 1:2], scalar2=None, op0=ALU.mult),
                  [xt, st_b], [xs_b])

        def norm_part2(gi, dst_hT, xs_b, tmp_b):
            for k in range(8):
                kb.op("pe", lambda e, k=k: e.transpose(out=psT_t[:, k, :], in_=xs_b[:, k * 128:(k + 1) * 128], identity=identb[:]),
                      [xs_b, identb], [PT], inc=(k == 7))
            kb.op("dve", lambda e: e.tensor_tensor(
                out=tmp_b[:].rearrange("p (k t) -> p k t", k=8), in0=psT_t[:],
                in1=gvec[:, :, gi:gi + 1].broadcast_to([128, 8, 128]), op=ALU.mult), [PT, gvec], [tmp_b])
            kb.op("pool", lambda e: e.tensor_tensor(
                out=dst_hT[:], in0=tmp_b[:].rearrange("p (k t) -> p k t", k=8),
                in1=gvec[:, :, gi + 1:gi + 2].broadcast_to([128, 8, 128]), op=ALU.add), [tmp_b, gvec], [dst_hT])

        def mm8(hTb, wb, pb, out_ap=None):
            for k in range(8):
                kb.op("pe", lambda e, k=k: e.matmul(bank(pb) if out_ap is None else out_ap, lhsT=hTb[:, k, :], rhs=wb[:, k, :],
                                                    start=(k == 0), stop=(k == 7)),
                      [hTb, wb], [PB[pb]], inc=(k == 7))

        def rotate(dst, dst_bufs, src_ap, src_bufs, tab, H, a):
            h = 128 // (2 * a)
            t1 = f1[:, 0:H * 128].rearrange("p (H d) -> p H d", H=H)
            t2 = f2[:, 0:H * 128].rearrange("p (H d) -> p H d", H=H)
            kb.op("dve", lambda e: e.tensor_tensor(out=t1, in0=src_ap, in1=tab[:, 0:128].unsqueeze(1).broadcast_to([128, H, 128]),
                                                   op=ALU.mult), src_bufs + [tab], [f1])
            for ai in range(a):
                for two in range(2):
                    o0 = ai * 2 * h + two * h
                    s0 = ai * 2 * h + (1 - two) * h
                    kb.op("dve", lambda e, o0=o0, s0=s0: e.tensor_tensor(
                        out=t2[:, :, o0:o0 + h], in0=src_ap[:, :, s0:s0 + h],
                        in1=tab[:, 128 + o0:128 + o0 + h].unsqueeze(1).broadcast_to([128, H, h]), op=ALU.mult),
                        src_bufs + [tab], [f2])
            kb.op("pool", lambda e: e.tensor_tensor(out=dst, in0=t1, in1=t2, op=ALU.add), [f1, f2], dst_bufs)

        def head_rms(src_ap, src_bufs, H, gcol, dstf):
            sq = f1[:, 0:H * 128]
            kb.op("act", lambda e: e.activation(out=sq.rearrange("p (H d) -> p H d", H=H), in_=src_ap, func=AF.Square),
                  src_bufs, [f1])
            kb.op("dve", lambda e: e.tensor_reduce(out=st4b[:, 0:H], in_=sq.rearrange("p (H d) -> p H d", H=H),
                                                   axis=AX.X, op=ALU.add), [f1], [st4b])
            rstd_pool(st4b[:, 0:H], H, 1.0 / 128, st4b[:, 0:H], st4b, st4b)
            kb.op("dve", lambda e: e.tensor_tensor(out=dstf, in0=src_ap,
                                                   in1=st4b[:, 0:H].unsqueeze(2).broadcast_to([128, H, 128]), op=ALU.mult),
                  src_bufs + [st4b], [f3])
            kb.op("pool", lambda e: e.tensor_tensor(out=dstf, in0=dstf,
                                                    in1=bcv[:, gcol:gcol + 128].unsqueeze(1).broadcast_to([128, H, 128]),
                                                    op=ALU.mult), [f3, bcv], [f3])

        def transposes(src_ap_fn, src_bufs, n, dst_ap, dst_bufs, eng="dve"):
            for k in range(n):
                kb.op("pe", lambda e, k=k: e.transpose(out=psT_t[:, k, :], in_=src_ap_fn(k), identity=identb[:]),
                      src_bufs + [identb], [PT], inc=(k == n - 1))
            if eng == "act":
                kb.op("act", lambda e: e.activation(out=dst_ap, in_=psT_t[:, 0:n, :], func=AF.Identity), [PT], dst_bufs)
            else:
                kb.op("dve", lambda e: e.tensor_copy(out=dst_ap, in_=psT_t[:, 0:n, :]), [PT], dst_bufs)

        def state_update_half(S, kd, cdoff, pb, hh):
            for h in (2 * hh, 2 * hh + 1):
                kb.op("pe", lambda e, h=h: e.matmul(psF_t[:, pb, (h % 2) * 256:(h % 2 + 1) * 256],
                                                    lhsT=kd[:, h, :], rhs=v_tok[:, h, :], start=True, stop=True),
                      [kd, v_tok], [PB[pb]], inc=True)
            for h in (2 * hh, 2 * hh + 1):
                kb.op("dve", lambda e, h=h: e.scalar_tensor_tensor(
                    out=S[:, h, :], in0=S[:, h, :], scalar=CD[:, cdoff + h:cdoff + h + 1],
                    in1=psF_t[:, pb, (h % 2) * 256:(h % 2 + 1) * 256], op0=ALU.mult, op1=ALU.add),
                    [S, CD, PB[pb]], [S])

        def dump(name, buf, shape, dt=F32):
            if name not in debug or kb.dry:
                return
            d = nc.dram_tensor("dbg_" + name, list(shape), dt, kind="ExternalOutput").ap()
            dbg_out[name] = d
            tok = kb.dma("sp", d, buf[:], [buf], [], key="dbg_" + name)
            kb.final_waits[tok[0]] = tok[1]

        WA = {}
        for i, b in enumerate((1, 2, 3, 8)):
            slot = (wringF + wringM)[i]
            kb.dma("sp", slot[:], wsc_d[WT_IN + b], [WSC[WT_IN + b]], [slot], key="wA%d" % i)
            WA[b] = slot

        tilesA = [("c", 0, "f"), ("c", 1, "fb"), ("c", 0, "b")] + [("l", c, "s") for c in range(NCH - 1, -1, -1)]
        if stop_after == "const":
            tilesA = []

        def loadA(ti):
            kind, c, mode = tilesA[ti]
            xt = xring[ti % 2]
            rtb = rtab[ti % 2]
            atb = atab[ti % 2]
            if kind == "c":
                src = ctx_d[c * 128:(c + 1) * 128, :]
                pos0 = c * 128
            else:
                src = x_d[c * 128:(c + 1) * 128, :]
                pos0 = NCTX + c * 128
            kb.dma("sp", xt[:], src, [], [xt], key="x%d" % (ti % 2))
            kb.dma("sp", rtb[:], rt_d[pos0:pos0 + 128, :], [], [rtb], key="rt%d" % (ti % 2))
            if kind == "l":
                kb.dma("sp", atb[:], at_d[c * 128:(c + 1) * 128, :], [], [atb], key="at%d" % (ti % 2))

        if tilesA:
            loadA(0)
        for ti, (kind, c, mode) in enumerate(tilesA):
            xt = xring[ti % 2]
            rtb = rtab[ti % 2]
            atb = atab[ti % 2]
            sc_idx = (NCH + c) if kind == "c" else c
            if ti + 1 < len(tilesA):
                loadA(ti + 1)
            if late_precasts and ti >= 1:
                precast(*late_precasts.pop(0))
            norm_part1(xt, xs, st4)
            norm_part2(0 if kind == "l" else 2, hT, xs, f3)
            for b, pb in ((1, 0), (2, 1), (3, 2), (8, 3)):
                mm8(hT, WA[b], pb)
            rotate(qk_tok[:, 4:8, :], [qk_tok], psF_t[:, 0, :].rearrange("p (H d) -> p H d", H=4), [PB[0]], rtb, 4, 1)
            kb.op("act", lambda e: e.activation(out=v_tok[:].rearrange("p h v -> p (h v)").rearrange("p (a n) -> p a n", a=2),
                                                in_=bank2(1), func=AF.Identity), [PB[1], PB[2]], [v_tok])
            do_kv = not (kind == "c" and mode == "b")
            if do_kv:
                akf = f3[:, 0:256].rearrange("p (H d) -> p H d", H=2)
                head_rms(psF_t[:, 3, 0:256].rearrange("p (H d) -> p H d", H=2), [PB[3]], 2, BC_KG, akf)
                if kind == "l":
                    rotate(ak_tok[:], [ak_tok], akf, [f3], atb, 2, 2)
                else:
                    kb.op("pool", lambda e, akf=akf: e.tensor_copy(out=ak_tok[:], in_=akf), [f3], [ak_tok])
                transposes(lambda k: ak_tok[:, k, :], [ak_tok], 2,
                           KT[:, :, sc_idx * 128:(sc_idx + 1) * 128], [KT])
                kb.op("act", lambda e, sc_idx=sc_idx: e.activation(
                    out=VA[:, sc_idx, :, 0:128], in_=psF_t[:, 3, 256:512].rearrange("p (a d) -> p a d", a=2),
                    func=AF.Identity), [PB[3]], [VA])
            if "f" in mode:
                kb.op("dve", lambda e: e.tensor_tensor(out=kdf[:], in0=qk_tok[:, 4:8, :],
                                                       in1=KD[:, 0:4].unsqueeze(2).broadcast_to([128, 4, 128]), op=ALU.mult),
                      [qk_tok, KD], [kdf])
                state_update_half(Sf, kdf, 0, 4, 0)
                state_update_half(Sf, kdf, 0, 5, 1)
            if "b" in mode or mode == "s":
                if mode == "s":
                    sr = sbs_ring[c % 2]
                    kb.op("act", lambda e, sr=sr: e.activation(out=sr[:], in_=Sb_[:], func=AF.Identity), [Sb_], [sr])
                    kb.dma("sp", sbs_d[c], sr[:], [sr], [SBSD[c]], key="sbsw%d" % (c % 2))
                if not (mode == "s" and c == 0):
                    kb.op("dve", lambda e: e.tensor_tensor(out=kdb[:], in0=qk_tok[:, 4:8, :],
                                                           in1=KD[:, 4:8].unsqueeze(2).broadcast_to([128, 4, 128]), op=ALU.mult),
                          [qk_tok, KD], [kdb])
                    state_update_half(Sb_, kdb, 4, 4, 0)
                    state_update_half(Sb_, kdb, 4, 5, 1)

        while late_precasts:
            precast(*late_precasts.pop(0))
        dump("KT", KT, [128, 2, NSC * 128], BF16)
        dump("Sf", Sf, [128, 4, 256])

        chunksB = list(range(NCH))
        if stop_after in ("const", "A"):
            chunksB = []
        if isinstance(stop_after, tuple) and stop_after[0] == "B":
            chunksB = list(range(stop_after[1]))
        nB = len(chunksB)
        att_scale = 128.0 ** -0.5
        F_BLOCKS = (6, 7, 0, 1, 2, 3, 4, 5, 9, 10, 11, 12)
        planF = []
        planM = []
        for _ in chunksB:
            planF.extend([WT_IN + b for b in F_BLOCKS])
            planM.extend([WT_BR_RET, WT_BR_RET + 1, WT_BR_ATT, WT_BR_ATT + 1, WT_OUT, WT_OUT + 1] +
                         [WT_UP + i for i in range(8)] + [WT_DOWN + i for i in range(8)])
        ring4b = wringF + wringM
        wst = {"order": [], "index": {}, "issued": 0}

        def wtile(which, n):
            if kb.dry:
                wst["index"][(which, n)] = len(wst["order"])
                wst["order"].append((planF if which == "F" else planM)[n])
                return ring4b[0]
            g = wst["index"][(which, n)]
            order = wst["order"]
            nw = len(ring4b)
            while wst["issued"] < min(len(order), g + nw):
                i = wst["issued"]
                t = order[i]
                slot = ring4b[i % nw]
                kb.dma("sp", slot[:], wsc_d[t], [WSC[t]], [slot], key="wr%d" % (i % nw))
                wst["issued"] += 1
            return ring4b[g % nw]

        bstate = {"nb": 0}

        def nbank():
            i = bstate["nb"]
            bstate["nb"] += 1
            return 4 + (i % 3)

        x0 = len(tilesA)

        def load_chunk_inputs(cj):
            cc = chunksB[cj]
            s2 = (x0 + cj) % 2
            kb.dma("sp", xring[s2][:], x_d[cc * 128:(cc + 1) * 128, :], [], [xring[s2]], key="x%d" % s2)
            kb.dma("sp", rtab[s2][:], rt_d[NCTX + cc * 128:NCTX + (cc + 1) * 128, :], [], [rtab[s2]], key="rt%d" % s2)
            kb.dma("sp", atab[s2][:], at_d[cc * 128:(cc + 1) * 128, :], [], [atab[s2]], key="at%d" % s2)
            kb.dma("sp", sbs_ring[cj % 2][:], sbs_d[cc], [SBSD[cc]], [sbs_ring[cj % 2]], key="sbsr%d" % (cj % 2))

        def stage_F(cj):
            c = chunksB[cj]
            par = cj % 2
            s2 = (x0 + cj) % 2
            xt, rtb, atb, sr = xring[s2], rtab[s2], atab[s2], sbs_ring[cj % 2]
            QTd, oretTd, sgrd, sgad = QT[par], oretT[par], sgr[par], sga[par]
            wb0 = cj * 12
            norm_part1(xt, xs, st4)
            yield
            norm_part2(0, hT, xs, f3)
            yield
            for half in range(2):
                pb = nbank()
                mm8(hT, wtile("F", wb0 + half), pb)
                yield
                aqf = f3[:, half * 512:(half + 1) * 512].rearrange("p (H d) -> p H d", H=4)
                head_rms(bank(pb).rearrange("p (H d) -> p H d", H=4), [PB[pb]], 4, BC_QG, aqf)
                rotate(aq_tok[:, half * 4:(half + 1) * 4, :], [aq_tok], aqf, [f3], atb, 4, 2)
                yield
            for half in range(2):
                pb = nbank()
                mm8(hT, wtile("F", wb0 + 2 + half), pb)
                yield
                rotate(qk_tok[:, half * 4:(half + 1) * 4, :], [qk_tok],
                       bank(pb).rearrange("p (H d) -> p H d", H=4), [PB[pb]], rtb, 4, 1)
                if half == 0:
                    transposes(lambda k: aq_tok[:, k, :], [aq_tok], 8, QTd[:], [QTd])
                else:
                    kb.op("pool", lambda e: e.tensor_tensor(out=kdf[:], in0=qk_tok[:, 4:8, :],
                                                            in1=KD[:, 0:4].unsqueeze(2).broadcast_to([128, 4, 128]), op=ALU.mult),
                          [qk_tok, KD], [kdf])
                yield
            for half in range(2):
                pb = nbank()
                mm8(hT, wtile("F", wb0 + 4 + half), pb)
                yield
                kb.op("act", lambda e, half=half, pb=pb: e.activation(
                    out=v_tok[:, 2 * half:2 * half + 2, :].rearrange("p h v -> p (h v)"), in_=bank(pb), func=AF.Identity),
                    [PB[pb]], [v_tok])
                yield
            for k in range(8):
                kb.op("pe", lambda e, k=k: e.transpose(out=psT_t[:, k, :], in_=qk_tok[:, k, :], identity=identb[:]),
                      [qk_tok, identb], [PT], inc=(k == 7))
            kb.op("dve", lambda e: e.tensor_copy(out=qT[:], in_=psT_t[:, 0:4, :]), [PT], [qT])
            kb.op("dve", lambda e: e.tensor_tensor(out=qfT[:], in0=psT_t[:, 0:4, :], in1=QF[:], op=ALU.mult), [PT, QF], [qfT])
            kb.op("dve", lambda e: e.tensor_tensor(out=qbT[:], in0=psT_t[:, 0:4, :], in1=QB[:], op=ALU.mult), [PT, QB], [qbT])
            kb.op("dve", lambda e: e.tensor_copy(out=kT[:], in_=psT_t[:, 4:8, :]), [PT], [kT])
            kb.op("act", lambda e: e.activation(out=Sf_bf[:], in_=Sf[:], func=AF.Identity), [Sf], [Sf_bf])
            yield
            for half in range(2):
                pb = nbank()
                mm8(hT, wtile("F", wb0 + 6 + half), pb)
                yield
                hs = slice(half * 512, (half + 1) * 512)
                kb.op("act", lambda e, hs=hs, pb=pb: e.activation(out=f2[:, hs], in_=bank(pb), func=AF.Tanh, scale=0.5),
                      [PB[pb]], [f2])
                kb.op("dve", lambda e, hs=hs, pb=pb: e.scalar_tensor_tensor(out=sg[:, hs], in0=f2[:, hs], scalar=1.0, in1=bank(pb),
                                                                             op0=ALU.add, op1=ALU.mult), [f2, PB[pb]], [sg])
                yield
            pbs = nbank()
            for h in range(4):
                kb.op("pe", lambda e, h=h: e.matmul(psF_t[:, pbs, h * 128:(h + 1) * 128], lhsT=kT[:, h, :], rhs=qT[:, h, :],
                                                    start=True, stop=True), [kT, qT], [PB[pbs]], inc=(h == 3))
            yield
            kb.op("dve", lambda e: e.tensor_tensor(out=PTm[:], in0=psF_t[:, pbs, :].rearrange("p (h i) -> p h i", h=4),
                                                   in1=DT[:], op=ALU.mult), [PB[pbs], DT], [PTm])
            yield
            for gi_, dstg in ((0, sgrd), (1, sgad)):
                for half in range(2):
                    pb = nbank()
                    mm8(hT, wtile("F", wb0 + 8 + 2 * gi_ + half), pb)
                    yield
                    hs = slice(half * 512, (half + 1) * 512)
                    kb.op("act", lambda e, hs=hs, pb=pb: e.activation(out=f2[:, hs], in_=bank(pb), func=AF.Tanh, scale=0.5),
                          [PB[pb]], [f2])
                    kb.op("pool", lambda e, hs=hs, dstg=dstg: e.tensor_scalar(out=dstg[:, hs], in0=f2[:, hs], scalar1=0.5, scalar2=0.5,
                                                                               op0=ALU.mult, op1=ALU.add), [f2], [dstg])
                    yield
            for hh in range(2):
                pb = nbank()
                for h in (2 * hh, 2 * hh + 1):
                    oap = psF_t[:, pb, (h % 2) * 256:(h % 2 + 1) * 256]
                    kb.op("pe", lambda e, h=h, oap=oap: e.matmul(oap, lhsT=PTm[:, h, :], rhs=v_tok[:, h, :], start=True, stop=False),
                          [PTm, v_tok], [PB[pb]], inc=False)
                    kb.op("pe", lambda e, h=h, oap=oap: e.matmul(oap, lhsT=qfT[:, h, :], rhs=Sf_bf[:, h, :], start=False, stop=False),
                          [qfT, Sf_bf], [PB[pb]], inc=False)
                    kb.op("pe", lambda e, h=h, oap=oap: e.matmul(oap, lhsT=qbT[:, h, :], rhs=sr[:, h, :], start=False, stop=True),
                          [qbT, sr], [PB[pb]], inc=True)
                yield
                for h in (2 * hh, 2 * hh + 1):
                    oap = psF_t[:, pb, (h % 2) * 256:(h % 2 + 1) * 256]
                    kb.op("dve", lambda e, h=h, oap=oap: e.bn_stats(out=bnst[:, h, :], in_=oap), [PB[pb]], [bnst])
                    kb.op("dve", lambda e, h=h: e.bn_aggr(out=bnag[:, h, :], in_=bnst[:, h, :]), [bnst], [bnag])
                rstd_pool(bnag[:, 2 * hh:2 * hh + 2, 1], 2, 1.0, st4b[:, 12 + 2 * hh:14 + 2 * hh], bnag, st4b)
                for h in (2 * hh, 2 * hh + 1):
                    oap = psF_t[:, pb, (h % 2) * 256:(h % 2 + 1) * 256]
                    kb.op("dve", lambda e, h=h, oap=oap: e.tensor_scalar(
                        out=f1[:, h * 256:(h + 1) * 256], in0=oap, scalar1=bnag[:, h, 0:1], scalar2=st4b[:, 12 + h:13 + h],
                        op0=ALU.subtract, op1=ALU.mult), [PB[pb], bnag, st4b], [f1])
                yield
            for hh in range(2):
                pb = nbank()
                state_update_half(Sf, kdf, 0, pb, hh)
                yield
            kb.op("pool", lambda e: e.tensor_tensor(out=f1[:], in0=f1[:], in1=bcv[:, BC_GNW:BC_GNW + 1024], op=ALU.mult),
                  [f1, bcv], [f1])
            kb.op("pool", lambda e: e.tensor_tensor(out=f1[:], in0=f1[:], in1=bcv[:, BC_GNB:BC_GNB + 1024], op=ALU.add),
                  [f1, bcv], [f1])
            kb.op("dve", lambda e: e.scalar_tensor_tensor(out=oret[:], in0=f1[:], scalar=0.5, in1=sg[:], op0=ALU.mult, op1=ALU.mult),
                  [f1, sg], [oret])
            yield
            transposes(lambda k: oret[:, k * 128:(k + 1) * 128], [oret], 8, oretTd[:], [oretTd])
            if cj == 0:
                dump("qk_tok", qk_tok, [128, 8, 128], BF16)
                dump("oret", oret, [128, 1024], BF16)
                dump("QT", QTd, [128, 8, 128], BF16)
            yield

        def stage_A(cj):
            par = cj % 2
            QTd = QT[par]
            iters = [(kvh, sc) for kvh in range(2) for sc in range(NSC)]
            SBANK = (0, 1)
            OBK = (2, 3)

            def emit_S(i):
                kvh, sc = iters[i]
                sbk = SBANK[i % 2]
                qrhs = QTd[:, kvh * 4:(kvh + 1) * 4, :].rearrange("p h t -> p (h t)")
                kb.op("pe", lambda e: e.matmul(bank(sbk), lhsT=KT[:, kvh, sc * 128:(sc + 1) * 128], rhs=qrhs,
                                               start=True, stop=True), [KT, QTd], [PB[sbk]], inc=True)

            def emit_PV(i):
                kvh, sc = iters[i]
                et = ET[i % len(ET)]
                for g in range(4):
                    ob = OBK[g // 2]
                    oap = psF_t[:, ob, (g % 2) * 129:(g % 2) * 129 + 129]
                    kb.op("pe", lambda e, g=g, oap=oap: e.matmul(
                        oap, lhsT=et[:, g * 128:(g + 1) * 128], rhs=VA[:, sc, kvh, :],
                        start=(sc == 0 and g % 2 == 0), stop=(sc == NSC - 1), skip_group_check=True),
                        [et, VA], [PB[ob]], inc=(g == 3))
                if sc == NSC - 1:
                    for g in range(4):
                        ob = OBK[g // 2]
                        off = (g % 2) * 129
                        kb.op("dve", lambda e, g=g, ob=ob, off=off: e.reciprocal(
                            out=rs8[:, kvh * 4 + g:kvh * 4 + g + 1], in_=psF_t[:, ob, off + 128:off + 129]), [PB[ob]], [rs8])
                        kb.op("dve", lambda e, g=g, ob=ob, off=off: e.tensor_scalar(
                            out=oatt_tok[:, kvh * 4 + g, :], in0=psF_t[:, ob, off:off + 128],
                            scalar1=rs8[:, kvh * 4 + g:kvh * 4 + g + 1], scalar2=None, op0=ALU.mult),
                            [PB[ob], rs8], [oatt_tok])

            emit_S(0)
            for i, (kvh, sc) in enumerate(iters):
                if i + 1 < len(iters):
                    emit_S(i + 1)
                sbk = SBANK[i % 2]
                et = ET[i % len(ET)]
                kb.op("act", lambda e, sbk=sbk, et=et: e.activation(out=et[:], in_=bank(sbk), func=AF.Exp, scale=att_scale),
                      [PB[sbk]], [et])
                if i >= 1:
                    emit_PV(i - 1)
                yield
            emit_PV(len(iters) - 1)
            yield

        def stage_M(cj):
            c = chunksB[cj]
            par = cj % 2
            oretTd, sgrd, sgad = oretT[par], sgr[par], sga[par]
            wb0 = cj * 22
            kb.dma("sp", xl[:], x_d[c * 128:(c + 1) * 128, :], [], [xl], key="xl")
            for cb in range(2):
                pb = nbank()
                hs = slice(cb * 512, (cb + 1) * 512)
                mm8(oretTd, wtile("M", wb0 + cb), pb)
                yield
                kb.op("dve", lambda e, hs=hs, pb=pb: e.tensor_tensor(out=m1[:, hs], in0=bank(pb), in1=sgrd[:, hs], op=ALU.mult),
                      [PB[pb], sgrd], [m1h[cb]])
                yield
            transposes(lambda k: oatt_tok[:, k, :], [oatt_tok], 8, oattT[:], [oattT])
            if cj == 0:
                dump("oattT", oattT, [128, 8, 128], BF16)
            yield
            for cb in range(2):
                pb = nbank()
                hs = slice(cb * 512, (cb + 1) * 512)
                mm8(oattT, wtile("M", wb0 + 2 + cb), pb)
                yield
                kb.op("dve", lambda e, hs=hs, pb=pb: e.tensor_tensor(out=m2[:, hs], in0=bank(pb), in1=sgad[:, hs], op=ALU.mult),
                      [PB[pb], sgad], [m2])
                yield
            kb.op("pool", lambda e: e.tensor_tensor(out=mrg[:], in0=m1[:], in1=m2[:], op=ALU.add), [m1h[0], m1h[1], m2], [mrg])
            yield
            transposes(lambda k: mrg[:, k * 128:(k + 1) * 128], [mrg], 8, mT[:], [mT])
            if cj == 0:
                dump("mrg", mrg, [128, 1024], BF16)
            yield
            for cb in range(2):
                pb = nbank()
                hs = slice(cb * 512, (cb + 1) * 512)
                mm8(mT, wtile("M", wb0 + 4 + cb), pb)
                yield
                kb.op("dve", lambda e, hs=hs, pb=pb: e.tensor_tensor(out=m1[:, hs], in0=bank(pb), in1=gm_bc[:, hs], op=ALU.mult),
                      [PB[pb], gm_bc], [m1h[cb]])
                yield
            kb.op("pool", lambda e: e.tensor_tensor(out=xl[:], in0=xl[:], in1=m1[:], op=ALU.add), [xl, m1h[0], m1h[1]], [xl])
            norm_part1(xl, xs2, st4m)
            if cj == 0:
                dump("xl", xl, [128, 1024])
            yield
            norm_part2(4, hT2, xs2, m2)
            yield
            for cb in range(8):
                pb = nbank()
                wb = wtile("M", wb0 + 6 + cb)
                for jj in range(4):
                    for k in range(8):
                        kb.op("pe", lambda e, k=k, jj=jj, wb=wb, pb=pb: e.matmul(
                            psF_t[:, pb, jj * 128:(jj + 1) * 128], lhsT=wb[:, k, jj * 128:(jj + 1) * 128], rhs=hT2[:, k, :],
                            start=(k == 0), stop=(k == 7)), [wb, hT2], [PB[pb]], inc=(k == 7 and jj == 3))
                hv = cb % 2
                hs = slice(hv * 512, (hv + 1) * 512)
                yield
                kb.op("act", lambda e, pb=pb, hs=hs: e.activation(out=m1[:, hs], in_=bank(pb), func=AF.Relu), [PB[pb]], [m1h[hv]])
                kb.op("pool", lambda e, cb=cb, hs=hs: e.tensor_tensor(out=hid[:, cb * 4:(cb + 1) * 4, :].rearrange("p j t -> p (j t)"),
                                                                      in0=m1[:, hs], in1=m1[:, hs], op=ALU.mult), [m1h[hv]], [hid])
                yield
            for cb in range(2):
                hs = slice(cb * 512, (cb + 1) * 512)
                for kg in range(4):
                    pb = nbank()
                    wb = wtile("M", wb0 + 14 + cb * 4 + kg)
                    for k in range(8):
                        j = kg * 8 + k
                        kb.op("pe", lambda e, k=k, j=j, wb=wb, pb=pb: e.matmul(
                            bank(pb), lhsT=hid[:, j, :], rhs=wb[:, k, :], start=(k == 0), stop=(k == 7)),
                            [hid, wb], [PB[pb]], inc=(k == 7))
                    yield
                    if kg == 0:
                        kb.op("dve", lambda e, hs=hs, pb=pb: e.tensor_copy(out=m1[:, hs], in_=bank(pb)), [PB[pb]], [m1h[cb]])
                    else:
                        kb.op("dve", lambda e, hs=hs, pb=pb: e.tensor_tensor(out=m1[:, hs], in0=m1[:, hs], in1=bank(pb), op=ALU.add),
                              [PB[pb], m1h[cb]], [m1h[cb]])
                    yield
                kb.op("pool", lambda e, hs=hs: e.tensor_tensor(out=m1[:, hs], in0=m1[:, hs], in1=gf_bc[:, hs], op=ALU.mult),
                      [m1h[cb], gf_bc], [m1h[cb]])
            kb.op("pool", lambda e: e.tensor_tensor(out=m1[:], in0=m1[:], in1=xl[:], op=ALU.add), [m1h[0], m1h[1], xl], [m1h[0], m1h[1]])
            kb.op("act", lambda e: e.activation(out=xs2[:], in_=m1[:], func=AF.Square, accum_out=st4m[:, 4:5]),
                  [m1h[0], m1h[1]], [xs2, st4m])
            rstd_pool(st4m[:, 4:5], 1, 1.0 / D, st4m[:, 5:6], st4m, st4m)
            kb.op("dve", lambda e: e.scalar_tensor_tensor(out=m2[:], in0=m1[:], scalar=st4m[:, 5:6],
                                                          in1=bcv[:, BC_FIN:BC_FIN + 1024], op0=ALU.mult, op1=ALU.mult),
                  [m1h[0], m1h[1], st4m, bcv], [m2])
            tok = kb.dma("sp", y_d[c * 128:(c + 1) * 128, :], m2[:], [m2], [], key="st0")
            if tok is not None:
                kb.final_waits[tok[0]] = tok[1]
            yield

        def exhaust(g):
            for _ in g:
                pass

        def step(g):
            try:
                next(g)
                return True
            except StopIteration:
                return False

        def phaseB():
            bstate["nb"] = 0
            if nB > 0:
                load_chunk_inputs(0)
                if nB > 1:
                    load_chunk_inputs(1)
                exhaust(stage_F(0))
            RATIO = 1.3
            ucount = [0, 0]
            for cj in range(nB):
                others = []
                if cj >= 1:
                    others.append(stage_M(cj - 1))
                if cj + 1 < nB:
                    if cj + 2 < nB:
                        pass
                    others.append(stage_F(cj + 1))
                ga = stage_A(cj)
                acc = 0.0
                rr = 0
                while step(ga):
                    acc += RATIO
                    while acc >= 1.0 and others:
                        acc -= 1.0
                        g = others[rr % len(others)]
                        if not step(g):
                            others.remove(g)
                        else:
                            rr += 1
                            ucount[0] += 1
                for g in others:
                    for _ in g:
                        ucount[1] += 1
                dbg_out["_units"] = list(ucount)
                if cj + 2 < nB:
                    load_chunk_inputs(cj + 2)
            if nB > 0:
                exhaust(stage_M(nB - 1))


        kb.dry = True
        phaseB()
        kb.dry = False
        phaseB()

        kb.wait_all("sp", list(kb.final_waits.items()))
        kb.run()
    return nc, dbg_out


def prep_inputs(inputs):
    f = np.float32
    x = np.asarray(inputs["x"], f)
    c = np.asarray(inputs["c"], f)
    ctx = np.asarray(inputs["ctx"], f)
    c_ctx = np.asarray(inputs["c_ctx"], f)

    def pl(v):
        return np.asarray(v, f).reshape(-1, 128).T

    rt, at = host_tabs()
    consts = host_consts()
    lam = np.concatenate([np.asarray(inputs["ret_log_lam_fwd"], f)[0], np.asarray(inputs["ret_log_lam_bwd"], f)[0]])
    lam_rep = np.ascontiguousarray(np.broadcast_to(lam[None, :], (128, 8)))
    bc = np.concatenate([np.asarray(inputs["ret_gn_w"], f)[0], np.asarray(inputs["ret_gn_b"], f)[0],
                         np.asarray(inputs["final_norm_g"], f), np.asarray(inputs["att_q_norm_g"], f)[0],
                         np.asarray(inputs["att_k_norm_g"], f)[0]])
    bc_rep = np.ascontiguousarray(np.broadcast_to(bc[None, :], (128, NBC)))
    shared = {
        "lam": lam_rep, "bc": bc_rep, "rt": rt, "at": at, "consts": consts,
        "mod_w": np.ascontiguousarray(np.asarray(inputs["mod_w"], f)[0]),
        "w_in": np.ascontiguousarray(np.asarray(inputs["w_in"], f)[0]),
        "w_br_ret": np.ascontiguousarray(np.asarray(inputs["w_br_ret"], f)[0]),
        "w_br_att": np.ascontiguousarray(np.asarray(inputs["w_br_att"], f)[0]),
        "w_out": np.ascontiguousarray(np.asarray(inputs["w_out"], f)[0]),
        "w_mlp_up": np.ascontiguousarray(np.asarray(inputs["w_mlp_up"], f)[0]),
        "w_mlp_down": np.ascontiguousarray(np.asarray(inputs["w_mlp_down"], f)[0]),
    }
    in_maps = []
    for b in range(8):
        vecs = np.concatenate([pl(c[b]), pl(c_ctx), pl(np.asarray(inputs["mod_b"], f)[0]),
                               pl(np.asarray(inputs["norm_mix_g"], f)[0]), pl(np.asarray(inputs["norm_mlp_g"], f)[0])], axis=1)
        m = dict(shared)
        m["x"] = np.ascontiguousarray(x[b])
        m["ctx"] = np.ascontiguousarray(ctx[b])
        m["vecs"] = np.ascontiguousarray(vecs.astype(f))
        in_maps.append(m)
    return in_maps


_CACHE = {}


def kernel(**inputs):
    in_maps = prep_inputs(inputs)
    if "nc" not in _CACHE:
        _CACHE["nc"] = build()[0]
    nc = _CACHE["nc"]
    res = run_bass_kernel_spmd(nc, in_maps, core_ids=list(range(8)))
    out = np.stack([np.asarray(r["y"], np.float32) for r in res.results], axis=0)
    return out
```

```python
import numpy as np
from contextlib import ExitStack
import concourse.bass as bass
import concourse.mybir as mybir
from concourse.bass_utils import run_bass_kernel_spmd

F32 = mybir.dt.float32
BF16 = mybir.dt.bfloat16
AF = mybir.ActivationFunctionType
ALU = mybir.AluOpType
AX = mybir.AxisListType

D = 1024
NTOK = 4096
NCTX = 256
NCH = NTOK // 128
NSC = NCH + NCTX // 128
DIN = 6656
DFF = 4096
EPS = 1e-6
NWT = 35
WT_IN = 0
WT_BR_RET = 13
WT_BR_ATT = 15
WT_OUT = 17
WT_UP = 19
WT_DOWN = 27


class Buf:
    def __init__(self, name, t):
        self.name = name
        self.t = t
        self.w = None
        self.r = {}

    def __getitem__(self, idx):
        return self.t[idx]


class KB:
    ENG = ("pe", "act", "dve", "pool", "sp")

    def __init__(self, nc, stack):
        self.nc = nc
        self.stack = stack
        self.q = {e: [] for e in self.ENG}
        self.cnt = {e: 0 for e in self.ENG}
        self.waited = {e: {} for e in self.ENG}
        self.sems = {}
        self.dcnt = {}
        for e in self.ENG:
            self.sems[e] = stack.enter_context(nc.semaphore("s_" + e))
        self.final_waits = {}

    def sem(self, key):
        if key not in self.sems:
            self.sems[key] = self.stack.enter_context(self.nc.semaphore("s_" + key))
            self.dcnt[key] = 0
        return self.sems[key]

    def sb(self, name, shape, dt):
        return Buf(name, self.stack.enter_context(self.nc.sbuf_tensor("sb_" + name, list(shape), dt)))

    def _deps(self, eng, reads, writes):
        deps = {}

        def add(k, v):
            if deps.get(k, 0) < v:
                deps[k] = v

        for b in reads:
            if b.w is not None:
                add(*b.w)
        for b in writes:
            if b.w is not None:
                add(*b.w)
            for k, v in b.r.items():
                add(k, v)
        waits = []
        for k, v in deps.items():
            if k == eng:
                if eng in ("pe", "sp"):
                    continue
                if v > self.cnt[eng]:
                    continue
            if self.waited[eng].get(k, 0) >= v:
                continue
            self.waited[eng][k] = v
            waits.append((k, v))
        return waits

    def _mark(self, tok, reads, writes):
        for b in writes:
            b.w = tok
            b.r = {}
        for b in reads:
            if b in writes:
                continue
            k, v = tok
            if b.r.get(k, 0) < v:
                b.r[k] = v

    dry = False

    def op(self, eng, fn, reads=(), writes=(), inc=True):
        if self.dry:
            return None
        pr = [b for b in reads if getattr(b, "psum", False)]
        if pr:
            reads = [b for b in reads if not getattr(b, "psum", False)]
            writes = list(writes) + [b for b in pr if b not in writes]
        waits = self._deps(eng, reads, writes)
        if inc:
            self.cnt[eng] += 1
            tok = (eng, self.cnt[eng])
        else:
            tok = (eng, self.cnt[eng] + 1)
        sems = self.sems

        def thunk(e, waits=waits, fn=fn, inc=inc, eng=eng):
            for k, v in waits:
                e.wait_ge(sems[k], v)
            ins = fn(e)
            if inc:
                ins.then_inc(sems[eng], 1)

        self.q[eng].append(thunk)
        self._mark(tok, reads, writes)
        return tok

    def dma(self, eng, out, in_, reads, writes, key, group_total=None, **kw):
        if self.dry:
            return None
        s = self.sem(key)
        waits = self._deps(eng, reads, writes)
        self.dcnt[key] += 1
        val = 16 * (group_total if group_total is not None else self.dcnt[key])
        tok = (key, val)
        sems = self.sems

        def thunk(e, waits=waits):
            for k, v in waits:
                e.wait_ge(sems[k], v)
            e.dma_start(out=out, in_=in_, **kw).then_inc(s, 16)

        self.q[eng].append(thunk)
        self._mark(tok, reads, writes)
        return tok

    def wait_all(self, eng, toks):
        sems = self.sems

        def thunk(e):
            for k, v in toks:
                e.wait_ge(sems[k], v)

        self.q[eng].append(thunk)

    def run(self):
        nc = self.nc
        with nc.Block() as block:
            @block.tensor
            def _(e):
                for f in self.q["pe"]:
                    f(e)

            @block.scalar
            def _(e):
                for f in self.q["act"]:
                    f(e)

            @block.vector
            def _(e):
                for f in self.q["dve"]:
                    f(e)

            @block.gpsimd
            def _(e):
                for f in self.q["pool"]:
                    f(e)

            @block.sync
            def _(e):
                for f in self.q["sp"]:
                    f(e)


def host_consts():
    i = np.arange(128, dtype=np.float32)
    ident = np.eye(128, dtype=np.float32)
    iota1 = np.tile((i + 1)[None, :], (128, 1))
    iota2 = np.tile((128 - i)[None, :], (128, 1))
    jj = i[:, None]
    ii = i[None, :]
    A = np.maximum(ii - jj, 0)
    B = np.maximum(jj - ii, 0)
    M1 = (ii >= jj).astype(np.float32)
    M2 = (jj > ii).astype(np.float32)
    p127 = (127 - i)[:, None]
    pj = i[:, None]
    return np.ascontiguousarray(
        np.concatenate([ident, iota1, iota2, A, B, M1, M2, p127, pj], axis=1).astype(np.float32))


def host_tabs():
    f32 = np.float32
    half = 64
    inv = (f32(10000.0) ** (-(np.arange(half, dtype=f32) / f32(half)))).astype(f32)
    pos = np.arange(NCTX + NTOK, dtype=f32)
    ang = (pos[:, None] * inv[None, :]).astype(f32)
    c, s = np.cos(ang).astype(f32), np.sin(ang).astype(f32)
    rt = np.concatenate([c, c, -s, s], axis=1).astype(f32)
    h2 = 32
    inv2 = (f32(10000.0) ** (-(np.arange(h2, dtype=f32) / f32(h2)))).astype(f32)
    t = np.arange(NTOK)
    row = (t // 64).astype(f32)
    col = (t % 64).astype(f32)
    ar = (row[:, None] * inv2[None, :]).astype(f32)
    ac = (col[:, None] * inv2[None, :]).astype(f32)
    cr, sr, cc, sc = np.cos(ar), np.sin(ar), np.cos(ac), np.sin(ac)
    at = np.concatenate([cr, cr, cc, cc, -sr, sr, -sc, sc], axis=1).astype(f32)
    return np.ascontiguousarray(rt), np.ascontiguousarray(at)


C_ID, C_I1, C_I2, C_A, C_B, C_M1, C_M2 = [k * 128 for k in range(7)]
C_P127 = 7 * 128
C_PJ = 7 * 128 + 1
NCONST = 7 * 128 + 2
V_C, V_CC, V_MODB, V_GMIX, V_GMLP = 0, 8, 16, 64, 72
NVEC = 80
BC_GNW, BC_GNB, BC_FIN, BC_QG, BC_KG = 0, 1024, 2048, 3072, 3200
NBC = 3328


def build(debug=None, stop_after=None):
    debug = debug or []
    nc = bass.Bass("TRN2", target_bir_lowering=False)

    def din(name, shape, dt=F32):
        return nc.dram_tensor(name, list(shape), dt, kind="ExternalInput").ap()

    x_d = din("x", [NTOK, D])
    ctx_d = din("ctx", [NCTX, D])
    vecs_d = din("vecs", [128, NVEC])
    lam_d = din("lam", [128, 8])
    bc_d = din("bc", [128, NBC])
    rt_d = din("rt", [NCTX + NTOK, 256])
    at_d = din("at", [NTOK, 256])
    consts_d = din("consts", [128, NCONST])
    modw_d = din("mod_w", [D, 6 * D])
    win_d = din("w_in", [D, DIN])
    wbr_ret_d = din("w_br_ret", [D, D])
    wbr_att_d = din("w_br_att", [D, D])
    wout_d = din("w_out", [D, D])
    wup_d = din("w_mlp_up", [D, DFF])
    wdown_d = din("w_mlp_down", [DFF, D])
    y_d = nc.dram_tensor("y", [NTOK, D], F32, kind="ExternalOutput").ap()
    wsc_d = nc.dram_tensor("wsc", [NWT, 128, 8, 512], BF16, kind="Internal").ap()
    sbs_d = nc.dram_tensor("sbs_scr", [NCH, 128, 4, 256], BF16, kind="Internal").ap()
    dbg_out = {}

    stack = ExitStack()
    with stack:
        kb = KB(nc, stack)
        sb = kb.sb
        psF_t = stack.enter_context(nc.psum_tensor("psF", [128, 7, 512], F32))
        psT_t = stack.enter_context(nc.psum_tensor("psT", [128, 8, 128], BF16))
        PB = [Buf("psb%d" % i, None) for i in range(7)]
        PT = Buf("psT", psT_t)
        for _b in PB + [PT]:
            _b.psum = True

        def bank(i):
            return psF_t[:, i, :]

        def bank2(i):
            return psF_t[:, i:i + 2, :]

        consts = sb("consts", [128, NCONST], F32)
        vecs = sb("vecs", [128, NVEC], F32)
        lam = sb("lam", [128, 8], F32)
        bcv = sb("bcv", [128, NBC], F32)
        identb = sb("identb", [128, 128], BF16)
        onesf = sb("onesf", [128, 128], F32)
        negh = sb("negh", [128, 16], F32)
        modv = sb("modv", [128, 48, 2], F32)
        silu_c = sb("silu_c", [128, 8, 2], F32)
        gvec = sb("gvec", [128, 8, 8], F32)
        gm_bc = sb("gm_bc", [128, 1024], F32)
        gf_bc = sb("gf_bc", [128, 1024], F32)
        LG = sb("LG", [128, 8], F32)
        DT = sb("DT", [128, 4, 128], F32)
        QF = sb("QF", [128, 4, 128], F32)
        QB = sb("QB", [128, 4, 128], F32)
        KD = sb("KD", [128, 8], F32)
        CD = sb("CD", [128, 8], F32)
        KT = sb("KT", [128, 2, NSC * 128], BF16)
        VA = sb("VA", [128, NSC, 2, 129], BF16)
        SBSD = [Buf("sbsd%d" % i, None) for i in range(NCH)]
        sbs_ring = [sb("sbsr%d" % i, [128, 4, 256], BF16) for i in range(2)]
        Sf = sb("Sf", [128, 4, 256], F32)
        Sb_ = sb("Sb", [128, 4, 256], F32)
        Sf_bf = sb("Sf_bf", [128, 4, 256], BF16)
        wringF = [sb("wF%d" % i, [128, 8, 512], BF16) for i in range(2)]
        wringM = [sb("wM%d" % i, [128, 8, 512], BF16) for i in range(2)]
        xring = [sb("x%d" % i, [128, 1024], F32) for i in range(2)]
        rtab = [sb("rtab%d" % i, [128, 256], F32) for i in range(2)]
        atab = [sb("atab%d" % i, [128, 256], F32) for i in range(2)]
        st4 = sb("st4", [128, 16], F32)
        st4b = sb("st4b", [128, 16], F32)
        xs = sb("xs", [128, 1024], BF16)
        hT = sb("hT", [128, 8, 128], BF16)
        f1 = sb("f1", [128, 1024], F32)
        f2 = sb("f2", [128, 1024], F32)
        f3 = sb("f3", [128, 1024], F32)
        qk_tok = sb("qk_tok", [128, 8, 128], BF16)
        qT = sb("qT", [128, 4, 128], BF16)
        qfT = sb("qfT", [128, 4, 128], BF16)
        qbT = sb("qbT", [128, 4, 128], BF16)
        kT = sb("kT", [128, 4, 128], BF16)
        kdf = sb("kdf", [128, 4, 128], BF16)
        v_tok = sb("v_tok", [128, 4, 256], BF16)
        sg = sb("sg", [128, 1024], BF16)
        aq_tok = sb("aq_tok", [128, 8, 128], BF16)
        ak_tok = sb("ak_tok", [128, 2, 128], BF16)
        PTm = sb("PTm", [128, 4, 128], BF16)
        kdb = PTm
        oret = sb("oret", [128, 1024], BF16)
        bnst = sb("bnst", [128, 4, 6], F32)
        bnag = sb("bnag", [128, 4, 2], F32)
        QT = [sb("QT%d" % i, [128, 8, 128], BF16) for i in range(2)]
        oretT = [sb("oretT%d" % i, [128, 8, 128], BF16) for i in range(2)]
        sgr = [sb("sgr%d" % i, [128, 1024], BF16) for i in range(2)]
        sga = [sb("sga%d" % i, [128, 1024], BF16) for i in range(2)]
        ET = [sb("ET%d" % i, [128, 512], BF16) for i in range(2)]
        for _i in range(2):
            _b = Buf("ETc%d" % _i, None)
            _b.t = consts[:, _i * 256:(_i + 1) * 256].bitcast(BF16)
            ET.append(_b)
        oatt_tok = sb("oatt_tok", [128, 8, 128], BF16)
        oattT = sb("oattT", [128, 8, 128], BF16)
        rs8 = sb("rs8", [128, 8], F32)
        m1 = sb("m1", [128, 1024], F32)
        m1h = [Buf("m1a", None), Buf("m1b", None)]
        m2 = sb("m2", [128, 1024], F32)
        xl = sb("xl", [128, 1024], F32)
        mrg = sb("mrg", [128, 1024], BF16)
        mT = sb("mT", [128, 8, 128], BF16)
        xs2 = sb("xs2", [128, 1024], BF16)
        hT2 = sb("hT2", [128, 8, 128], BF16)
        hid = sb("hid", [128, 32, 128], BF16)
        st4m = sb("st4m", [128, 16], F32)

        def cslice(off, n=128):
            return consts[:, off:off + n]

        WSC = [Buf("wsc%d" % i, None) for i in range(NWT)]

        def precast(tile, src, group, total):
            kb.dma("pool", wsc_d[tile], src.rearrange("(k p) n -> p k n", p=128), [], [WSC[tile]],
                   key="pc_" + group, group_total=total)

        for b in (1, 2, 3, 8):
            precast(WT_IN + b, win_d[:, b * 512:(b + 1) * 512], "a", 4)
        late_precasts = []
        for b in (6, 7, 0, 4, 5, 9, 10, 11, 12):
            late_precasts.append((WT_IN + b, win_d[:, b * 512:(b + 1) * 512], "b", 9))
        for cb in range(2):
            late_precasts.append((WT_BR_RET + cb, wbr_ret_d[:, cb * 512:(cb + 1) * 512], "c", 6))
        for cb in range(2):
            late_precasts.append((WT_BR_ATT + cb, wbr_att_d[:, cb * 512:(cb + 1) * 512], "c", 6))
        for cb in range(2):
            late_precasts.append((WT_OUT + cb, wout_d[:, cb * 512:(cb + 1) * 512], "c", 6))
        for cb in range(8):
            late_precasts.append((WT_UP + cb, wup_d[:, cb * 512:(cb + 1) * 512], "d", 8))
        for cb in range(2):
            for kg in range(4):
                late_precasts.append((WT_DOWN + cb * 4 + kg,
                                      wdown_d[kg * 1024:(kg + 1) * 1024, cb * 512:(cb + 1) * 512], "e", 8))

        kb.dma("sp", consts[:], consts_d[:, :], [], [consts], key="c0")
        kb.dma("sp", vecs[:], vecs_d[:, :], [], [vecs], key="c1")
        kb.dma("sp", lam[:], lam_d[:, :], [], [lam], key="c2")
        kb.dma("sp", bcv[:], bc_d[:, :], [], [bcv], key="c3")
        kb.op("dve", lambda e: e.tensor_copy(out=identb[:], in_=cslice(C_ID)), [consts], [identb])
        kb.op("pool", lambda e: e.memset(onesf[:], 1.0), [], [onesf])
        kb.op("pool", lambda e: e.memset(negh[:], -0.5), [], [negh])
        kb.op("pool", lambda e: e.memset(Sf[:], 0.0), [], [Sf])
        kb.op("pool", lambda e: e.memset(Sb_[:], 0.0), [], [Sb_])
        kb.op("pool", lambda e: e.memset(VA[:, :, :, 128:129], 1.0), [], [VA])

        kb.op("act", lambda e: e.activation(out=LG[:], in_=lam[:], func=AF.Exp), [lam], [LG])
        kb.op("dve", lambda e: e.tensor_scalar(out=LG[:], in0=LG[:], scalar1=-1.0, scalar2=None, op0=ALU.mult),
              [LG], [LG])
        sc_k = 128.0 ** -0.5
        for h in range(4):
            kb.op("act", lambda e, h=h: e.activation(out=QF[:, h, :], in_=cslice(C_I1), func=AF.Exp,
                                                     scale=LG[:, h:h + 1]), [LG, consts], [QF])
            kb.op("act", lambda e, h=h: e.activation(out=QB[:, h, :], in_=cslice(C_I2), func=AF.Exp,
                                                     scale=LG[:, 4 + h:5 + h]), [LG, consts], [QB])
            kb.op("act", lambda e, h=h: e.activation(out=KD[:, h:h + 1], in_=consts[:, C_P127:C_P127 + 1],
                                                     func=AF.Exp, scale=LG[:, h:h + 1]), [LG, consts], [KD])
            kb.op("act", lambda e, h=h: e.activation(out=KD[:, 4 + h:5 + h], in_=consts[:, C_PJ:C_PJ + 1],
                                                     func=AF.Exp, scale=LG[:, 4 + h:5 + h]), [LG, consts], [KD])
            kb.op("act", lambda e, h=h: e.activation(out=f1[:, 0:128], in_=cslice(C_A), func=AF.Exp,
                                                     scale=LG[:, h:h + 1]), [LG, consts], [f1])
            kb.op("act", lambda e, h=h: e.activation(out=f2[:, 0:128], in_=cslice(C_B), func=AF.Exp,
                                                     scale=LG[:, 4 + h:5 + h]), [LG, consts], [f2])
            kb.op("dve", lambda e: e.tensor_tensor(out=f1[:, 0:128], in0=f1[:, 0:128], in1=cslice(C_M1),
                                                   op=ALU.mult), [f1, consts], [f1])
            kb.op("dve", lambda e: e.tensor_tensor(out=f2[:, 0:128], in0=f2[:, 0:128], in1=cslice(C_M2),
                                                   op=ALU.mult), [f2, consts], [f2])
            kb.op("dve", lambda e: e.tensor_tensor(out=f1[:, 0:128], in0=f1[:, 0:128], in1=f2[:, 0:128],
                                                   op=ALU.add), [f1, f2], [f1])
            kb.op("dve", lambda e, h=h: e.tensor_scalar(out=DT[:, h, :], in0=f1[:, 0:128], scalar1=sc_k,
                                                        scalar2=None, op0=ALU.mult), [f1], [DT])
        kb.op("dve", lambda e: e.tensor_scalar(out=KD[:], in0=KD[:], scalar1=sc_k, scalar2=None, op0=ALU.mult),
              [KD], [KD])
        kb.op("act", lambda e: e.activation(out=CD[:], in_=LG[:], func=AF.Exp, scale=128.0), [LG], [CD])

        for col in range(2):
            kb.op("act", lambda e, col=col: e.activation(out=silu_c[:, :, col], in_=vecs[:, V_C + 8 * col:V_C + 8 * col + 8],
                                                         func=AF.Silu), [vecs], [silu_c])
        MODPS = PB[0]
        ring4 = wringF + wringM
        for slab in range(24):
            mwb = ring4[slab % 4]
            mw = mwb[:].bitcast(F32)
            kb.dma("sp" if slab % 2 == 0 else "act", mw,
                   modw_d[:, slab * 256:(slab + 1) * 256].rearrange("(k p) n -> p k n", p=128),
                   [], [mwb], key="modw%d" % (slab % 4))
            for jj in range(2):
                j = slab * 2 + jj
                for k in range(8):
                    kb.op("pe", lambda e, j=j, jj=jj, k=k, mw=mw: e.matmul(
                        psF_t[:, 0, 2 * j:2 * j + 2], lhsT=mw[:, k, jj * 128:(jj + 1) * 128], rhs=silu_c[:, k, :],
                        start=(k == 0), stop=(k == 7)), [mwb, silu_c], [MODPS], inc=(k == 7))
        kb.op("dve", lambda e: e.tensor_tensor(
            out=modv[:], in0=psF_t[:, 0, 0:96].rearrange("p (j c) -> p j c", c=2),
            in1=vecs[:, V_MODB:V_MODB + 48].unsqueeze(2).broadcast_to([128, 48, 2]), op=ALU.add),
            [MODPS, vecs], [modv])
        for (dst, gcol, sc0, col) in ((0, V_GMIX, 8, 0), (2, V_GMIX, 8, 1), (4, V_GMLP, 32, 0)):
            kb.op("dve", lambda e, dst=dst, gcol=gcol, sc0=sc0, col=col: e.scalar_tensor_tensor(
                out=gvec[:, :, dst], in0=modv[:, sc0:sc0 + 8, col], scalar=1.0, in1=vecs[:, gcol:gcol + 8],
                op0=ALU.add, op1=ALU.mult), [modv, vecs], [gvec])
        for (dst, j0, col) in ((1, 0, 0), (3, 0, 1), (5, 24, 0), (6, 16, 0), (7, 40, 0)):
            kb.op("dve", lambda e, dst=dst, j0=j0, col=col: e.tensor_copy(out=gvec[:, :, dst], in_=modv[:, j0:j0 + 8, col]),
                  [modv], [gvec])
        for (dstb, gi, pb) in ((gm_bc, 6, 1), (gf_bc, 7, 3)):
            for k in range(8):
                kb.op("dve", lambda e, k=k, gi=gi: e.tensor_scalar(
                    out=f3[:, k * 128:(k + 1) * 128], in0=cslice(C_ID), scalar1=gvec[:, k, gi:gi + 1], scalar2=None,
                    op0=ALU.mult), [consts, gvec], [f3])
            for k in range(8):
                kb.op("pe", lambda e, k=k, pb=pb: e.matmul(
                    psF_t[:, pb + k // 4, (k % 4) * 128:(k % 4 + 1) * 128], lhsT=onesf[:], rhs=f3[:, k * 128:(k + 1) * 128],
                    start=True, stop=True), [onesf, f3], [PB[pb + k // 4]], inc=True)
            kb.op("act", lambda e, dstb=dstb, pb=pb: e.activation(out=dstb[:].rearrange("p (a n) -> p a n", a=2),
                                                                  in_=bank2(pb), func=AF.Identity),
                  [PB[pb], PB[pb + 1]], [dstb])

        def rstd_pool(ss_ap, n, scale, dst_ap, ssbuf, dstbuf):
            kb.op("pool", lambda e: e.tensor_scalar(out=dst_ap, in0=ss_ap, scalar1=scale, scalar2=EPS,
                                                    op0=ALU.mult, op1=ALU.add), [ssbuf], [dstbuf])
            kb.op("pool", lambda e: e.tensor_tensor(out=dst_ap, in0=dst_ap, in1=negh[:, 0:n], op=ALU.pow),
                  [dstbuf, negh], [dstbuf])

        def norm_part1(xt, xs_b, st_b):
            kb.op("act", lambda e: e.activation(out=xs_b[:], in_=xt[:], func=AF.Square, accum_out=st_b[:, 0:1]),
                  [xt], [xs_b, st_b])
            rstd_pool(st_b[:, 0:1], 1, 1.0 / D, st_b[:, 1:2], st_b, st_b)
            kb.op("dve", lambda e: e.tensor_scalar(out=xs_b[:], in0=xt[:], scalar1=st_b[:, 1:2], scalar2=None, op0=ALU.mult),
                  [xt, st_b], [xs_b])

        def norm_part2(gi, dst_hT, xs_b, tmp_b):
            for k in range(8):
                kb.op("pe", lambda e, k=k: e.transpose(out=psT_t[:, k, :], in_=xs_b[:, k * 128:(k + 1) * 128], identity=identb[:]),
                      [xs_b, identb], [PT], inc=(k == 7))
            kb.op("dve", lambda e: e.tensor_tensor(
                out=tmp_b[:].rearrange("p (k t) -> p k t", k=8), in0=psT_t[:],
                in1=gvec[:, :, gi:gi + 1].broadcast_to([128, 8, 128]), op=ALU.mult), [PT, gvec], [tmp_b])
            kb.op("pool", lambda e: e.tensor_tensor(
                out=dst_hT[:], in0=tmp_b[:].rearrange("p (k t) -> p k t", k=8),
                in1=gvec[:, :, gi + 1:gi + 2].broadcast_to([128, 8, 128]), op=ALU.add), [tmp_b, gvec], [dst_hT])

        def mm8(hTb, wb, pb, out_ap=None):
            for k in range(8):
                kb.op("pe", lambda e, k=k: e.matmul(bank(pb) if out_ap is None else out_ap, lhsT=hTb[:, k, :], rhs=wb[:, k, :],
                                                    start=(k == 0), stop=(k == 7)),
                      [hTb, wb], [PB[pb]], inc=(k == 7))

        class SC:
            pass

        sc0 = SC()
        sc0.f1, sc0.f2, sc0.f3, sc0.st4b = f1, f2, f3, st4b
        sc0.f1b, sc0.f2b, sc0.f3b = [f1], [f2], [f3]
        sc0.st4bv = st4b[:, 0:16]

        def rotate(dst, dst_bufs, src_ap, src_bufs, tab, H, a, sc=None):
            sc = sc or sc0
            f1, f2 = sc.f1, sc.f2
            f1b, f2b = sc.f1b, sc.f2b
            h = 128 // (2 * a)
            t1 = f1[:, 0:H * 128].rearrange("p (H d) -> p H d", H=H)
            t2 = f2[:, 0:H * 128].rearrange("p (H d) -> p H d", H=H)
            kb.op("dve", lambda e: e.tensor_tensor(out=t1, in0=src_ap, in1=tab[:, 0:128].unsqueeze(1).broadcast_to([128, H, 128]),
                                                   op=ALU.mult), src_bufs + [tab], f1b)
            for ai in range(a):
                for two in range(2):
                    o0 = ai * 2 * h + two * h
                    s0 = ai * 2 * h + (1 - two) * h
                    kb.op("dve", lambda e, o0=o0, s0=s0: e.tensor_tensor(
                        out=t2[:, :, o0:o0 + h], in0=src_ap[:, :, s0:s0 + h],
                        in1=tab[:, 128 + o0:128 + o0 + h].unsqueeze(1).broadcast_to([128, H, h]), op=ALU.mult),
                        src_bufs + [tab], f2b)
            kb.op("pool", lambda e: e.tensor_tensor(out=dst, in0=t1, in1=t2, op=ALU.add), f1b + f2b, dst_bufs)

        def head_rms(src_ap, src_bufs, H, gcol, dstf, sc=None):
            sc = sc or sc0
            f1, f1b, f3b = sc.f1, sc.f1b, sc.f3b
            stv, stb = sc.st4bv, sc.st4b
            sq = f1[:, 0:H * 128]
            kb.op("act", lambda e: e.activation(out=sq.rearrange("p (H d) -> p H d", H=H), in_=src_ap, func=AF.Square),
                  src_bufs, f1b)
            kb.op("dve", lambda e: e.tensor_reduce(out=stv[:, 0:H], in_=sq.rearrange("p (H d) -> p H d", H=H),
                                                   axis=AX.X, op=ALU.add), f1b, [stb])
            rstd_pool(stv[:, 0:H], H, 1.0 / 128, stv[:, 0:H], stb, stb)
            kb.op("dve", lambda e: e.tensor_tensor(out=dstf, in0=src_ap,
                                                   in1=stv[:, 0:H].unsqueeze(2).broadcast_to([128, H, 128]), op=ALU.mult),
                  src_bufs + [stb], f3b)
            kb.op("pool", lambda e: e.tensor_tensor(out=dstf, in0=dstf,
                                                    in1=bcv[:, gcol:gcol + 128].unsqueeze(1).broadcast_to([128, H, 128]),
                                                    op=ALU.mult), f3b + [bcv], f3b)

        def transposes(src_ap_fn, src_bufs, n, dst_ap, dst_bufs, eng="dve"):
            for k in range(n):
                kb.op("pe", lambda e, k=k: e.transpose(out=psT_t[:, k, :], in_=src_ap_fn(k), identity=identb[:]),
                      src_bufs + [identb], [PT], inc=(k == n - 1))
            if eng == "act":
                kb.op("act", lambda e: e.activation(out=dst_ap, in_=psT_t[:, 0:n, :], func=AF.Identity), [PT], dst_bufs)
            else:
                kb.op("dve", lambda e: e.tensor_copy(out=dst_ap, in_=psT_t[:, 0:n, :]), [PT], dst_bufs)

        def state_update_half(S, kd, cdoff, pb, hh, vv=None, vb=None, kdv=None):
            vv = v_tok[:] if vv is None else vv
            vb = v_tok if vb is None else vb
            kdv = kd[:] if kdv is None else kdv
            for h in (2 * hh, 2 * hh + 1):
                kb.op("pe", lambda e, h=h: e.matmul(psF_t[:, pb, (h % 2) * 256:(h % 2 + 1) * 256],
                                                    lhsT=kdv[:, h, :], rhs=vv[:, h, :], start=True, stop=True),
                      [kd, vb], [PB[pb]], inc=True)
            for h in (2 * hh, 2 * hh + 1):
                kb.op("dve", lambda e, h=h: e.scalar_tensor_tensor(
                    out=S[:, h, :], in0=S[:, h, :], scalar=CD[:, cdoff + h:cdoff + h + 1],
                    in1=psF_t[:, pb, (h % 2) * 256:(h % 2 + 1) * 256], op0=ALU.mult, op1=ALU.add),
                    [S, CD, PB[pb]], [S])

        def dump(name, buf, shape, dt=F32):
            if name not in debug or kb.dry:
                return
            d = nc.dram_tensor("dbg_" + name, list(shape), dt, kind="ExternalOutput").ap()
            dbg_out[name] = d
            tok = kb.dma("sp", d, buf[:], [buf], [], key="dbg_" + name)
            kb.final_waits[tok[0]] = tok[1]

        WA = {}
        for i, b in enumerate((1, 2, 3, 8)):
            slot = (wringF + wringM)[i]
            kb.dma("sp", slot[:], wsc_d[WT_IN + b], [WSC[WT_IN + b]], [slot], key="wA%d" % i)
            WA[b] = slot

        tilesA = [("c", 0, "f"), ("c", 1, "fb"), ("c", 0, "b")] + [("l", c, "s") for c in range(NCH - 1, -1, -1)]
        if stop_after == "const":
            tilesA = []

        def loadA(ti):
            kind, c, mode = tilesA[ti]
            xt = xring[ti % 2]
            rtb = rtab[ti % 2]
            atb = atab[ti % 2]
            if kind == "c":
                src = ctx_d[c * 128:(c + 1) * 128, :]
                pos0 = c * 128
            else:
                src = x_d[c * 128:(c + 1) * 128, :]
                pos0 = NCTX + c * 128
            kb.dma("sp", xt[:], src, [], [xt], key="x%d" % (ti % 2))
            kb.dma("sp", rtb[:], rt_d[pos0:pos0 + 128, :], [], [rtb], key="rt%d" % (ti % 2))
            if kind == "l":
                kb.dma("sp", atb[:], at_d[c * 128:(c + 1) * 128, :], [], [atb], key="at%d" % (ti % 2))

        sc1 = SC()
        sc1.f1, sc1.f2, sc1.f3, sc1.st4b = m1, m2, xl, bnst
        sc1.f1b, sc1.f2b, sc1.f3b = [m1], [m2], [xl]
        sc1.st4bv = bnst[:].rearrange("p a b -> p (a b)")[:, 0:16]
        setsA = [
            dict(sc=sc0, xs=xs, hT=hT, st=st4, kq=qk_tok[:, 4:8, :], kqb=qk_tok, v=v_tok[:], vb=v_tok,
                 ak=ak_tok[:], akb=ak_tok, kdf=kdf[:], kdfb=kdf, kdb=kdb[:], kdbb=kdb, banks=(0, 1, 2)),
            dict(sc=sc1, xs=xs2, hT=hT2, st=st4m, kq=mrg[:].rearrange("p (h d) -> p h d", h=8)[:, 4:8, :], kqb=mrg,
                 v=hid[:, 0:8, :].rearrange("p (h a) d -> p h (a d)", a=2), vb=hid,
                 ak=oatt_tok[:, 0:2, :], akb=oatt_tok, kdf=mT[:, 0:4, :], kdfb=mT, kdb=oattT[:, 0:4, :], kdbb=oattT,
                 banks=(3, 4, 5)),
        ]
        turn = [0]

        def stage_P(par):
            S = setsA[par]
            sc = S["sc"]
            ba, bb, bc = S["banks"]
            xs_, hT_, st_ = S["xs"], S["hT"], S["st"]
            for ti in range(par, len(tilesA), 2):
                kind, c, mode = tilesA[ti]
                xt = xring[ti % 2]
                rtb = rtab[ti % 2]
                atb = atab[ti % 2]
                sc_idx = (NCH + c) if kind == "c" else c
                if late_precasts and ti >= 1:
                    precast(*late_precasts.pop(0))
                norm_part1(xt, xs_, st_)
                yield
                norm_part2(0 if kind == "l" else 2, hT_, xs_, sc.f3)
                yield
                mm8(hT_, WA[1], ba)
                yield
                rotate(S["kq"], [S["kqb"]], psF_t[:, ba, :].rearrange("p (H d) -> p H d", H=4), [PB[ba]], rtb, 4, 1, sc)
                mm8(hT_, WA[2], bb)
                yield
                kb.op("act", lambda e: e.activation(out=S["v"][:, 0:2, :], in_=psF_t[:, bb, :].rearrange("p (h v) -> p h v", h=2),
                                                    func=AF.Identity), [PB[bb]], [S["vb"]])
                mm8(hT_, WA[3], bc)
                yield
                kb.op("act", lambda e: e.activation(out=S["v"][:, 2:4, :], in_=psF_t[:, bc, :].rearrange("p (h v) -> p h v", h=2),
                                                    func=AF.Identity), [PB[bc]], [S["vb"]])
                do_kv = not (kind == "c" and mode == "b")
                if do_kv:
                    mm8(hT_, WA[8], ba)
                    yield
                    akf = sc.f3[:, 0:256].rearrange("p (H d) -> p H d", H=2)
                    head_rms(psF_t[:, ba, 0:256].rearrange("p (H d) -> p H d", H=2), [PB[ba]], 2, BC_KG, akf, sc)
                    kb.op("act", lambda e, sc_idx=sc_idx: e.activation(
                        out=VA[:, sc_idx, :, 0:128], in_=psF_t[:, ba, 256:512].rearrange("p (a d) -> p a d", a=2),
                        func=AF.Identity), [PB[ba]], [VA])
                    if kind == "l":
                        rotate(S["ak"], [S["akb"]], akf, sc.f3b, atb, 2, 2, sc)
                    else:
                        kb.op("pool", lambda e, akf=akf: e.tensor_copy(out=S["ak"], in_=akf), sc.f3b, [S["akb"]])
                    yield
                    transposes(lambda k: S["ak"][:, k, :], [S["akb"]], 2,
                               KT[:, :, sc_idx * 128:(sc_idx + 1) * 128], [KT])
                    yield
                while turn[0] != ti:
                    yield
                if "f" in mode:
                    kb.op("dve", lambda e: e.tensor_tensor(out=S["kdf"], in0=S["kq"],
                                                           in1=KD[:, 0:4].unsqueeze(2).broadcast_to([128, 4, 128]), op=ALU.mult),
                          [S["kqb"], KD], [S["kdfb"]])
                    state_update_half(Sf, S["kdfb"], 0, 6, 0, S["v"], S["vb"], S["kdf"])
                    state_update_half(Sf, S["kdfb"], 0, 6, 1, S["v"], S["vb"], S["kdf"])
                if "b" in mode or mode == "s":
                    if mode == "s":
                        sr = sbs_ring[c % 2]
                        kb.op("act", lambda e, sr=sr: e.activation(out=sr[:], in_=Sb_[:], func=AF.Identity), [Sb_], [sr])
                        kb.dma("sp", sbs_d[c], sr[:], [sr], [SBSD[c]], key="sbsw%d" % (c % 2))
                    if not (mode == "s" and c == 0):
                        kb.op("dve", lambda e: e.tensor_tensor(out=S["kdb"], in0=S["kq"],
                                                               in1=KD[:, 4:8].unsqueeze(2).broadcast_to([128, 4, 128]), op=ALU.mult),
                              [S["kqb"], KD], [S["kdbb"]])
                        state_update_half(Sb_, S["kdbb"], 4, 6, 0, S["v"], S["vb"], S["kdb"])
                        state_update_half(Sb_, S["kdbb"], 4, 6, 1, S["v"], S["vb"], S["kdb"])
                turn[0] = ti + 1
                if ti + 2 < len(tilesA):
                    loadA(ti + 2)
                yield

        if tilesA:
            loadA(0)
            if len(tilesA) > 1:
                loadA(1)
            gens = [stage_P(0), stage_P(1)]
            alive = [True, True]
            while any(alive):
                for gi_ in range(2):
                    if alive[gi_]:
                        try:
                            next(gens[gi_])
                        except StopIteration:
                            alive[gi_] = False
            for hnd in m1h:
                hnd.w = m1.w
                hnd.r = dict(m1.r)
        while late_precasts:
            precast(*late_precasts.pop(0))
        dump("KT", KT, [128, 2, NSC * 128], BF16)
        dump("Sf", Sf, [128, 4, 256])

        chunksB = list(range(NCH))
        if stop_after in ("const", "A"):
            chunksB = []
        if isinstance(stop_after, tuple) and stop_after[0] == "B":
            chunksB = list(range(stop_after[1]))
        nB = len(chunksB)
        att_scale = 128.0 ** -0.5
        F_BLOCKS = (6, 7, 0, 1, 2, 3, 4, 5, 9, 10, 11, 12)
        planF = []
        planM = []
        for _ in chunksB:
            planF.extend([WT_IN + b for b in F_BLOCKS])
            planM.extend([WT_BR_RET, WT_BR_RET + 1, WT_BR_ATT, WT_BR_ATT + 1, WT_OUT, WT_OUT + 1] +
                         [WT_UP + i for i in range(8)] + [WT_DOWN + i for i in range(8)])
        ring4b = wringF + wringM
        wst = {"order": [], "index": {}, "issued": 0}

        def wtile(which, n):
            if kb.dry:
                wst["index"][(which, n)] = len(wst["order"])
                wst["order"].append((planF if which == "F" else planM)[n])
                return ring4b[0]
            g = wst["index"][(which, n)]
            order = wst["order"]
            nw = len(ring4b)
            while wst["issued"] < min(len(order), g + nw):
                i = wst["issued"]
                t = order[i]
                slot = ring4b[i % nw]
                kb.dma("sp", slot[:], wsc_d[t], [WSC[t]], [slot], key="wr%d" % (i % nw))
                wst["issued"] += 1
            return ring4b[g % nw]

        bstate = {"nb": 0}

        def nbank():
            i = bstate["nb"]
            bstate["nb"] += 1
            return 4 + (i % 3)

        x0 = len(tilesA)

        def load_chunk_inputs(cj):
            cc = chunksB[cj]
            s2 = (x0 + cj) % 2
            kb.dma("sp", xring[s2][:], x_d[cc * 128:(cc + 1) * 128, :], [], [xring[s2]], key="x%d" % s2)
            kb.dma("sp", rtab[s2][:], rt_d[NCTX + cc * 128:NCTX + (cc + 1) * 128, :], [], [rtab[s2]], key="rt%d" % s2)
            kb.dma("sp", atab[s2][:], at_d[cc * 128:(cc + 1) * 128, :], [], [atab[s2]], key="at%d" % s2)
            kb.dma("sp", sbs_ring[cj % 2][:], sbs_d[cc], [SBSD[cc]], [sbs_ring[cj % 2]], key="sbsr%d" % (cj % 2))

        def stage_F(cj):
            c = chunksB[cj]
            par = cj % 2
            s2 = (x0 + cj) % 2
            xt, rtb, atb, sr = xring[s2], rtab[s2], atab[s2], sbs_ring[cj % 2]
            QTd, oretTd, sgrd, sgad = QT[par], oretT[par], sgr[par], sga[par]
            wb0 = cj * 12
            norm_part1(xt, xs, st4)
            yield
            norm_part2(0, hT, xs, f3)
            yield
            for half in range(2):
                pb = nbank()
                mm8(hT, wtile("F", wb0 + half), pb)
                yield
                aqf = f3[:, half * 512:(half + 1) * 512].rearrange("p (H d) -> p H d", H=4)
                head_rms(bank(pb).rearrange("p (H d) -> p H d", H=4), [PB[pb]], 4, BC_QG, aqf)
                rotate(aq_tok[:, half * 4:(half + 1) * 4, :], [aq_tok], aqf, [f3], atb, 4, 2)
                yield
            for half in range(2):
                pb = nbank()
                mm8(hT, wtile("F", wb0 + 2 + half), pb)
                yield
                rotate(qk_tok[:, half * 4:(half + 1) * 4, :], [qk_tok],
                       bank(pb).rearrange("p (H d) -> p H d", H=4), [PB[pb]], rtb, 4, 1)
                if half == 0:
                    transposes(lambda k: aq_tok[:, k, :], [aq_tok], 8, QTd[:], [QTd])
                else:
                    kb.op("pool", lambda e: e.tensor_tensor(out=kdf[:], in0=qk_tok[:, 4:8, :],
                                                            in1=KD[:, 0:4].unsqueeze(2).broadcast_to([128, 4, 128]), op=ALU.mult),
                          [qk_tok, KD], [kdf])
                yield
            for half in range(2):
                pb = nbank()
                mm8(hT, wtile("F", wb0 + 4 + half), pb)
                yield
                kb.op("act", lambda e, half=half, pb=pb: e.activation(
                    out=v_tok[:, 2 * half:2 * half + 2, :].rearrange("p h v -> p (h v)"), in_=bank(pb), func=AF.Identity),
                    [PB[pb]], [v_tok])
                yield
            for k in range(8):
                kb.op("pe", lambda e, k=k: e.transpose(out=psT_t[:, k, :], in_=qk_tok[:, k, :], identity=identb[:]),
                      [qk_tok, identb], [PT], inc=(k == 7))
            kb.op("dve", lambda e: e.tensor_copy(out=qT[:], in_=psT_t[:, 0:4, :]), [PT], [qT])
            kb.op("dve", lambda e: e.tensor_tensor(out=qfT[:], in0=psT_t[:, 0:4, :], in1=QF[:], op=ALU.mult), [PT, QF], [qfT])
            kb.op("dve", lambda e: e.tensor_tensor(out=qbT[:], in0=psT_t[:, 0:4, :], in1=QB[:], op=ALU.mult), [PT, QB], [qbT])
            kb.op("dve", lambda e: e.tensor_copy(out=kT[:], in_=psT_t[:, 4:8, :]), [PT], [kT])
            kb.op("act", lambda e: e.activation(out=Sf_bf[:], in_=Sf[:], func=AF.Identity), [Sf], [Sf_bf])
            yield
            for half in range(2):
                pb = nbank()
                mm8(hT, wtile("F", wb0 + 6 + half), pb)
                yield
                hs = slice(half * 512, (half + 1) * 512)
                kb.op("act", lambda e, hs=hs, pb=pb: e.activation(out=f2[:, hs], in_=bank(pb), func=AF.Tanh, scale=0.5),
                      [PB[pb]], [f2])
                kb.op("dve", lambda e, hs=hs, pb=pb: e.scalar_tensor_tensor(out=sg[:, hs], in0=f2[:, hs], scalar=1.0, in1=bank(pb),
                                                                             op0=ALU.add, op1=ALU.mult), [f2, PB[pb]], [sg])
                yield
            pbs = nbank()
            for h in range(4):
                kb.op("pe", lambda e, h=h: e.matmul(psF_t[:, pbs, h * 128:(h + 1) * 128], lhsT=kT[:, h, :], rhs=qT[:, h, :],
                                                    start=True, stop=True), [kT, qT], [PB[pbs]], inc=(h == 3))
            yield
            kb.op("dve", lambda e: e.tensor_tensor(out=PTm[:], in0=psF_t[:, pbs, :].rearrange("p (h i) -> p h i", h=4),
                                                   in1=DT[:], op=ALU.mult), [PB[pbs], DT], [PTm])
            yield
            for gi_, dstg in ((0, sgrd), (1, sgad)):
                for half in range(2):
                    pb = nbank()
                    mm8(hT, wtile("F", wb0 + 8 + 2 * gi_ + half), pb)
                    yield
                    hs = slice(half * 512, (half + 1) * 512)
                    kb.op("act", lambda e, hs=hs, pb=pb: e.activation(out=f2[:, hs], in_=bank(pb), func=AF.Tanh, scale=0.5),
                          [PB[pb]], [f2])
                    kb.op("pool", lambda e, hs=hs, dstg=dstg: e.tensor_scalar(out=dstg[:, hs], in0=f2[:, hs], scalar1=0.5, scalar2=0.5,
                                                                               op0=ALU.mult, op1=ALU.add), [f2], [dstg])
                    yield
            for hh in range(2):
                pb = nbank()
                for h in (2 * hh, 2 * hh + 1):
                    oap = psF_t[:, pb, (h % 2) * 256:(h % 2 + 1) * 256]
                    kb.op("pe", lambda e, h=h, oap=oap: e.matmul(oap, lhsT=PTm[:, h, :], rhs=v_tok[:, h, :], start=True, stop=False),
                          [PTm, v_tok], [PB[pb]], inc=False)
                    kb.op("pe", lambda e, h=h, oap=oap: e.matmul(oap, lhsT=qfT[:, h, :], rhs=Sf_bf[:, h, :], start=False, stop=False),
                          [qfT, Sf_bf], [PB[pb]], inc=False)
                    kb.op("pe", lambda e, h=h, oap=oap: e.matmul(oap, lhsT=qbT[:, h, :], rhs=sr[:, h, :], start=False, stop=True),
                          [qbT, sr], [PB[pb]], inc=True)
                yield
                for h in (2 * hh, 2 * hh + 1):
                    oap = psF_t[:, pb, (h % 2) * 256:(h % 2 + 1) * 256]
                    kb.op("dve", lambda e, h=h, oap=oap: e.bn_stats(out=bnst[:, h, :], in_=oap), [PB[pb]], [bnst])
                    kb.op("dve", lambda e, h=h: e.bn_aggr(out=bnag[:, h, :], in_=bnst[:, h, :]), [bnst], [bnag])
                rstd_pool(bnag[:, 2 * hh:2 * hh + 2, 1], 2, 1.0, st4b[:, 12 + 2 * hh:14 + 2 * hh], bnag, st4b)
                for h in (2 * hh, 2 * hh + 1):
                    oap = psF_t[:, pb, (h % 2) * 256:(h % 2 + 1) * 256]
                    kb.op("dve", lambda e, h=h, oap=oap: e.tensor_scalar(
                        out=f1[:, h * 256:(h + 1) * 256], in0=oap, scalar1=bnag[:, h, 0:1], scalar2=st4b[:, 12 + h:13 + h],
                        op0=ALU.subtract, op1=ALU.mult), [PB[pb], bnag, st4b], [f1])
                yield
            for hh in range(2):
                pb = nbank()
                state_update_half(Sf, kdf, 0, pb, hh)
                yield
            kb.op("pool", lambda e: e.tensor_tensor(out=f1[:], in0=f1[:], in1=bcv[:, BC_GNW:BC_GNW + 1024], op=ALU.mult),
                  [f1, bcv], [f1])
            kb.op("pool", lambda e: e.tensor_tensor(out=f1[:], in0=f1[:], in1=bcv[:, BC_GNB:BC_GNB + 1024], op=ALU.add),
                  [f1, bcv], [f1])
            kb.op("dve", lambda e: e.scalar_tensor_tensor(out=oret[:], in0=f1[:], scalar=0.5, in1=sg[:], op0=ALU.mult, op1=ALU.mult),
                  [f1, sg], [oret])
            yield
            transposes(lambda k: oret[:, k * 128:(k + 1) * 128], [oret], 8, oretTd[:], [oretTd])
            if cj == 0:
                dump("qk_tok", qk_tok, [128, 8, 128], BF16)
                dump("oret", oret, [128, 1024], BF16)
                dump("QT", QTd, [128, 8, 128], BF16)
            yield

        def stage_A(cj):
            par = cj % 2
            QTd = QT[par]
            iters = [(kvh, sc) for kvh in range(2) for sc in range(NSC)]
            SBANK = (0, 1)
            OBK = (2, 3)

            def emit_S(i):
                kvh, sc = iters[i]
                sbk = SBANK[i % 2]
                qrhs = QTd[:, kvh * 4:(kvh + 1) * 4, :].rearrange("p h t -> p (h t)")
                kb.op("pe", lambda e: e.matmul(bank(sbk), lhsT=KT[:, kvh, sc * 128:(sc + 1) * 128], rhs=qrhs,
                                               start=True, stop=True), [KT, QTd], [PB[sbk]], inc=True)

            def emit_PV(i):
                kvh, sc = iters[i]
                et = ET[i % len(ET)]
                for g in range(4):
                    ob = OBK[g // 2]
                    oap = psF_t[:, ob, (g % 2) * 129:(g % 2) * 129 + 129]
                    kb.op("pe", lambda e, g=g, oap=oap: e.matmul(
                        oap, lhsT=et[:, g * 128:(g + 1) * 128], rhs=VA[:, sc, kvh, :],
                        start=(sc == 0 and g % 2 == 0), stop=(sc == NSC - 1), skip_group_check=True),
                        [et, VA], [PB[ob]], inc=(g == 3))
                if sc == NSC - 1:
                    for g in range(4):
                        ob = OBK[g // 2]
                        off = (g % 2) * 129
                        kb.op("dve", lambda e, g=g, ob=ob, off=off: e.reciprocal(
                            out=rs8[:, kvh * 4 + g:kvh * 4 + g + 1], in_=psF_t[:, ob, off + 128:off + 129]), [PB[ob]], [rs8])
                        kb.op("dve", lambda e, g=g, ob=ob, off=off: e.tensor_scalar(
                            out=oatt_tok[:, kvh * 4 + g, :], in0=psF_t[:, ob, off:off + 128],
                            scalar1=rs8[:, kvh * 4 + g:kvh * 4 + g + 1], scalar2=None, op0=ALU.mult),
                            [PB[ob], rs8], [oatt_tok])

            emit_S(0)
            for i, (kvh, sc) in enumerate(iters):
                if i + 1 < len(iters):
                    emit_S(i + 1)
                sbk = SBANK[i % 2]
                et = ET[i % len(ET)]
                kb.op("act", lambda e, sbk=sbk, et=et: e.activation(out=et[:], in_=bank(sbk), func=AF.Exp, scale=att_scale),
                      [PB[sbk]], [et])
                if i >= 1:
                    emit_PV(i - 1)
                yield
            emit_PV(len(iters) - 1)
            yield

        def stage_M(cj):
            c = chunksB[cj]
            par = cj % 2
            oretTd, sgrd, sgad = oretT[par], sgr[par], sga[par]
            wb0 = cj * 22
            kb.dma("sp", xl[:], x_d[c * 128:(c + 1) * 128, :], [], [xl], key="xl")
            for cb in range(2):
                pb = nbank()
                hs = slice(cb * 512, (cb + 1) * 512)
                mm8(oretTd, wtile("M", wb0 + cb), pb)
                yield
                kb.op("dve", lambda e, hs=hs, pb=pb: e.tensor_tensor(out=m1[:, hs], in0=bank(pb), in1=sgrd[:, hs], op=ALU.mult),
                      [PB[pb], sgrd], [m1h[cb]])
                yield
            transposes(lambda k: oatt_tok[:, k, :], [oatt_tok], 8, oattT[:], [oattT])
            if cj == 0:
                dump("oattT", oattT, [128, 8, 128], BF16)
            yield
            for cb in range(2):
                pb = nbank()
                hs = slice(cb * 512, (cb + 1) * 512)
                mm8(oattT, wtile("M", wb0 + 2 + cb), pb)
                yield
                kb.op("dve", lambda e, hs=hs, pb=pb: e.tensor_tensor(out=m2[:, hs], in0=bank(pb), in1=sgad[:, hs], op=ALU.mult),
                      [PB[pb], sgad], [m2])
                yield
            kb.op("pool", lambda e: e.tensor_tensor(out=mrg[:], in0=m1[:], in1=m2[:], op=ALU.add), [m1h[0], m1h[1], m2], [mrg])
            yield
            transposes(lambda k: mrg[:, k * 128:(k + 1) * 128], [mrg], 8, mT[:], [mT])
            if cj == 0:
                dump("mrg", mrg, [128, 1024], BF16)
            yield
            for cb in range(2):
                pb = nbank()
                hs = slice(cb * 512, (cb + 1) * 512)
                mm8(mT, wtile("M", wb0 + 4 + cb), pb)
                yield
                kb.op("dve", lambda e, hs=hs, pb=pb: e.tensor_tensor(out=m1[:, hs], in0=bank(pb), in1=gm_bc[:, hs], op=ALU.mult),
                      [PB[pb], gm_bc], [m1h[cb]])
                yield
            kb.op("pool", lambda e: e.tensor_tensor(out=xl[:], in0=xl[:], in1=m1[:], op=ALU.add), [xl, m1h[0], m1h[1]], [xl])
            norm_part1(xl, xs2, st4m)
            if cj == 0:
                dump("xl", xl, [128, 1024])
            yield
            norm_part2(4, hT2, xs2, m2)
            yield
            for cb in range(8):
                pb = nbank()
                wb = wtile("M", wb0 + 6 + cb)
                for jj in range(4):
                    for k in range(8):
                        kb.op("pe", lambda e, k=k, jj=jj, wb=wb, pb=pb: e.matmul(
                            psF_t[:, pb, jj * 128:(jj + 1) * 128], lhsT=wb[:, k, jj * 128:(jj + 1) * 128], rhs=hT2[:, k, :],
                            start=(k == 0), stop=(k == 7)), [wb, hT2], [PB[pb]], inc=(k == 7 and jj == 3))
                hv = cb % 2
                hs = slice(hv * 512, (hv + 1) * 512)
                yield
                kb.op("act", lambda e, pb=pb, hs=hs: e.activation(out=m1[:, hs], in_=bank(pb), func=AF.Relu), [PB[pb]], [m1h[hv]])
                kb.op("pool", lambda e, cb=cb, hs=hs: e.tensor_tensor(out=hid[:, cb * 4:(cb + 1) * 4, :].rearrange("p j t -> p (j t)"),
                                                                      in0=m1[:, hs], in1=m1[:, hs], op=ALU.mult), [m1h[hv]], [hid])
                yield
            for cb in range(2):
                hs = slice(cb * 512, (cb + 1) * 512)
                for kg in range(4):
                    pb = nbank()
                    wb = wtile("M", wb0 + 14 + cb * 4 + kg)
                    for k in range(8):
                        j = kg * 8 + k
                        kb.op("pe", lambda e, k=k, j=j, wb=wb, pb=pb: e.matmul(
                            bank(pb), lhsT=hid[:, j, :], rhs=wb[:, k, :], start=(k == 0), stop=(k == 7)),
                            [hid, wb], [PB[pb]], inc=(k == 7))
                    yield
                    if kg == 0:
                        kb.op("dve", lambda e, hs=hs, pb=pb: e.tensor_copy(out=m1[:, hs], in_=bank(pb)), [PB[pb]], [m1h[cb]])
                    else:
                        kb.op("dve", lambda e, hs=hs, pb=pb: e.tensor_tensor(out=m1[:, hs], in0=m1[:, hs], in1=bank(pb), op=ALU.add),
                              [PB[pb], m1h[cb]], [m1h[cb]])
                    yield
                kb.op("pool", lambda e, hs=hs: e.tensor_tensor(out=m1[:, hs], in0=m1[:, hs], in1=gf_bc[:, hs], op=ALU.mult),
                      [m1h[cb], gf_bc], [m1h[cb]])
            kb.op("pool", lambda e: e.tensor_tensor(out=m1[:], in0=m1[:], in1=xl[:], op=ALU.add), [m1h[0], m1h[1], xl], [m1h[0], m1h[1]])
            kb.op("act", lambda e: e.activation(out=xs2[:], in_=m1[:], func=AF.Square, accum_out=st4m[:, 4:5]),
                  [m1h[0], m1h[1]], [xs2, st4m])
            rstd_pool(st4m[:, 4:5], 1, 1.0 / D, st4m[:, 5:6], st4m, st4m)
            kb.op("dve", lambda e: e.scalar_tensor_tensor(out=m2[:], in0=m1[:], scalar=st4m[:, 5:6],
                                                          in1=bcv[:, BC_FIN:BC_FIN + 1024], op0=ALU.mult, op1=ALU.mult),
                  [m1h[0], m1h[1], st4m, bcv], [m2])
            tok = kb.dma("sp", y_d[c * 128:(c + 1) * 128, :], m2[:], [m2], [], key="st0")
            if tok is not None:
                kb.final_waits[tok[0]] = tok[1]
            yield

        def exhaust(g):
            for _ in g:
                pass

        def step(g):
            try:
                next(g)
                return True
            except StopIteration:
                return False

        def phaseB():
            bstate["nb"] = 0
            if nB > 0:
                load_chunk_inputs(0)
                if nB > 1:
                    load_chunk_inputs(1)
                exhaust(stage_F(0))
            RATIO = 1.3
            ucount = [0, 0]
            for cj in range(nB):
                others = []
                if cj >= 1:
                    others.append(stage_M(cj - 1))
                if cj + 1 < nB:
                    if cj + 2 < nB:
                        pass
                    others.append(stage_F(cj + 1))
                ga = stage_A(cj)
                acc = 0.0
                rr = 0
                while step(ga):
                    acc += RATIO
                    while acc >= 1.0 and others:
                        acc -= 1.0
                        g = others[rr % len(others)]
                        if not step(g):
                            others.remove(g)
                        else:
                            rr += 1
                            ucount[0] += 1
                for g in others:
                    for _ in g:
                        ucount[1] += 1
                dbg_out["_units"] = list(ucount)
                if cj + 2 < nB:
                    load_chunk_inputs(cj + 2)
            if nB > 0:
                exhaust(stage_M(nB - 1))


        kb.dry = True
        phaseB()
        kb.dry = False
        phaseB()

        kb.wait_all("sp", list(kb.final_waits.items()))
        kb.run()
    return nc, dbg_out


def prep_inputs(inputs):
    f = np.float32
    x = np.asarray(inputs["x"], f)
    c = np.asarray(inputs["c"], f)
    ctx = np.asarray(inputs["ctx"], f)
    c_ctx = np.asarray(inputs["c_ctx"], f)

    def pl(v):
        return np.asarray(v, f).reshape(-1, 128).T

    rt, at = host_tabs()
    consts = host_consts()
    lam = np.concatenate([np.asarray(inputs["ret_log_lam_fwd"], f)[0], np.asarray(inputs["ret_log_lam_bwd"], f)[0]])
    lam_rep = np.ascontiguousarray(np.broadcast_to(lam[None, :], (128, 8)))
    bc = np.concatenate([np.asarray(inputs["ret_gn_w"], f)[0], np.asarray(inputs["ret_gn_b"], f)[0],
                         np.asarray(inputs["final_norm_g"], f), np.asarray(inputs["att_q_norm_g"], f)[0],
                         np.asarray(inputs["att_k_norm_g"], f)[0]])
    bc_rep = np.ascontiguousarray(np.broadcast_to(bc[None, :], (128, NBC)))
    shared = {
        "lam": lam_rep, "bc": bc_rep, "rt": rt, "at": at, "consts": consts,
        "mod_w": np.ascontiguousarray(np.asarray(inputs["mod_w"], f)[0]),
        "w_in": np.ascontiguousarray(np.asarray(inputs["w_in"], f)[0]),
        "w_br_ret": np.ascontiguousarray(np.asarray(inputs["w_br_ret"], f)[0]),
        "w_br_att": np.ascontiguousarray(np.asarray(inputs["w_br_att"], f)[0]),
        "w_out": np.ascontiguousarray(np.asarray(inputs["w_out"], f)[0]),
        "w_mlp_up": np.ascontiguousarray(np.asarray(inputs["w_mlp_up"], f)[0]),
        "w_mlp_down": np.ascontiguousarray(np.asarray(inputs["w_mlp_down"], f)[0]),
    }
    in_maps = []
    for b in range(8):
        vecs = np.concatenate([pl(c[b]), pl(c_ctx), pl(np.asarray(inputs["mod_b"], f)[0]),
                               pl(np.asarray(inputs["norm_mix_g"], f)[0]), pl(np.asarray(inputs["norm_mlp_g"], f)[0])], axis=1)
        m = dict(shared)
        m["x"] = np.ascontiguousarray(x[b])
        m["ctx"] = np.ascontiguousarray(ctx[b])
        m["vecs"] = np.ascontiguousarray(vecs.astype(f))
        in_maps.append(m)
    return in_maps


_CACHE = {}


def kernel(**inputs):
    in_maps = prep_inputs(inputs)
    if "nc" not in _CACHE:
        _CACHE["nc"] = build()[0]
    nc = _CACHE["nc"]
    res = run_bass_kernel_spmd(nc, in_maps, core_ids=list(range(8)))
    out = np.stack([np.asarray(r["y"], np.float32) for r in res.results], axis=0)
    return out
```

```python
import numpy as np
from contextlib import ExitStack
import concourse.bass as bass
import concourse.mybir as mybir
from concourse.bass_utils import run_bass_kernel_spmd

F32 = mybir.dt.float32
BF16 = mybir.dt.bfloat16
AF = mybir.ActivationFunctionType
ALU = mybir.AluOpType
AX = mybir.AxisListType

D = 1024
NTOK = 4096
NCTX = 256
NCH = NTOK // 128
NSC = NCH + NCTX // 128
DIN = 6656
DFF = 4096
EPS = 1e-6
NWT = 35
WT_IN = 0
WT_BR_RET = 13
WT_BR_ATT = 15
WT_OUT = 17
WT_UP = 19
WT_DOWN = 27


class Buf:
    def __init__(self, name, t):
        self.name = name
        self.t = t
        self.w = None
        self.r = {}

    def __getitem__(self, idx):
        return self.t[idx]


class KB:
    ENG = ("pe", "act", "dve", "pool", "sp")

    def __init__(self, nc, stack):
        self.nc = nc
        self.stack = stack
        self.q = {e: [] for e in self.ENG}
        self.cnt = {e: 0 for e in self.ENG}
        self.waited = {e: {} for e in self.ENG}
        self.sems = {}
        self.dcnt = {}
        for e in self.ENG:
            self.sems[e] = stack.enter_context(nc.semaphore("s_" + e))
        self.final_waits = {}

    def sem(self, key):
        if key not in self.sems:
            self.sems[key] = self.stack.enter_context(self.nc.semaphore("s_" + key))
            self.dcnt[key] = 0
        return self.sems[key]

    def sb(self, name, shape, dt):
        return Buf(name, self.stack.enter_context(self.nc.sbuf_tensor("sb_" + name, list(shape), dt)))

    def _deps(self, eng, reads, writes):
        deps = {}

        def add(k, v):
            if deps.get(k, 0) < v:
                deps[k] = v

        for b in reads:
            if b.w is not None:
                add(*b.w)
        for b in writes:
            if b.w is not None:
                add(*b.w)
            for k, v in b.r.items():
                add(k, v)
        waits = []
        for k, v in deps.items():
            if k == eng:
                if eng in ("pe", "sp"):
                    continue
                if v > self.cnt[eng]:
                    continue
            if self.waited[eng].get(k, 0) >= v:
                continue
            self.waited[eng][k] = v
            waits.append((k, v))
        return waits

    def _mark(self, tok, reads, writes):
        for b in writes:
            b.w = tok
            b.r = {}
        for b in reads:
            if b in writes:
                continue
            k, v = tok
            if b.r.get(k, 0) < v:
                b.r[k] = v

    dry = False

    def op(self, eng, fn, reads=(), writes=(), inc=True):
        if self.dry:
            return None
        pr = [b for b in reads if getattr(b, "psum", False)]
        if pr:
            reads = [b for b in reads if not getattr(b, "psum", False)]
            writes = list(writes) + [b for b in pr if b not in writes]
        waits = self._deps(eng, reads, writes)
        if inc:
            self.cnt[eng] += 1
            tok = (eng, self.cnt[eng])
        else:
            tok = (eng, self.cnt[eng] + 1)
        sems = self.sems

        def thunk(e, waits=waits, fn=fn, inc=inc, eng=eng):
            for k, v in waits:
                e.wait_ge(sems[k], v)
            ins = fn(e)
            if inc:
                ins.then_inc(sems[eng], 1)

        self.q[eng].append(thunk)
        self._mark(tok, reads, writes)
        return tok

    def dma(self, eng, out, in_, reads, writes, key, group_total=None, **kw):
        if self.dry:
            return None
        s = self.sem(key)
        waits = self._deps(eng, reads, writes)
        self.dcnt[key] += 1
        val = 16 * (group_total if group_total is not None else self.dcnt[key])
        tok = (key, val)
        sems = self.sems

        def thunk(e, waits=waits):
            for k, v in waits:
                e.wait_ge(sems[k], v)
            e.dma_start(out=out, in_=in_, **kw).then_inc(s, 16)

        self.q[eng].append(thunk)
        self._mark(tok, reads, writes)
        return tok

    def wait_all(self, eng, toks):
        sems = self.sems

        def thunk(e):
            for k, v in toks:
                e.wait_ge(sems[k], v)

        self.q[eng].append(thunk)

    def run(self):
        nc = self.nc
        with nc.Block() as block:
            @block.tensor
            def _(e):
                for f in self.q["pe"]:
                    f(e)

            @block.scalar
            def _(e):
                for f in self.q["act"]:
                    f(e)

            @block.vector
            def _(e):
                for f in self.q["dve"]:
                    f(e)

            @block.gpsimd
            def _(e):
                for f in self.q["pool"]:
                    f(e)

            @block.sync
            def _(e):
                for f in self.q["sp"]:
                    f(e)


def host_consts():
    i = np.arange(128, dtype=np.float32)
    ident = np.eye(128, dtype=np.float32)
    iota1 = np.tile((i + 1)[None, :], (128, 1))
    iota2 = np.tile((128 - i)[None, :], (128, 1))
    jj = i[:, None]
    ii = i[None, :]
    A = np.maximum(ii - jj, 0)
    B = np.maximum(jj - ii, 0)
    M1 = (ii >= jj).astype(np.float32)
    M2 = (jj > ii).astype(np.float32)
    p127 = (127 - i)[:, None]
    pj = i[:, None]
    return np.ascontiguousarray(
        np.concatenate([ident, iota1, iota2, A, B, M1, M2, p127, pj], axis=1).astype(np.float32))


def host_tabs():
    f32 = np.float32
    half = 64
    inv = (f32(10000.0) ** (-(np.arange(half, dtype=f32) / f32(half)))).astype(f32)
    pos = np.arange(NCTX + NTOK, dtype=f32)
    ang = (pos[:, None] * inv[None, :]).astype(f32)
    c, s = np.cos(ang).astype(f32), np.sin(ang).astype(f32)
    rt = np.concatenate([c, c, -s, s], axis=1).astype(f32)
    h2 = 32
    inv2 = (f32(10000.0) ** (-(np.arange(h2, dtype=f32) / f32(h2)))).astype(f32)
    t = np.arange(NTOK)
    row = (t // 64).astype(f32)
    col = (t % 64).astype(f32)
    ar = (row[:, None] * inv2[None, :]).astype(f32)
    ac = (col[:, None] * inv2[None, :]).astype(f32)
    cr, sr, cc, sc = np.cos(ar), np.sin(ar), np.cos(ac), np.sin(ac)
    at = np.concatenate([cr, cr, cc, cc, -sr, sr, -sc, sc], axis=1).astype(f32)
    return np.ascontiguousarray(rt), np.ascontiguousarray(at)


C_ID, C_I1, C_I2, C_A, C_B, C_M1, C_M2 = [k * 128 for k in range(7)]
C_P127 = 7 * 128
C_PJ = 7 * 128 + 1
NCONST = 7 * 128 + 2
V_C, V_CC, V_MODB, V_GMIX, V_GMLP = 0, 8, 16, 64, 72
NVEC = 80
BC_GNW, BC_GNB, BC_FIN, BC_QG, BC_KG = 0, 1024, 2048, 3072, 3200
NBC = 3328


def build(debug=None, stop_after=None):
    debug = debug or []
    nc = bass.Bass("TRN2", target_bir_lowering=False)

    def din(name, shape, dt=F32):
        return nc.dram_tensor(name, list(shape), dt, kind="ExternalInput").ap()

    x_d = din("x", [NTOK, D])
    ctx_d = din("ctx", [NCTX, D])
    vecs_d = din("vecs", [128, NVEC])
    lam_d = din("lam", [128, 8])
    bc_d = din("bc", [128, NBC])
    rt_d = din("rt", [NCTX + NTOK, 256])
    at_d = din("at", [NTOK, 256])
    consts_d = din("consts", [128, NCONST])
    modw_d = din("mod_w", [D, 6 * D])
    win_d = din("w_in", [D, DIN])
    wbr_ret_d = din("w_br_ret", [D, D])
    wbr_att_d = din("w_br_att", [D, D])
    wout_d = din("w_out", [D, D])
    wup_d = din("w_mlp_up", [D, DFF])
    wdown_d = din("w_mlp_down", [DFF, D])
    y_d = nc.dram_tensor("y", [NTOK, D], F32, kind="ExternalOutput").ap()
    wsc_d = nc.dram_tensor("wsc", [NWT, 128, 8, 512], BF16, kind="Internal").ap()
    sbs_d = nc.dram_tensor("sbs_scr", [NCH, 128, 4, 256], BF16, kind="Internal").ap()
    dbg_out = {}

    stack = ExitStack()
    with stack:
        kb = KB(nc, stack)
        sb = kb.sb
        psF_t = stack.enter_context(nc.psum_tensor("psF", [128, 7, 512], F32))
        psT_t = stack.enter_context(nc.psum_tensor("psT", [128, 8, 128], BF16))
        PB = [Buf("psb%d" % i, None) for i in range(7)]
        PT = Buf("psT", psT_t)
        for _b in PB + [PT]:
            _b.psum = True

        def bank(i):
            return psF_t[:, i, :]

        def bank2(i):
            return psF_t[:, i:i + 2, :]

        consts = sb("consts", [128, NCONST], F32)
        vecs = sb("vecs", [128, NVEC], F32)
        lam = sb("lam", [128, 8], F32)
        bcv = sb("bcv", [128, NBC], F32)
        identb = sb("identb", [128, 128], BF16)
        onesf = sb("onesf", [128, 128], F32)
        negh = sb("negh", [128, 16], F32)
        modv = sb("modv", [128, 48, 2], F32)
        silu_c = sb("silu_c", [128, 8, 2], F32)
        gvec = sb("gvec", [128, 8, 8], F32)
        gm_bc = sb("gm_bc", [128, 1024], F32)
        gf_bc = sb("gf_bc", [128, 1024], F32)
        LG = sb("LG", [128, 8], F32)
        DT = sb("DT", [128, 4, 128], F32)
        QF = sb("QF", [128, 4, 128], F32)
        QB = sb("QB", [128, 4, 128], F32)
        KD = sb("KD", [128, 8], F32)
        CD = sb("CD", [128, 8], F32)
        KT = sb("KT", [128, 2, NSC * 128], BF16)
        VA = sb("VA", [128, NSC, 2, 129], BF16)
        SBSD = [Buf("sbsd%d" % i, None) for i in range(NCH)]
        sbs_ring = [sb("sbsr%d" % i, [128, 4, 256], BF16) for i in range(2)]
        Sf = sb("Sf", [128, 4, 256], F32)
        Sb_ = sb("Sb", [128, 4, 256], F32)
        Sf_bf = sb("Sf_bf", [128, 4, 256], BF16)
        wringF = [sb("wF%d" % i, [128, 8, 512], BF16) for i in range(2)]
        wringM = [sb("wM%d" % i, [128, 8, 512], BF16) for i in range(2)]
        xring = [sb("x%d" % i, [128, 1024], F32) for i in range(2)]
        rtab = [sb("rtab%d" % i, [128, 256], F32) for i in range(2)]
        atab = [sb("atab%d" % i, [128, 256], F32) for i in range(2)]
        st4 = sb("st4", [128, 16], F32)
        st4b = sb("st4b", [128, 16], F32)
        xs = sb("xs", [128, 1024], BF16)
        hT = sb("hT", [128, 8, 128], BF16)
        f1 = sb("f1", [128, 1024], F32)
        f2 = sb("f2", [128, 1024], F32)
        f3 = sb("f3", [128, 1024], F32)
        qk_tok = sb("qk_tok", [128, 8, 128], BF16)
        qT = sb("qT", [128, 4, 128], BF16)
        qfT = sb("qfT", [128, 4, 128], BF16)
        qbT = sb("qbT", [128, 4, 128], BF16)
        kT = sb("kT", [128, 4, 128], BF16)
        kdf = sb("kdf", [128, 4, 128], BF16)
        v_tok = sb("v_tok", [128, 4, 256], BF16)
        sg = sb("sg", [128, 1024], BF16)
        aq_tok = sb("aq_tok", [128, 8, 128], BF16)
        ak_tok = sb("ak_tok", [128, 2, 128], BF16)
        PTm = sb("PTm", [128, 4, 128], BF16)
        kdb = PTm
        oret = sb("oret", [128, 1024], BF16)
        bnst = sb("bnst", [128, 4, 6], F32)
        bnag = sb("bnag", [128, 4, 2], F32)
        QT = [sb("QT%d" % i, [128, 8, 128], BF16) for i in range(2)]
        oretT = [sb("oretT%d" % i, [128, 8, 128], BF16) for i in range(2)]
        sgr = [sb("sgr%d" % i, [128, 1024], BF16) for i in range(2)]
        sga = [sb("sga%d" % i, [128, 1024], BF16) for i in range(2)]
        ET = [sb("ET%d" % i, [128, 512], BF16) for i in range(2)]
        for _i in range(2):
            _b = Buf("ETc%d" % _i, None)
            _b.t = consts[:, _i * 256:(_i + 1) * 256].bitcast(BF16)
            ET.append(_b)
        oatt_tok = sb("oatt_tok", [128, 8, 128], BF16)
        oattT = sb("oattT", [128, 8, 128], BF16)
        rs8 = sb("rs8", [128, 8], F32)
        m1 = sb("m1", [128, 1024], F32)
        m1h = [Buf("m1a", None), Buf("m1b", None)]
        m2 = sb("m2", [128, 1024], F32)
        xl = sb("xl", [128, 1024], F32)
        mrg = sb("mrg", [128, 1024], BF16)
        mT = sb("mT", [128, 8, 128], BF16)
        xs2 = sb("xs2", [128, 1024], BF16)
        hT2 = sb("hT2", [128, 8, 128], BF16)
        hid = sb("hid", [128, 32, 128], BF16)
        st4m = sb("st4m", [128, 16], F32)

        def cslice(off, n=128):
            return consts[:, off:off + n]

        WSC = [Buf("wsc%d" % i, None) for i in range(NWT)]

        def precast(tile, src, group, total):
            kb.dma("pool", wsc_d[tile], src.rearrange("(k p) n -> p k n", p=128), [], [WSC[tile]],
                   key="pc_" + group, group_total=total)

        for b in (1, 2, 3, 8):
            precast(WT_IN + b, win_d[:, b * 512:(b + 1) * 512], "a", 4)
        late_precasts = []
        for b in (6, 7, 0, 4, 5, 9, 10, 11, 12):
            late_precasts.append((WT_IN + b, win_d[:, b * 512:(b + 1) * 512], "b", 9))
        for cb in range(2):
            late_precasts.append((WT_BR_RET + cb, wbr_ret_d[:, cb * 512:(cb + 1) * 512], "c", 6))
        for cb in range(2):
            late_precasts.append((WT_BR_ATT + cb, wbr_att_d[:, cb * 512:(cb + 1) * 512], "c", 6))
        for cb in range(2):
            late_precasts.append((WT_OUT + cb, wout_d[:, cb * 512:(cb + 1) * 512], "c", 6))
        for cb in range(8):
            late_precasts.append((WT_UP + cb, wup_d[:, cb * 512:(cb + 1) * 512], "d", 8))
        for cb in range(2):
            for kg in range(4):
                late_precasts.append((WT_DOWN + cb * 4 + kg,
                                      wdown_d[kg * 1024:(kg + 1) * 1024, cb * 512:(cb + 1) * 512], "e", 8))

        kb.dma("sp", consts[:], consts_d[:, :], [], [consts], key="c0")
        kb.dma("sp", vecs[:], vecs_d[:, :], [], [vecs], key="c1")
        kb.dma("sp", lam[:], lam_d[:, :], [], [lam], key="c2")
        kb.dma("sp", bcv[:], bc_d[:, :], [], [bcv], key="c3")
        kb.op("dve", lambda e: e.tensor_copy(out=identb[:], in_=cslice(C_ID)), [consts], [identb])
        kb.op("pool", lambda e: e.memset(onesf[:], 1.0), [], [onesf])
        kb.op("pool", lambda e: e.memset(negh[:], -0.5), [], [negh])
        kb.op("pool", lambda e: e.memset(Sf[:], 0.0), [], [Sf])
        kb.op("pool", lambda e: e.memset(Sb_[:], 0.0), [], [Sb_])
        kb.op("pool", lambda e: e.memset(VA[:, :, :, 128:129], 1.0), [], [VA])

        kb.op("act", lambda e: e.activation(out=LG[:], in_=lam[:], func=AF.Exp), [lam], [LG])
        kb.op("dve", lambda e: e.tensor_scalar(out=LG[:], in0=LG[:], scalar1=-1.0, scalar2=None, op0=ALU.mult),
              [LG], [LG])
        sc_k = 128.0 ** -0.5
        for h in range(4):
            kb.op("act", lambda e, h=h: e.activation(out=QF[:, h, :], in_=cslice(C_I1), func=AF.Exp,
                                                     scale=LG[:, h:h + 1]), [LG, consts], [QF])
            kb.op("act", lambda e, h=h: e.activation(out=QB[:, h, :], in_=cslice(C_I2), func=AF.Exp,
                                                     scale=LG[:, 4 + h:5 + h]), [LG, consts], [QB])
            kb.op("act", lambda e, h=h: e.activation(out=KD[:, h:h + 1], in_=consts[:, C_P127:C_P127 + 1],
                                                     func=AF.Exp, scale=LG[:, h:h + 1]), [LG, consts], [KD])
            kb.op("act", lambda e, h=h: e.activation(out=KD[:, 4 + h:5 + h], in_=consts[:, C_PJ:C_PJ + 1],
                                                     func=AF.Exp, scale=LG[:, 4 + h:5 + h]), [LG, consts], [KD])
            kb.op("act", lambda e, h=h: e.activation(out=f1[:, 0:128], in_=cslice(C_A), func=AF.Exp,
                                                     scale=LG[:, h:h + 1]), [LG, consts], [f1])
            kb.op("act", lambda e, h=h: e.activation(out=f2[:, 0:128], in_=cslice(C_B), func=AF.Exp,
                                                     scale=LG[:, 4 + h:5 + h]), [LG, consts], [f2])
            kb.op("dve", lambda e: e.tensor_tensor(out=f1[:, 0:128], in0=f1[:, 0:128], in1=cslice(C_M1),
                                                   op=ALU.mult), [f1, consts], [f1])
            kb.op("dve", lambda e: e.tensor_tensor(out=f2[:, 0:128], in0=f2[:, 0:128], in1=cslice(C_M2),
                                                   op=ALU.mult), [f2, consts], [f2])
            kb.op("dve", lambda e: e.tensor_tensor(out=f1[:, 0:128], in0=f1[:, 0:128], in1=f2[:, 0:128],
                                                   op=ALU.add), [f1, f2], [f1])
            kb.op("dve", lambda e, h=h: e.tensor_scalar(out=DT[:, h, :], in0=f1[:, 0:128], scalar1=sc_k,
                                                        scalar2=None, op0=ALU.mult), [f1], [DT])
        kb.op("dve", lambda e: e.tensor_scalar(out=KD[:], in0=KD[:], scalar1=sc_k, scalar2=None, op0=ALU.mult),
              [KD], [KD])
        kb.op("act", lambda e: e.activation(out=CD[:], in_=LG[:], func=AF.Exp, scale=128.0), [LG], [CD])

        for col in range(2):
            kb.op("act", lambda e, col=col: e.activation(out=silu_c[:, :, col], in_=vecs[:, V_C + 8 * col:V_C + 8 * col + 8],
                                                         func=AF.Silu), [vecs], [silu_c])
        MODPS = PB[0]
        ring4 = wringF + wringM
        for slab in range(24):
            mwb = ring4[slab % 4]
            mw = mwb[:].bitcast(F32)
            kb.dma("sp" if slab % 2 == 0 else "act", mw,
                   modw_d[:, slab * 256:(slab + 1) * 256].rearrange("(k p) n -> p k n", p=128),
                   [], [mwb], key="modw%d" % (slab % 4))
            for jj in range(2):
                j = slab * 2 + jj
                for k in range(8):
                    kb.op("pe", lambda e, j=j, jj=jj, k=k, mw=mw: e.matmul(
                        psF_t[:, 0, 2 * j:2 * j + 2], lhsT=mw[:, k, jj * 128:(jj + 1) * 128], rhs=silu_c[:, k, :],
                        start=(k == 0), stop=(k == 7)), [mwb, silu_c], [MODPS], inc=(k == 7))
        kb.op("dve", lambda e: e.tensor_tensor(
            out=modv[:], in0=psF_t[:, 0, 0:96].rearrange("p (j c) -> p j c", c=2),
            in1=vecs[:, V_MODB:V_MODB + 48].unsqueeze(2).broadcast_to([128, 48, 2]), op=ALU.add),
            [MODPS, vecs], [modv])
        for (dst, gcol, sc0, col) in ((0, V_GMIX, 8, 0), (2, V_GMIX, 8, 1), (4, V_GMLP, 32, 0)):
            kb.op("dve", lambda e, dst=dst, gcol=gcol, sc0=sc0, col=col: e.scalar_tensor_tensor(
                out=gvec[:, :, dst], in0=modv[:, sc0:sc0 + 8, col], scalar=1.0, in1=vecs[:, gcol:gcol + 8],
                op0=ALU.add, op1=ALU.mult), [modv, vecs], [gvec])
        for (dst, j0, col) in ((1, 0, 0), (3, 0, 1), (5, 24, 0), (6, 16, 0), (7, 40, 0)):
            kb.op("dve", lambda e, dst=dst, j0=j0, col=col: e.tensor_copy(out=gvec[:, :, dst], in_=modv[:, j0:j0 + 8, col]),
                  [modv], [gvec])
        for (dstb, gi, pb) in ((gm_bc, 6, 1), (gf_bc, 7, 3)):
            for k in range(8):
                kb.op("dve", lambda e, k=k, gi=gi: e.tensor_scalar(
                    out=f3[:, k * 128:(k + 1) * 128], in0=cslice(C_ID), scalar1=gvec[:, k, gi:gi + 1], scalar2=None,
                    op0=ALU.mult), [consts, gvec], [f3])
            for k in range(8):
                kb.op("pe", lambda e, k=k, pb=pb: e.matmul(
                    psF_t[:, pb + k // 4, (k % 4) * 128:(k % 4 + 1) * 128], lhsT=onesf[:], rhs=f3[:, k * 128:(k + 1) * 128],
                    start=True, stop=True), [onesf, f3], [PB[pb + k // 4]], inc=True)
            kb.op("act", lambda e, dstb=dstb, pb=pb: e.activation(out=dstb[:].rearrange("p (a n) -> p a n", a=2),
                                                                  in_=bank2(pb), func=AF.Identity),
                  [PB[pb], PB[pb + 1]], [dstb])

        def rstd_pool(ss_ap, n, scale, dst_ap, ssbuf, dstbuf):
            kb.op("pool", lambda e: e.tensor_scalar(out=dst_ap, in0=ss_ap, scalar1=scale, scalar2=EPS,
                                                    op0=ALU.mult, op1=ALU.add), [ssbuf], [dstbuf])
            kb.op("pool", lambda e: e.tensor_tensor(out=dst_ap, in0=dst_ap, in1=negh[:, 0:n], op=ALU.pow),
                  [dstbuf, negh], [dstbuf])

        def norm_part1(xt, xs_b, st_b):
            kb.op("act", lambda e: e.activation(out=xs_b[:], in_=xt[:], func=AF.Square, accum_out=st_b[:, 0:1]),
                  [xt], [xs_b, st_b])
            rstd_pool(st_b[:, 0:1], 1, 1.0 / D, st_b[:, 1:2], st_b, st_b)
            kb.op("dve", lambda e: e.tensor_scalar(out=xs_b[:], in0=xt[:], scalar1=st_b[:, 1:2], scalar2=None, op0=ALU.mult),
                  [xt, st_b], [xs_b])

        def norm_part2(gi, dst_hT, xs_b, tmp_b):
            for k in range(8):
                kb.op("pe", lambda e, k=k: e.transpose(out=psT_t[:, k, :], in_=xs_b[:, k * 128:(k + 1) * 128], identity=identb[:]),
                      [xs_b, identb], [PT], inc=(k == 7))
            kb.op("dve", lambda e: e.tensor_tensor(
                out=tmp_b[:].rearrange("p (k t) -> p k t", k=8), in0=psT_t[:],
                in1=gvec[:, :, gi:gi + 1].broadcast_to([128, 8, 128]), op=ALU.mult), [PT, gvec], [tmp_b])
            kb.op("pool", lambda e: e.tensor_tensor(
                out=dst_hT[:], in0=tmp_b[:].rearrange("p (k t) -> p k t", k=8),
                in1=gvec[:, :, gi + 1:gi + 2].broadcast_to([128, 8, 128]), op=ALU.add), [tmp_b, gvec], [dst_hT])

        def mm8(hTb, wb, pb, out_ap=None):
            for k in range(8):
                kb.op("pe", lambda e, k=k: e.matmul(bank(pb) if out_ap is None else out_ap, lhsT=hTb[:, k, :], rhs=wb[:, k, :],
                                                    start=(k == 0), stop=(k == 7)),
                      [hTb, wb], [PB[pb]], inc=(k == 7))

        class SC:
            pass

        sc0 = SC()
        sc0.f1, sc0.f2, sc0.f3, sc0.st4b = f1, f2, f3, st4b
        sc0.f1b, sc0.f2b, sc0.f3b = [f1], [f2], [f3]
        sc0.st4bv = st4b[:, 0:16]

        def rotate(dst, dst_bufs, src_ap, src_bufs, tab, H, a, sc=None):
            sc = sc or sc0
            f1, f2 = sc.f1, sc.f2
            f1b, f2b = sc.f1b, sc.f2b
            h = 128 // (2 * a)
            t1 = f1[:, 0:H * 128].rearrange("p (H d) -> p H d", H=H)
            t2 = f2[:, 0:H * 128].rearrange("p (H d) -> p H d", H=H)
            kb.op("dve", lambda e: e.tensor_tensor(out=t1, in0=src_ap, in1=tab[:, 0:128].unsqueeze(1).broadcast_to([128, H, 128]),
                                                   op=ALU.mult), src_bufs + [tab], f1b)
            for ai in range(a):
                for two in range(2):
                    o0 = ai * 2 * h + two * h
                    s0 = ai * 2 * h + (1 - two) * h
                    kb.op("dve", lambda e, o0=o0, s0=s0: e.tensor_tensor(
                        out=t2[:, :, o0:o0 + h], in0=src_ap[:, :, s0:s0 + h],
                        in1=tab[:, 128 + o0:128 + o0 + h].unsqueeze(1).broadcast_to([128, H, h]), op=ALU.mult),
                        src_bufs + [tab], f2b)
            kb.op("pool", lambda e: e.tensor_tensor(out=dst, in0=t1, in1=t2, op=ALU.add), f1b + f2b, dst_bufs)

        def head_rms(src_ap, src_bufs, H, gcol, dstf, sc=None):
            sc = sc or sc0
            f1, f1b, f3b = sc.f1, sc.f1b, sc.f3b
            stv, stb = sc.st4bv, sc.st4b
            sq = f1[:, 0:H * 128]
            kb.op("act", lambda e: e.activation(out=sq.rearrange("p (H d) -> p H d", H=H), in_=src_ap, func=AF.Square),
                  src_bufs, f1b)
            kb.op("dve", lambda e: e.tensor_reduce(out=stv[:, 0:H], in_=sq.rearrange("p (H d) -> p H d", H=H),
                                                   axis=AX.X, op=ALU.add), f1b, [stb])
            rstd_pool(stv[:, 0:H], H, 1.0 / 128, stv[:, 0:H], stb, stb)
            kb.op("dve", lambda e: e.tensor_tensor(out=dstf, in0=src_ap,
                                                   in1=stv[:, 0:H].unsqueeze(2).broadcast_to([128, H, 128]), op=ALU.mult),
                  src_bufs + [stb], f3b)
            kb.op("pool", lambda e: e.tensor_tensor(out=dstf, in0=dstf,
                                                    in1=bcv[:, gcol:gcol + 128].unsqueeze(1).broadcast_to([128, H, 128]),
                                                    op=ALU.mult), f3b + [bcv], f3b)

        def transposes(src_ap_fn, src_bufs, n, dst_ap, dst_bufs, eng="dve"):
            for k in range(n):
                kb.op("pe", lambda e, k=k: e.transpose(out=psT_t[:, k, :], in_=src_ap_fn(k), identity=identb[:]),
                      src_bufs + [identb], [PT], inc=(k == n - 1))
            if eng == "act":
                kb.op("act", lambda e: e.activation(out=dst_ap, in_=psT_t[:, 0:n, :], func=AF.Identity), [PT], dst_bufs)
            else:
                kb.op("dve", lambda e: e.tensor_copy(out=dst_ap, in_=psT_t[:, 0:n, :]), [PT], dst_bufs)

        def state_update_half(S, kd, cdoff, pb, hh, vv=None, vb=None, kdv=None, part=None):
            vv = v_tok[:] if vv is None else vv
            vb = v_tok if vb is None else vb
            kdv = kd[:] if kdv is None else kdv
            for h in (2 * hh, 2 * hh + 1):
                if part == 1:
                    break
                kb.op("pe", lambda e, h=h: e.matmul(psF_t[:, pb, (h % 2) * 256:(h % 2 + 1) * 256],
                                                    lhsT=kdv[:, h, :], rhs=vv[:, h, :], start=True, stop=True),
                      [kd, vb], [PB[pb]], inc=True)
            for h in (2 * hh, 2 * hh + 1):
                if part == 0:
                    break
                kb.op("dve", lambda e, h=h: e.scalar_tensor_tensor(
                    out=S[:, h, :], in0=S[:, h, :], scalar=CD[:, cdoff + h:cdoff + h + 1],
                    in1=psF_t[:, pb, (h % 2) * 256:(h % 2 + 1) * 256], op0=ALU.mult, op1=ALU.add),
                    [S, CD, PB[pb]], [S])

        def dump(name, buf, shape, dt=F32):
            if name not in debug or kb.dry:
                return
            d = nc.dram_tensor("dbg_" + name, list(shape), dt, kind="ExternalOutput").ap()
            dbg_out[name] = d
            tok = kb.dma("sp", d, buf[:], [buf], [], key="dbg_" + name)
            kb.final_waits[tok[0]] = tok[1]

        WA = {}
        for i, b in enumerate((1, 2, 3, 8)):
            slot = (wringF + wringM)[i]
            kb.dma("sp", slot[:], wsc_d[WT_IN + b], [WSC[WT_IN + b]], [slot], key="wA%d" % i)
            WA[b] = slot

        tilesA = [("c", 0, "f"), ("c", 1, "fb"), ("c", 0, "b")] + [("l", c, "s") for c in range(NCH - 1, -1, -1)]
        if stop_after == "const":
            tilesA = []

        def loadA(ti):
            kind, c, mode = tilesA[ti]
            xt = xring[ti % 2]
            rtb = rtab[ti % 2]
            atb = atab[ti % 2]
            if kind == "c":
                src = ctx_d[c * 128:(c + 1) * 128, :]
                pos0 = c * 128
            else:
                src = x_d[c * 128:(c + 1) * 128, :]
                pos0 = NCTX + c * 128
            kb.dma("sp", xt[:], src, [], [xt], key="x%d" % (ti % 2))
            kb.dma("sp", rtb[:], rt_d[pos0:pos0 + 128, :], [], [rtb], key="rt%d" % (ti % 2))
            if kind == "l":
                kb.dma("sp", atb[:], at_d[c * 128:(c + 1) * 128, :], [], [atb], key="at%d" % (ti % 2))

        sc1 = SC()
        sc1.f1, sc1.f2, sc1.f3, sc1.st4b = m1, m2, xl, bnst
        sc1.f1b, sc1.f2b, sc1.f3b = [m1], [m2], [xl]
        sc1.st4bv = bnst[:].rearrange("p a b -> p (a b)")[:, 0:16]
        setsA = [
            dict(sc=sc0, xs=xs, hT=hT, st=st4, kq=qk_tok[:, 4:8, :], kqb=qk_tok, v=v_tok[:], vb=v_tok,
                 ak=ak_tok[:], akb=ak_tok, kdf=kdf[:], kdfb=kdf, kdb=kdb[:], kdbb=kdb, banks=(0, 1, 2)),
            dict(sc=sc1, xs=xs2, hT=hT2, st=st4m, kq=mrg[:].rearrange("p (h d) -> p h d", h=8)[:, 4:8, :], kqb=mrg,
                 v=hid[:, 0:8, :].rearrange("p (h a) d -> p h (a d)", a=2), vb=hid,
                 ak=oatt_tok[:, 0:2, :], akb=oatt_tok, kdf=mT[:, 0:4, :], kdfb=mT, kdb=oattT[:, 0:4, :], kdbb=oattT,
                 banks=(3, 4, 5)),
        ]
        turn = [0]

        def stage_P(par):
            S = setsA[par]
            sc = S["sc"]
            ba, bb, bc = S["banks"]
            xs_, hT_, st_ = S["xs"], S["hT"], S["st"]
            for ti in range(par, len(tilesA), 2):
                kind, c, mode = tilesA[ti]
                xt = xring[ti % 2]
                rtb = rtab[ti % 2]
                atb = atab[ti % 2]
                sc_idx = (NCH + c) if kind == "c" else c
                if late_precasts and ti >= 1:
                    precast(*late_precasts.pop(0))
                norm_part1(xt, xs_, st_)
                yield
                norm_part2(0 if kind == "l" else 2, hT_, xs_, sc.f3)
                yield
                mm8(hT_, WA[1], ba)
                yield
                rotate(S["kq"], [S["kqb"]], psF_t[:, ba, :].rearrange("p (H d) -> p H d", H=4), [PB[ba]], rtb, 4, 1, sc)
                mm8(hT_, WA[2], bb)
                yield
                kb.op("act", lambda e: e.activation(out=S["v"][:, 0:2, :], in_=psF_t[:, bb, :].rearrange("p (h v) -> p h v", h=2),
                                                    func=AF.Identity), [PB[bb]], [S["vb"]])
                mm8(hT_, WA[3], bc)
                yield
                kb.op("act", lambda e: e.activation(out=S["v"][:, 2:4, :], in_=psF_t[:, bc, :].rearrange("p (h v) -> p h v", h=2),
                                                    func=AF.Identity), [PB[bc]], [S["vb"]])
                do_kv = not (kind == "c" and mode == "b")
                if do_kv:
                    mm8(hT_, WA[8], ba)
                    yield
                    akf = sc.f3[:, 0:256].rearrange("p (H d) -> p H d", H=2)
                    head_rms(psF_t[:, ba, 0:256].rearrange("p (H d) -> p H d", H=2), [PB[ba]], 2, BC_KG, akf, sc)
                    kb.op("act", lambda e, sc_idx=sc_idx: e.activation(
                        out=VA[:, sc_idx, :, 0:128], in_=psF_t[:, ba, 256:512].rearrange("p (a d) -> p a d", a=2),
                        func=AF.Identity), [PB[ba]], [VA])
                    if kind == "l":
                        rotate(S["ak"], [S["akb"]], akf, sc.f3b, atb, 2, 2, sc)
                    else:
                        kb.op("pool", lambda e, akf=akf: e.tensor_copy(out=S["ak"], in_=akf), sc.f3b, [S["akb"]])
                    yield
                    transposes(lambda k: S["ak"][:, k, :], [S["akb"]], 2,
                               KT[:, :, sc_idx * 128:(sc_idx + 1) * 128], [KT])
                    yield
                while turn[0] != ti:
                    yield
                if "f" in mode:
                    kb.op("dve", lambda e: e.tensor_tensor(out=S["kdf"], in0=S["kq"],
                                                           in1=KD[:, 0:4].unsqueeze(2).broadcast_to([128, 4, 128]), op=ALU.mult),
                          [S["kqb"], KD], [S["kdfb"]])
                    state_update_half(Sf, S["kdfb"], 0, 6, 0, S["v"], S["vb"], S["kdf"])
                    state_update_half(Sf, S["kdfb"], 0, 6, 1, S["v"], S["vb"], S["kdf"])
                if "b" in mode or mode == "s":
                    if mode == "s":
                        sr = sbs_ring[c % 2]
                        kb.op("act", lambda e, sr=sr: e.activation(out=sr[:], in_=Sb_[:], func=AF.Identity), [Sb_], [sr])
                        kb.dma("sp", sbs_d[c], sr[:], [sr], [SBSD[c]], key="sbsw%d" % (c % 2))
                    if not (mode == "s" and c == 0):
                        kb.op("dve", lambda e: e.tensor_tensor(out=S["kdb"], in0=S["kq"],
                                                               in1=KD[:, 4:8].unsqueeze(2).broadcast_to([128, 4, 128]), op=ALU.mult),
                              [S["kqb"], KD], [S["kdbb"]])
                        state_update_half(Sb_, S["kdbb"], 4, 6, 0, S["v"], S["vb"], S["kdb"])
                        state_update_half(Sb_, S["kdbb"], 4, 6, 1, S["v"], S["vb"], S["kdb"])
                turn[0] = ti + 1
                if ti + 2 < len(tilesA):
                    loadA(ti + 2)
                yield

        if tilesA:
            loadA(0)
            if len(tilesA) > 1:
                loadA(1)
            gens = [stage_P(0), stage_P(1)]
            alive = [True, True]
            while any(alive):
                for gi_ in range(2):
                    if alive[gi_]:
                        try:
                            next(gens[gi_])
                        except StopIteration:
                            alive[gi_] = False
            for hnd in m1h:
                hnd.w = m1.w
                hnd.r = dict(m1.r)
        while late_precasts:
            precast(*late_precasts.pop(0))
        dump("KT", KT, [128, 2, NSC * 128], BF16)
        dump("Sf", Sf, [128, 4, 256])

        chunksB = list(range(NCH))
        if stop_after in ("const", "A"):
            chunksB = []
        if isinstance(stop_after, tuple) and stop_after[0] == "B":
            chunksB = list(range(stop_after[1]))
        nB = len(chunksB)
        att_scale = 128.0 ** -0.5
        F_BLOCKS = (6, 7, 0, 1, 2, 3, 4, 5, 9, 10, 11, 12)
        planF = []
        planM = []
        for _ in chunksB:
            planF.extend([WT_IN + b for b in F_BLOCKS])
            planM.extend([WT_BR_RET, WT_BR_RET + 1, WT_BR_ATT, WT_BR_ATT + 1, WT_OUT, WT_OUT + 1] +
                         [WT_UP + i for i in range(8)] + [WT_DOWN + i for i in range(8)])
        ring4b = wringF + wringM
        wst = {"order": [], "index": {}, "issued": 0}

        def wtile(which, n):
            if kb.dry:
                wst["index"][(which, n)] = len(wst["order"])
                wst["order"].append((planF if which == "F" else planM)[n])
                return ring4b[0]
            g = wst["index"][(which, n)]
            order = wst["order"]
            nw = len(ring4b)
            while wst["issued"] < min(len(order), g + nw):
                i = wst["issued"]
                t = order[i]
                slot = ring4b[i % nw]
                kb.dma("sp", slot[:], wsc_d[t], [WSC[t]], [slot], key="wr%d" % (i % nw))
                wst["issued"] += 1
            return ring4b[g % nw]

        bstate = {"nb": 0}

        def nbank():
            i = bstate["nb"]
            bstate["nb"] += 1
            return 4 + (i % 3)

        x0 = len(tilesA)

        def load_chunk_inputs(cj):
            cc = chunksB[cj]
            s2 = (x0 + cj) % 2
            kb.dma("sp", xring[s2][:], x_d[cc * 128:(cc + 1) * 128, :], [], [xring[s2]], key="x%d" % s2)
            kb.dma("sp", rtab[s2][:], rt_d[NCTX + cc * 128:NCTX + (cc + 1) * 128, :], [], [rtab[s2]], key="rt%d" % s2)
            kb.dma("sp", atab[s2][:], at_d[cc * 128:(cc + 1) * 128, :], [], [atab[s2]], key="at%d" % s2)
            kb.dma("sp", sbs_ring[cj % 2][:], sbs_d[cc], [SBSD[cc]], [sbs_ring[cj % 2]], key="sbsr%d" % (cj % 2))

        def stage_F(cj):
            c = chunksB[cj]
            par = cj % 2
            s2 = (x0 + cj) % 2
            xt, rtb, atb, sr = xring[s2], rtab[s2], atab[s2], sbs_ring[cj % 2]
            QTd, oretTd, sgrd, sgad = QT[par], oretT[par], sgr[par], sga[par]
            wb0 = cj * 12
            norm_part1(xt, xs, st4)
            yield
            norm_part2(0, hT, xs, f3)
            yield
            for half in range(2):
                pb = nbank()
                mm8(hT, wtile("F", wb0 + half), pb)
                yield
                aqf = f3[:, half * 512:(half + 1) * 512].rearrange("p (H d) -> p H d", H=4)
                head_rms(bank(pb).rearrange("p (H d) -> p H d", H=4), [PB[pb]], 4, BC_QG, aqf)
                rotate(aq_tok[:, half * 4:(half + 1) * 4, :], [aq_tok], aqf, [f3], atb, 4, 2)
                yield
            for half in range(2):
                pb = nbank()
                mm8(hT, wtile("F", wb0 + 2 + half), pb)
                yield
                rotate(qk_tok[:, half * 4:(half + 1) * 4, :], [qk_tok],
                       bank(pb).rearrange("p (H d) -> p H d", H=4), [PB[pb]], rtb, 4, 1)
                if half == 0:
                    transposes(lambda k: aq_tok[:, k, :], [aq_tok], 8, QTd[:], [QTd])
                else:
                    kb.op("pool", lambda e: e.tensor_tensor(out=kdf[:], in0=qk_tok[:, 4:8, :],
                                                            in1=KD[:, 0:4].unsqueeze(2).broadcast_to([128, 4, 128]), op=ALU.mult),
                          [qk_tok, KD], [kdf])
                yield
            for half in range(2):
                pb = nbank()
                mm8(hT, wtile("F", wb0 + 4 + half), pb)
                yield
                kb.op("act", lambda e, half=half, pb=pb: e.activation(
                    out=v_tok[:, 2 * half:2 * half + 2, :].rearrange("p h v -> p (h v)"), in_=bank(pb), func=AF.Identity),
                    [PB[pb]], [v_tok])
                yield
            for k in range(8):
                kb.op("pe", lambda e, k=k: e.transpose(out=psT_t[:, k, :], in_=qk_tok[:, k, :], identity=identb[:]),
                      [qk_tok, identb], [PT], inc=(k == 7))
            kb.op("dve", lambda e: e.tensor_copy(out=qT[:], in_=psT_t[:, 0:4, :]), [PT], [qT])
            kb.op("dve", lambda e: e.tensor_tensor(out=qfT[:], in0=psT_t[:, 0:4, :], in1=QF[:], op=ALU.mult), [PT, QF], [qfT])
            kb.op("dve", lambda e: e.tensor_tensor(out=qbT[:], in0=psT_t[:, 0:4, :], in1=QB[:], op=ALU.mult), [PT, QB], [qbT])
            kb.op("dve", lambda e: e.tensor_copy(out=kT[:], in_=psT_t[:, 4:8, :]), [PT], [kT])
            kb.op("act", lambda e: e.activation(out=Sf_bf[:], in_=Sf[:], func=AF.Identity), [Sf], [Sf_bf])
            yield
            for half in range(2):
                pb = nbank()
                mm8(hT, wtile("F", wb0 + 6 + half), pb)
                yield
                hs = slice(half * 512, (half + 1) * 512)
                kb.op("act", lambda e, hs=hs, pb=pb: e.activation(out=f2[:, hs], in_=bank(pb), func=AF.Tanh, scale=0.5),
                      [PB[pb]], [f2])
                kb.op("dve", lambda e, hs=hs, pb=pb: e.scalar_tensor_tensor(out=sg[:, hs], in0=f2[:, hs], scalar=1.0, in1=bank(pb),
                                                                             op0=ALU.add, op1=ALU.mult), [f2, PB[pb]], [sg])
                yield
            pbs = nbank()
            for h in range(4):
                kb.op("pe", lambda e, h=h: e.matmul(psF_t[:, pbs, h * 128:(h + 1) * 128], lhsT=kT[:, h, :], rhs=qT[:, h, :],
                                                    start=True, stop=True), [kT, qT], [PB[pbs]], inc=(h == 3))
            yield
            kb.op("dve", lambda e: e.tensor_tensor(out=PTm[:], in0=psF_t[:, pbs, :].rearrange("p (h i) -> p h i", h=4),
                                                   in1=DT[:], op=ALU.mult), [PB[pbs], DT], [PTm])
            yield
            for gi_, dstg in ((0, sgrd), (1, sgad)):
                for half in range(2):
                    pb = nbank()
                    mm8(hT, wtile("F", wb0 + 8 + 2 * gi_ + half), pb)
                    yield
                    hs = slice(half * 512, (half + 1) * 512)
                    kb.op("act", lambda e, hs=hs, pb=pb: e.activation(out=f2[:, hs], in_=bank(pb), func=AF.Tanh, scale=0.5),
                          [PB[pb]], [f2])
                    kb.op("pool", lambda e, hs=hs, dstg=dstg: e.tensor_scalar(out=dstg[:, hs], in0=f2[:, hs], scalar1=0.5, scalar2=0.5,
                                                                               op0=ALU.mult, op1=ALU.add), [f2], [dstg])
                    yield
            for hh in range(2):
                pb = nbank()
                for h in (2 * hh, 2 * hh + 1):
                    oap = psF_t[:, pb, (h % 2) * 256:(h % 2 + 1) * 256]
                    kb.op("pe", lambda e, h=h, oap=oap: e.matmul(oap, lhsT=PTm[:, h, :], rhs=v_tok[:, h, :], start=True, stop=False),
                          [PTm, v_tok], [PB[pb]], inc=False)
                    kb.op("pe", lambda e, h=h, oap=oap: e.matmul(oap, lhsT=qfT[:, h, :], rhs=Sf_bf[:, h, :], start=False, stop=False),
                          [qfT, Sf_bf], [PB[pb]], inc=False)
                    kb.op("pe", lambda e, h=h, oap=oap: e.matmul(oap, lhsT=qbT[:, h, :], rhs=sr[:, h, :], start=False, stop=True),
                          [qbT, sr], [PB[pb]], inc=True)
                yield
                for h in (2 * hh, 2 * hh + 1):
                    oap = psF_t[:, pb, (h % 2) * 256:(h % 2 + 1) * 256]
                    kb.op("dve", lambda e, h=h, oap=oap: e.bn_stats(out=bnst[:, h, :], in_=oap), [PB[pb]], [bnst])
                    kb.op("dve", lambda e, h=h: e.bn_aggr(out=bnag[:, h, :], in_=bnst[:, h, :]), [bnst], [bnag])
                rstd_pool(bnag[:, 2 * hh:2 * hh + 2, 1], 2, 1.0, st4b[:, 12 + 2 * hh:14 + 2 * hh], bnag, st4b)
                for h in (2 * hh, 2 * hh + 1):
                    oap = psF_t[:, pb, (h % 2) * 256:(h % 2 + 1) * 256]
                    kb.op("dve", lambda e, h=h, oap=oap: e.tensor_scalar(
                        out=f1[:, h * 256:(h + 1) * 256], in0=oap, scalar1=bnag[:, h, 0:1], scalar2=st4b[:, 12 + h:13 + h],
                        op0=ALU.subtract, op1=ALU.mult), [PB[pb], bnag, st4b], [f1])
                yield
            for hh in range(2):
                pb = nbank()
                state_update_half(Sf, kdf, 0, pb, hh, part=0)
                yield
                state_update_half(Sf, kdf, 0, pb, hh, part=1)
                yield
            kb.op("pool", lambda e: e.tensor_tensor(out=f1[:], in0=f1[:], in1=bcv[:, BC_GNW:BC_GNW + 1024], op=ALU.mult),
                  [f1, bcv], [f1])
            kb.op("pool", lambda e: e.tensor_tensor(out=f1[:], in0=f1[:], in1=bcv[:, BC_GNB:BC_GNB + 1024], op=ALU.add),
                  [f1, bcv], [f1])
            kb.op("dve", lambda e: e.scalar_tensor_tensor(out=oret[:], in0=f1[:], scalar=0.5, in1=sg[:], op0=ALU.mult, op1=ALU.mult),
                  [f1, sg], [oret])
            yield
            yield
            transposes(lambda k: oret[:, k * 128:(k + 1) * 128], [oret], 8, oretTd[:], [oretTd])
            if cj == 0:
                dump("qk_tok", qk_tok, [128, 8, 128], BF16)
                dump("oret", oret, [128, 1024], BF16)
                dump("QT", QTd, [128, 8, 128], BF16)
            yield

        def stage_A(cj):
            par = cj % 2
            QTd = QT[par]
            iters = [(kvh, sc) for kvh in range(2) for sc in range(NSC)]
            SBANK = (0, 1)
            OBK = (2, 3)

            def emit_S(i):
                kvh, sc = iters[i]
                sbk = SBANK[i % 2]
                qrhs = QTd[:, kvh * 4:(kvh + 1) * 4, :].rearrange("p h t -> p (h t)")
                kb.op("pe", lambda e: e.matmul(bank(sbk), lhsT=KT[:, kvh, sc * 128:(sc + 1) * 128], rhs=qrhs,
                                               start=True, stop=True), [KT, QTd], [PB[sbk]], inc=True)

            def emit_PV(i):
                kvh, sc = iters[i]
                et = ET[i % len(ET)]
                for g in range(4):
                    ob = OBK[g // 2]
                    oap = psF_t[:, ob, (g % 2) * 129:(g % 2) * 129 + 129]
                    kb.op("pe", lambda e, g=g, oap=oap: e.matmul(
                        oap, lhsT=et[:, g * 128:(g + 1) * 128], rhs=VA[:, sc, kvh, :],
                        start=(sc == 0 and g % 2 == 0), stop=(sc == NSC - 1), skip_group_check=True),
                        [et, VA], [PB[ob]], inc=(g == 3))
                if sc == NSC - 1:
                    for g in range(4):
                        ob = OBK[g // 2]
                        off = (g % 2) * 129
                        kb.op("dve", lambda e, g=g, ob=ob, off=off: e.reciprocal(
                            out=rs8[:, kvh * 4 + g:kvh * 4 + g + 1], in_=psF_t[:, ob, off + 128:off + 129]), [PB[ob]], [rs8])
                        kb.op("dve", lambda e, g=g, ob=ob, off=off: e.tensor_scalar(
                            out=oatt_tok[:, kvh * 4 + g, :], in0=psF_t[:, ob, off:off + 128],
                            scalar1=rs8[:, kvh * 4 + g:kvh * 4 + g + 1], scalar2=None, op0=ALU.mult),
                            [PB[ob], rs8], [oatt_tok])

            emit_S(0)
            for i, (kvh, sc) in enumerate(iters):
                if i + 1 < len(iters):
                    emit_S(i + 1)
                sbk = SBANK[i % 2]
                et = ET[i % len(ET)]
                kb.op("act", lambda e, sbk=sbk, et=et: e.activation(out=et[:], in_=bank(sbk), func=AF.Exp, scale=att_scale),
                      [PB[sbk]], [et])
                if i >= 1:
                    emit_PV(i - 1)
                yield
            emit_PV(len(iters) - 1)
            yield

        def stage_M(cj):
            c = chunksB[cj]
            par = cj % 2
            oretTd, sgrd, sgad = oretT[par], sgr[par], sga[par]
            wb0 = cj * 22
            kb.dma("sp", xl[:], x_d[c * 128:(c + 1) * 128, :], [], [xl], key="xl")
            for cb in range(2):
                pb = nbank()
                hs = slice(cb * 512, (cb + 1) * 512)
                mm8(oretTd, wtile("M", wb0 + cb), pb)
                yield
                kb.op("dve", lambda e, hs=hs, pb=pb: e.tensor_tensor(out=m1[:, hs], in0=bank(pb), in1=sgrd[:, hs], op=ALU.mult),
                      [PB[pb], sgrd], [m1h[cb]])
                yield
                if cb == 0:
                    transposes(lambda k: oatt_tok[:, k, :], [oatt_tok], 8, oattT[:], [oattT])
                    if cj == 0:
                        dump("oattT", oattT, [128, 8, 128], BF16)
                    yield
            for cb in range(2):
                pb = nbank()
                hs = slice(cb * 512, (cb + 1) * 512)
                mm8(oattT, wtile("M", wb0 + 2 + cb), pb)
                yield
                kb.op("dve", lambda e, hs=hs, pb=pb: e.tensor_tensor(out=m2[:, hs], in0=bank(pb), in1=sgad[:, hs], op=ALU.mult),
                      [PB[pb], sgad], [m2])
                yield
            kb.op("pool", lambda e: e.tensor_tensor(out=mrg[:], in0=m1[:], in1=m2[:], op=ALU.add), [m1h[0], m1h[1], m2], [mrg])
            yield
            transposes(lambda k: mrg[:, k * 128:(k + 1) * 128], [mrg], 8, mT[:], [mT])
            if cj == 0:
                dump("mrg", mrg, [128, 1024], BF16)
            yield
            for cb in range(2):
                pb = nbank()
                hs = slice(cb * 512, (cb + 1) * 512)
                mm8(mT, wtile("M", wb0 + 4 + cb), pb)
                yield
                kb.op("dve", lambda e, hs=hs, pb=pb: e.tensor_tensor(out=m1[:, hs], in0=bank(pb), in1=gm_bc[:, hs], op=ALU.mult),
                      [PB[pb], gm_bc], [m1h[cb]])
                yield
            kb.op("pool", lambda e: e.tensor_tensor(out=xl[:], in0=xl[:], in1=m1[:], op=ALU.add), [xl, m1h[0], m1h[1]], [xl])
            yield
            yield
            norm_part1(xl, xs2, st4m)
            if cj == 0:
                dump("xl", xl, [128, 1024])
            yield
            norm_part2(4, hT2, xs2, m2)
            yield
            for cb in range(8):
                pb = nbank()
                wb = wtile("M", wb0 + 6 + cb)
                for jj in range(4):
                    for k in range(8):
                        kb.op("pe", lambda e, k=k, jj=jj, wb=wb, pb=pb: e.matmul(
                            psF_t[:, pb, jj * 128:(jj + 1) * 128], lhsT=wb[:, k, jj * 128:(jj + 1) * 128], rhs=hT2[:, k, :],
                            start=(k == 0), stop=(k == 7)), [wb, hT2], [PB[pb]], inc=(k == 7 and jj == 3))
                hv = cb % 2
                hs = slice(hv * 512, (hv + 1) * 512)
                yield
                kb.op("act", lambda e, pb=pb, hs=hs: e.activation(out=m1[:, hs], in_=bank(pb), func=AF.Relu), [PB[pb]], [m1h[hv]])
                kb.op("pool", lambda e, cb=cb, hs=hs: e.tensor_tensor(out=hid[:, cb * 4:(cb + 1) * 4, :].rearrange("p j t -> p (j t)"),
                                                                      in0=m1[:, hs], in1=m1[:, hs], op=ALU.mult), [m1h[hv]], [hid])
                yield
            for cb in range(2):
                hs = slice(cb * 512, (cb + 1) * 512)
                for kg in range(4):
                    pb = nbank()
                    wb = wtile("M", wb0 + 14 + cb * 4 + kg)
                    for k in range(8):
                        j = kg * 8 + k
                        kb.op("pe", lambda e, k=k, j=j, wb=wb, pb=pb: e.matmul(
                            bank(pb), lhsT=hid[:, j, :], rhs=wb[:, k, :], start=(k == 0), stop=(k == 7)),
                            [hid, wb], [PB[pb]], inc=(k == 7))
                    yield
                    if kg == 0:
                        kb.op("dve", lambda e, hs=hs, pb=pb: e.tensor_copy(out=m1[:, hs], in_=bank(pb)), [PB[pb]], [m1h[cb]])
                    else:
                        kb.op("dve", lambda e, hs=hs, pb=pb: e.tensor_tensor(out=m1[:, hs], in0=m1[:, hs], in1=bank(pb), op=ALU.add),
                              [PB[pb], m1h[cb]], [m1h[cb]])
                    yield
                kb.op("pool", lambda e, hs=hs: e.tensor_tensor(out=m1[:, hs], in0=m1[:, hs], in1=gf_bc[:, hs], op=ALU.mult),
                      [m1h[cb], gf_bc], [m1h[cb]])
            kb.op("pool", lambda e: e.tensor_tensor(out=m1[:], in0=m1[:], in1=xl[:], op=ALU.add), [m1h[0], m1h[1], xl], [m1h[0], m1h[1]])
            yield
            yield
            kb.op("act", lambda e: e.activation(out=xs2[:], in_=m1[:], func=AF.Square, accum_out=st4m[:, 4:5]),
                  [m1h[0], m1h[1]], [xs2, st4m])
            rstd_pool(st4m[:, 4:5], 1, 1.0 / D, st4m[:, 5:6], st4m, st4m)
            kb.op("dve", lambda e: e.scalar_tensor_tensor(out=m2[:], in0=m1[:], scalar=st4m[:, 5:6],
                                                          in1=bcv[:, BC_FIN:BC_FIN + 1024], op0=ALU.mult, op1=ALU.mult),
                  [m1h[0], m1h[1], st4m, bcv], [m2])
            tok = kb.dma("sp", y_d[c * 128:(c + 1) * 128, :], m2[:], [m2], [], key="st0")
            if tok is not None:
                kb.final_waits[tok[0]] = tok[1]
            yield

        def exhaust(g):
            for _ in g:
                pass

        def step(g):
            try:
                next(g)
                return True
            except StopIteration:
                return False

        def phaseB():
            bstate["nb"] = 0
            if nB > 0:
                load_chunk_inputs(0)
                if nB > 1:
                    load_chunk_inputs(1)
                exhaust(stage_F(0))
            RATIO = 1.4
            ucount = [0, 0]
            for cj in range(nB):
                others = []
                if cj >= 1:
                    others.append(stage_M(cj - 1))
                if cj + 1 < nB:
                    if cj + 2 < nB:
                        pass
                    others.append(stage_F(cj + 1))
                ga = stage_A(cj)
                acc = 0.0
                rr = 0
                while step(ga):
                    acc += RATIO
                    while acc >= 1.0 and others:
                        acc -= 1.0
                        g = others[rr % len(others)]
                        if not step(g):
                            others.remove(g)
                        else:
                            rr += 1
                            ucount[0] += 1
                for g in others:
                    for _ in g:
                        ucount[1] += 1
                dbg_out["_units"] = list(ucount)
                if cj + 2 < nB:
                    load_chunk_inputs(cj + 2)
            if nB > 0:
                exhaust(stage_M(nB - 1))


        kb.dry = True
        phaseB()
        kb.dry = False
        phaseB()

        kb.wait_all("sp", list(kb.final_waits.items()))
        kb.run()
    return nc, dbg_out


def prep_inputs(inputs):
    f = np.float32
    x = np.asarray(inputs["x"], f)
    c = np.asarray(inputs["c"], f)
    ctx = np.asarray(inputs["ctx"], f)
    c_ctx = np.asarray(inputs["c_ctx"], f)

    def pl(v):
        return np.asarray(v, f).reshape(-1, 128).T

    rt, at = host_tabs()
    consts = host_consts()
    lam = np.concatenate([np.asarray(inputs["ret_log_lam_fwd"], f)[0], np.asarray(inputs["ret_log_lam_bwd"], f)[0]])
    lam_rep = np.ascontiguousarray(np.broadcast_to(lam[None, :], (128, 8)))
    bc = np.concatenate([np.asarray(inputs["ret_gn_w"], f)[0], np.asarray(inputs["ret_gn_b"], f)[0],
                         np.asarray(inputs["final_norm_g"], f), np.asarray(inputs["att_q_norm_g"], f)[0],
                         np.asarray(inputs["att_k_norm_g"], f)[0]])
    bc_rep = np.ascontiguousarray(np.broadcast_to(bc[None, :], (128, NBC)))
    shared = {
        "lam": lam_rep, "bc": bc_rep, "rt": rt, "at": at, "consts": consts,
        "mod_w": np.ascontiguousarray(np.asarray(inputs["mod_w"], f)[0]),
        "w_in": np.ascontiguousarray(np.asarray(inputs["w_in"], f)[0]),
        "w_br_ret": np.ascontiguousarray(np.asarray(inputs["w_br_ret"], f)[0]),
        "w_br_att": np.ascontiguousarray(np.asarray(inputs["w_br_att"], f)[0]),
        "w_out": np.ascontiguousarray(np.asarray(inputs["w_out"], f)[0]),
        "w_mlp_up": np.ascontiguousarray(np.asarray(inputs["w_mlp_up"], f)[0]),
        "w_mlp_down": np.ascontiguousarray(np.asarray(inputs["w_mlp_down"], f)[0]),
    }
    in_maps = []
    for b in range(8):
        vecs = np.concatenate([pl(c[b]), pl(c_ctx), pl(np.asarray(inputs["mod_b"], f)[0]),
                               pl(np.asarray(inputs["norm_mix_g"], f)[0]), pl(np.asarray(inputs["norm_mlp_g"], f)[0])], axis=1)
        m = dict(shared)
        m["x"] = np.ascontiguousarray(x[b])
        m["ctx"] = np.ascontiguousarray(ctx[b])
        m["vecs"] = np.ascontiguousarray(vecs.astype(f))
        in_maps.append(m)
    return in_maps


_CACHE = {}


def kernel(**inputs):
    in_maps = prep_inputs(inputs)
    if "nc" not in _CACHE:
        _CACHE["nc"] = build()[0]
    nc = _CACHE["nc"]
    res = run_bass_kernel_spmd(nc, in_maps, core_ids=list(range(8)))
    out = np.stack([np.asarray(r["y"], np.float32) for r in res.results], axis=0)
    return out
```
